# Optimizing a Trainium2 kernel written in Bass

```python
import math
import jax, jax.numpy as jnp
from jax import lax
import numpy as np

D_MODEL = 1024
BATCH = 16
SEQ = 256
DEPTH = 2
DEC_BATCH = 8
DEC_SEQ = 4096
PAST_LEN = 512

GRID_W = 64
D_MIX = D_MODEL
C_CONV = D_MIX // 4
CONV_WIDTH = 31
H_GLA = 4
DK_GLA = 64
DV_GLA = 64
W_GLA = H_GLA * DV_GLA
GLA_RANK = 16
GLA_TAU = 16.0
GLA_CHUNK = 64
H_DIFF = 4
DH_DIFF = 64
DV_DIFF = 2 * DH_DIFF
W_DIFF = H_DIFF * DV_DIFF
Q_BLOCK = 128
ROPE_BASE = 10000.0
IN_CONV = 2 * C_CONV
IN_GLA = 4 * W_GLA + 2 * GLA_RANK
IN_DIFF = 3 * W_DIFF
IN_COLS = IN_CONV + IN_GLA + IN_DIFF
D_FF = 2816
ALPHA = (2 * DEPTH) ** 0.25
BETA = (8 * DEPTH) ** -0.25
LN_EPS = 1e-5
N_MOD = 9

kernel_name = 'hybrid_dit_conv_gla_diffattn_step'


def layer_norm(x, g, b):
    xf = x.astype(jnp.float32)
    mu = jnp.mean(xf, axis=-1, keepdims=True)
    var = jnp.mean(jnp.square(xf - mu), axis=-1, keepdims=True)
    y = (xf - mu) * lax.rsqrt(var + LN_EPS)
    return (y * g.astype(jnp.float32) + b.astype(jnp.float32)).astype(x.dtype)


def rms_norm(x, g):
    xf = x.astype(jnp.float32)
    y = xf * lax.rsqrt(jnp.mean(jnp.square(xf), axis=-1, keepdims=True) + LN_EPS)
    return (y * g.astype(jnp.float32)).astype(x.dtype)


def swiglu_ffn(h, w_in, w_out):
    a, b = jnp.split(h @ w_in, 2, axis=-1)
    return (jax.nn.silu(a) * b) @ w_out


def conv_module(u, w, bias, g, b):
    a, gt = jnp.split(u, 2, axis=-1)
    y = a * jax.nn.sigmoid(gt)
    pad = CONV_WIDTH // 2
    y = lax.conv_general_dilated(y, w[:, None, :].astype(y.dtype), window_strides=(1,),
                                 padding=[(pad, pad)], dimension_numbers=('NWC', 'WIO', 'NWC'),
                                 feature_group_count=C_CONV) + bias
    return jax.nn.silu(layer_norm(y, g, b))


def gla_features(z, w_a2, b_a):
    B, T, _ = z.shape
    q, k, v, gate, lr_f, lr_b = jnp.split(
        z, [W_GLA, 2 * W_GLA, 3 * W_GLA, 4 * W_GLA, 4 * W_GLA + GLA_RANK], axis=-1)
    heads = lambda t: t.reshape(B, T, H_GLA, -1).transpose(0, 2, 1, 3)
    logdec = lambda lr, d: heads(jax.nn.log_sigmoid((lr @ w_a2[d] + b_a[d]).astype(jnp.float32)) / GLA_TAU)
    return heads(q) * DK_GLA ** -0.5, heads(k), heads(v), gate, logdec(lr_f, 0), logdec(lr_b, 1)


def gla_chunked(q, k, v, g, s0):
    B, H, T, DK = q.shape
    DV = v.shape[-1]
    N = T // GLA_CHUNK
    f32 = jnp.float32
    qc = q.astype(f32).reshape(B, H, N, GLA_CHUNK, DK)
    kc = k.astype(f32).reshape(B, H, N, GLA_CHUNK, DK)
    vc = v.astype(f32).reshape(B, H, N, GLA_CHUNK, DV)
    G = jnp.cumsum(g.astype(f32).reshape(B, H, N, GLA_CHUNK, DK), axis=3)
    G_last = G[:, :, :, -1:, :]
    q_t = qc * jnp.exp(G)
    k_t = kc * jnp.exp(-G)
    k_hat = kc * jnp.exp(G_last - G)
    lower = jnp.tril(jnp.ones((GLA_CHUNK, GLA_CHUNK), dtype=bool))
    A = jnp.where(lower, jnp.einsum('bhncd,bhnsd->bhncs', q_t, k_t), 0.0)
    o_intra = jnp.einsum('bhncs,bhnsv->bhncv', A, vc)
    kv = jnp.einsum('bhnsd,bhnsv->bhndv', k_hat, vc)
    decay = jnp.exp(G_last[:, :, :, 0, :])

    def step(S, inp):
        d, u = inp
        return d[..., None] * S + u, S

    s_final, s_in = lax.scan(step, s0.astype(f32), (jnp.moveaxis(decay, 2, 0), jnp.moveaxis(kv, 2, 0)))
    s_in = jnp.moveaxis(s_in, 0, 2)
    o_inter = jnp.einsum('bhncd,bhndv->bhncv', q_t, s_in)
    return (o_intra + o_inter).reshape(B, H, T, DV), s_final


def axial_rope_tables(rows):
    row = jnp.repeat(jnp.arange(rows, dtype=jnp.float32), GRID_W)
    col = jnp.tile(jnp.arange(GRID_W, dtype=jnp.float32), rows)
    seg = DH_DIFF // 2
    inv = ROPE_BASE ** (-jnp.arange(0, seg, 2, dtype=jnp.float32) / seg)
    a_r = row[:, None] * inv
    a_c = col[:, None] * inv
    ang = jnp.concatenate([a_r, a_r, a_c, a_c], axis=-1)
    return jnp.cos(ang), jnp.sin(ang)


def apply_rope(x, cos, sin):
    xf = x.astype(jnp.float32)
    xr = xf.reshape(x.shape[:-1] + (2, 2, DH_DIFF // 4))
    rot = jnp.stack([-xr[..., 1, :], xr[..., 0, :]], axis=-2).reshape(x.shape)
    return (xf * cos[:, None, :] + rot * sin[:, None, :]).astype(x.dtype)


def diff_lambda(lam_p, layer_idx):
    lam_init = 0.8 - 0.6 * math.exp(-0.3 * layer_idx)
    p = lam_p.astype(jnp.float32)
    lam = jnp.exp(jnp.sum(p[0] * p[1])) - jnp.exp(jnp.sum(p[2] * p[3])) + lam_init
    return lam, lam_init


def diff_attend(q, k, v, lam):
    B, H, Tq = q.shape[:3]
    nb = Tq // Q_BLOCK
    qb = q.reshape(B, H, nb, Q_BLOCK, 2, DH_DIFF).transpose(2, 0, 1, 3, 4, 5)
    scale = DH_DIFF ** -0.5

    def block(qi):
        s = jnp.einsum('bhqmd,bhkmd->bhmqk', qi, k).astype(jnp.float32) * scale
        p = jax.nn.softmax(s, axis=-1)
        w = p[:, :, 0] - lam * p[:, :, 1]
        return jnp.einsum('bhqk,bhkv->bhqv', w.astype(v.dtype), v)

    o = lax.map(block, qb)
    return o.transpose(1, 2, 0, 3, 4).reshape(B, H, Tq, DV_DIFF)


def token_mixer(h, l, P, ctx):
    B, T, _ = h.shape
    z = h @ P['w_in'][l]
    z_conv, z_gla, z_diff = jnp.split(z, [IN_CONV, IN_CONV + IN_GLA], axis=-1)
    y_conv = conv_module(z_conv, P['conv_w'][l], P['conv_b'][l], P['conv_ln_g'][l], P['conv_ln_b'][l])
    q, k, v, gate, g_f, g_b = gla_features(z_gla, P['gla_w_a2'][l], P['gla_b_a'][l])
    if ctx is None:
        s_f0 = jnp.zeros((B, H_GLA, DK_GLA, DV_GLA), jnp.float32)
        s_b0 = s_f0
    else:
        s_f0 = ctx['state'][:, 0]
        s_b0 = ctx['state'][:, 1]
    flip = lambda t: jnp.flip(t, axis=2)
    o_f, s_f = gla_chunked(q, k, v, g_f, s_f0)
    o_b, s_b = gla_chunked(flip(q), flip(k), flip(v), flip(g_b), s_b0)
    o = (o_f + flip(o_b)).astype(h.dtype)
    y_gla = rms_norm(o, P['gla_norm_g'][l]).transpose(0, 2, 1, 3).reshape(B, T, W_GLA) * jax.nn.silu(gate)
    qd, kd, vd = jnp.split(z_diff, 3, axis=-1)
    qd = qd.reshape(B, T, H_DIFF, 2, DH_DIFF).transpose(0, 2, 1, 3, 4)
    kd = kd.reshape(B, T, H_DIFF, 2, DH_DIFF).transpose(0, 2, 1, 3, 4)
    vd = vd.reshape(B, T, H_DIFF, DV_DIFF).transpose(0, 2, 1, 3)
    if ctx is None:
        keys, vals = kd, vd
    else:
        qd = apply_rope(qd, ctx['cos'], ctx['sin'])
        ck = ctx['k'].reshape(B, H_DIFF, -1, 2, DH_DIFF)
        keys = jnp.concatenate([apply_rope(kd, ctx['cos'], ctx['sin']), ck], axis=2)
        vals = jnp.concatenate([vd, ctx['v']], axis=2)
    lam, lam_init = diff_lambda(P['diff_lam'][l], l)
    od = diff_attend(qd, keys, vals, lam)
    y_diff = (rms_norm(od, P['diff_norm_g'][l]) * (1.0 - lam_init)).transpose(0, 2, 1, 3).reshape(B, T, W_DIFF)
    y = jnp.concatenate([y_conv, y_gla, y_diff.astype(h.dtype)], axis=-1) @ P['w_out'][l]
    if ctx is None:
        new_ctx = (kd.reshape(B, H_DIFF, T, 2 * DH_DIFF), vd,
                   jnp.stack([s_f, s_b], axis=1).astype(h.dtype))
    else:
        new_ctx = None
    return y, new_ctx


def trunk_layer(x, mod, l, P, ctx):
    sh0, sc0, g0, sh1, sc1, g1, sh2, sc2, g2 = jnp.split(mod, N_MOD, axis=-1)
    h = x * (1.0 + sc0) + sh0
    x = layer_norm(ALPHA * x + 0.5 * g0 * swiglu_ffn(h, P['w_ffn1_in'][l], P['w_ffn1_out'][l]),
                   P['ln_g'][l, 0], P['ln_b'][l, 0])
    h = x * (1.0 + sc1) + sh1
    y, new_ctx = token_mixer(h, l, P, ctx)
    x = layer_norm(ALPHA * x + g1 * y, P['ln_g'][l, 1], P['ln_b'][l, 1])
    h = x * (1.0 + sc2) + sh2
    x = layer_norm(ALPHA * x + 0.5 * g2 * swiglu_ffn(h, P['w_ffn2_in'][l], P['w_ffn2_out'][l]),
                   P['ln_g'][l, 2], P['ln_b'][l, 2])
    return x, new_ctx


def setup_inputs(seed: int = 0) -> dict:
    key = jax.random.key(seed)
    ks = jax.random.split(key, 32)
    f32 = jnp.float32
    nrm = lambda k, shape, s: jax.random.normal(k, shape, f32) * s
    D = D_MODEL
    return {
        'x_prompt': nrm(ks[0], (BATCH, SEQ, D), 1.0),
        'x_sample': nrm(ks[1], (DEC_BATCH, DEC_SEQ, D), 1.0),
        'cache_diff_k': nrm(ks[2], (DEC_BATCH, DEPTH, H_DIFF, PAST_LEN, 2 * DH_DIFF), 1.0),
        'cache_diff_v': nrm(ks[3], (DEC_BATCH, DEPTH, H_DIFF, PAST_LEN, DV_DIFF), 1.0),
        'state_gla': nrm(ks[4], (DEC_BATCH, DEPTH, 2, H_GLA, DK_GLA, DV_GLA), 1.0),
        'c': nrm(ks[5], (DEC_BATCH, D), 1.0),
        'c_ctx': nrm(ks[6], (D,), 1.0),
        'w_ada': nrm(ks[7], (DEPTH, D, N_MOD * D), 0.5 * D ** -0.5),
        'b_ada': nrm(ks[8], (DEPTH, N_MOD * D), 0.01),
        'w_ffn1_in': nrm(ks[9], (DEPTH, D, 2 * D_FF), D ** -0.5),
        'w_ffn1_out': nrm(ks[10], (DEPTH, D_FF, D), BETA * D_FF ** -0.5),
        'w_ffn2_in': nrm(ks[11], (DEPTH, D, 2 * D_FF), D ** -0.5),
        'w_ffn2_out': nrm(ks[12], (DEPTH, D_FF, D), BETA * D_FF ** -0.5),
        'w_in': nrm(ks[13], (DEPTH, D, IN_COLS), D ** -0.5),
        'conv_w': nrm(ks[14], (DEPTH, CONV_WIDTH, C_CONV), CONV_WIDTH ** -0.5),
        'conv_b': nrm(ks[15], (DEPTH, C_CONV), 0.01),
        'conv_ln_g': 1.0 + nrm(ks[16], (DEPTH, C_CONV), 0.01),
        'conv_ln_b': nrm(ks[17], (DEPTH, C_CONV), 0.01),
        'gla_w_a2': nrm(ks[18], (DEPTH, 2, GLA_RANK, W_GLA), GLA_RANK ** -0.5),
        'gla_b_a': nrm(ks[19], (DEPTH, 2, W_GLA), 0.01),
        'gla_norm_g': 1.0 + nrm(ks[20], (DEPTH, DV_GLA), 0.01),
        'diff_lam': nrm(ks[21], (DEPTH, 4, DH_DIFF), 0.1),
        'diff_norm_g': 1.0 + nrm(ks[22], (DEPTH, DV_DIFF), 0.01),
        'w_out': nrm(ks[23], (DEPTH, D_MIX, D), BETA * D_MIX ** -0.5),
        'ln_g': 1.0 + nrm(ks[24], (DEPTH, 3, D), 0.01),
        'ln_b': nrm(ks[25], (DEPTH, 3, D), 0.01),
    }


def reference(x_prompt, x_sample, cache_diff_k, cache_diff_v, state_gla, c, c_ctx,
              w_ada, b_ada, w_ffn1_in, w_ffn1_out, w_ffn2_in, w_ffn2_out, w_in,
              conv_w, conv_b, conv_ln_g, conv_ln_b, gla_w_a2, gla_b_a, gla_norm_g,
              diff_lam, diff_norm_g, w_out, ln_g, ln_b):
    P = dict(w_ffn1_in=w_ffn1_in, w_ffn1_out=w_ffn1_out, w_ffn2_in=w_ffn2_in, w_ffn2_out=w_ffn2_out,
             w_in=w_in, conv_w=conv_w, conv_b=conv_b, conv_ln_g=conv_ln_g, conv_ln_b=conv_ln_b,
             gla_w_a2=gla_w_a2, gla_b_a=gla_b_a, gla_norm_g=gla_norm_g, diff_lam=diff_lam,
             diff_norm_g=diff_norm_g, w_out=w_out, ln_g=ln_g, ln_b=ln_b)
    xp = x_prompt
    new_k, new_v, new_s = [], [], []
    for l in range(DEPTH):
        mod = (jax.nn.silu(c_ctx[None, :]) @ w_ada[l] + b_ada[l])[:, None, :]
        xp, (k_l, v_l, s_l) = trunk_layer(xp, mod, l, P, None)
        new_k.append(k_l)
        new_v.append(v_l)
        new_s.append(s_l)
    rows = x_sample.shape[1] // GRID_W
    cos, sin = axial_rope_tables(rows)
    xs = x_sample
    for l in range(DEPTH):
        mod = (jax.nn.silu(c) @ w_ada[l] + b_ada[l])[:, None, :]
        ctx = dict(k=cache_diff_k[:, l], v=cache_diff_v[:, l], state=state_gla[:, l], cos=cos, sin=sin)
        xs, _ = trunk_layer(xs, mod, l, P, ctx)
    new_diff_k = jnp.stack(new_k, axis=1)
    new_diff_v = jnp.stack(new_v, axis=1)
    new_gla = jnp.stack(new_s, axis=1)
    return (xp, xs, new_diff_k, new_diff_v, new_gla)
```

```python
import math
from contextlib import ExitStack

import numpy as np
import concourse.bass as bass
import concourse.mybir as mybir
from concourse.bass_utils import run_bass_kernel_spmd

F32 = mybir.dt.float32
BF16 = mybir.dt.bfloat16
AF = mybir.ActivationFunctionType
ALU = mybir.AluOpType
AX = mybir.AxisListType

ENGS = ("pe", "act", "dve", "pool", "sp")
SEM_LIMIT = 30000


class Buf:
    __slots__ = ("name", "w", "r", "dsem", "excl", "epoch")

    def __init__(self, name="", excl=False):
        self.name = name
        self.w = None
        self.r = {}
        self.dsem = None
        self.epoch = -1
        self.excl = excl


class DSem:
    _n = 0

    def __init__(self):
        self.id = "d%d" % DSem._n
        DSem._n += 1
        self.total = 0
        self.last = None
        self.handle = None
        self.kind = None


class Op:
    __slots__ = ("eng", "idx", "fn", "deps", "dma", "need_inc", "waits", "known", "incval")

    def __init__(self, eng, idx, fn, deps, dma):
        self.eng = eng
        self.idx = idx
        self.fn = fn
        self.deps = deps
        self.dma = dma
        self.need_inc = False
        self.waits = []
        self.known = None
        self.incval = 0


class Prog:
    def __init__(self):
        self.ops = {e: [] for e in ENGS}
        self.order = []
        self.dsems = []
        self.free = []
        self.epoch = 0

    def _add(self, eng, fn, reads, writes, dma_sem=None, ndma=1):
        deps = {}

        def add_dep(t):
            if t is None:
                return
            key = t[1] if t[0] == "c" else t[1].id
            if key not in deps or deps[key][2] < t[2]:
                deps[key] = t

        for b in reads:
            add_dep(b.w)
            if b.excl:
                for t in b.r.values():
                    add_dep(t)
        for b in writes:
            add_dep(b.w)
            for t in b.r.values():
                add_dep(t)
        raw_same = set()
        for b in reads:
            if b.w is not None and b.w[0] == "c" and b.w[1] == eng:
                raw_same.add(b.w[2])
        if eng in deps and eng == "pe":
            del deps[eng]
        idx = len(self.ops[eng])
        dma = None
        if dma_sem is not None:
            if dma_sem.last is not None:
                add_dep(dma_sem.last)
            dma_sem.total += 16 * ndma
            dma = (dma_sem, dma_sem.total, ndma)
        op = Op(eng, idx, fn, list(deps.values()), dma)
        self.ops[eng].append(op)
        self.order.append(op)
        if dma is not None:
            tok = ("d", dma_sem, dma_sem.total, op)
            dma_sem.last = tok
            rkey = dma_sem.id
        else:
            tok = ("c", eng, idx)
            rkey = eng
        for b in reads:
            b.r[rkey] = tok
        for b in writes:
            b.w = tok
            b.r = {}
        return op

    def op(self, eng, fn, reads=(), writes=()):
        return self._add(eng, fn, list(reads), list(writes))

    def dma(self, eng, fns, reads=(), writes=(), sb=None):
        if not isinstance(fns, (list, tuple)):
            fns = [fns]
        b = sb
        kind = "sw" if eng == "pool" else "hw"
        if b.dsem is None or b.epoch != self.epoch or b.dsem.kind != kind or b.dsem.total + 16 * len(fns) > SEM_LIMIT:
            ds = None
            rest = []
            while self.free:
                c_ = self.free.pop()
                if c_.kind == kind and c_.total + 16 * len(fns) + 2000 <= SEM_LIMIT:
                    ds = c_
                    break
                rest.append(c_)
            self.free.extend(rest)
            if ds is None:
                ds = DSem()
                ds.kind = kind
                self.dsems.append(ds)
            b.dsem = ds
            b.epoch = self.epoch
        return self._add(eng, list(fns), list(reads), list(writes), dma_sem=b.dsem, ndma=len(fns))

    def barrier(self):
        self.epoch += 1
        self.free = list(self.dsems)
        toks = []
        for e in ENGS:
            if self.ops[e]:
                o = self.ops[e][-1]
                if o.dma is None:
                    toks.append(("c", e, o.idx))
        for s in self.dsems:
            if s.last is not None:
                toks.append(s.last)
        for e in ENGS:
            deps = {}
            for t in toks:
                if t[0] == "c":
                    if t[1] == e:
                        continue
                    deps[t[1]] = t
                else:
                    deps[t[1].id] = t
            idx = len(self.ops[e])
            op = Op(e, idx, None, list(deps.values()), None)
            self.ops[e].append(op)
            self.order.append(op)

    def resolve(self):
        cur = {e: {} for e in ENGS}
        for op in self.order:
            k = cur[op.eng]
            newk = None
            for t in op.deps:
                base = newk if newk is not None else k
                if t[0] == "c":
                    _, e, i = t
                    if base.get(e, -1) >= i:
                        continue
                    src = self.ops[e][i]
                    if src.fn is None:
                        pass
                    src.need_inc = True
                    key, val = e, i
                else:
                    _, s, v, src = t
                    if base.get(s.id, -1) >= v:
                        continue
                    key, val = s.id, v
                op.waits.append(t)
                if newk is None:
                    newk = dict(k)
                if src.known:
                    for kk, vv in src.known.items():
                        if newk.get(kk, -1) < vv:
                            newk[kk] = vv
                if newk.get(key, -1) < val:
                    newk[key] = val
            if newk is not None:
                cur[op.eng] = newk
                k = newk
            op.known = k
        self.ninc = {}
        for e in ENGS:
            c = 0
            for op in self.ops[e]:
                if op.need_inc:
                    c += 1
                op.incval = c
            self.ninc[e] = c

    def emit(self, nc, stack):
        self.resolve()
        esems = {}
        for e in ENGS:
            n = self.ninc[e] // SEM_LIMIT + 1
            esems[e] = [stack.enter_context(nc.semaphore("s_%s_%d" % (e, j))) for j in range(n)]
        for s in self.dsems:
            s.handle = stack.enter_context(nc.semaphore("s_" + s.id))
        prog = self

        def emit_wait(eng, t):
            if t[0] == "c":
                v = prog.ops[t[1]][t[2]].incval
                j = (v - 1) // SEM_LIMIT
                eng.wait_ge(esems[t[1]][j], v - j * SEM_LIMIT)
            else:
                eng.wait_ge(t[1].handle, t[2])

        def run(ename):
            def body(eng):
                for op in prog.ops[ename]:
                    for t in op.waits:
                        emit_wait(eng, t)
                    if op.fn is None:
                        if op.need_inc:
                            j = (op.incval - 1) // SEM_LIMIT
                            eng.nop().then_inc(esems[ename][j], 1)
                        continue
                    if op.dma is not None:
                        for f in op.fn:
                            f(eng).then_inc(op.dma[0].handle, 16)
                    else:
                        ins = op.fn(eng)
                        if op.need_inc:
                            j = (op.incval - 1) // SEM_LIMIT
                            ins.then_inc(esems[ename][j], 1)
            return body

        with nc.Block() as block:
            block.tensor(run("pe"))
            block.scalar(run("act"))
            block.vector(run("dve"))
            block.gpsimd(run("pool"))
            block.sync(run("sp"))


D = 1024
KC = 8
DFF = 2816
FC = 22
DEPTH = 2
T_S = 4096
T_P = 256
NTOK = T_S + 2 * T_P
PAST = 512
INC = 3104
ALPHA = (2 * DEPTH) ** 0.25
LN_EPS = 1e-5
NCORES = 8

BIGW = 53000


def _prod(s):
    n = 1
    for v in s:
        n *= v
    return n


def _view(v, shape):
    if len(shape) == 1:
        return v
    if len(shape) == 2:
        return v.rearrange("p (a b) -> p a b", b=shape[1])
    if len(shape) == 3:
        return v.rearrange("p (a b c) -> p a b c", b=shape[1], c=shape[2])
    if len(shape) == 4:
        return v.rearrange("p (a b c d) -> p a b c d", b=shape[1], c=shape[2], d=shape[3])
    raise ValueError


class Arena:
    def __init__(self, big, lo, hi):
        self.big = big
        self.lo = lo
        self.hi = hi
        self.off = lo

    def reset(self):
        self.off = self.lo

    def f32(self, *shape):
        n = _prod(shape)
        a = self.off
        self.off += n
        assert self.off <= self.hi, "SBUF arena overflow %d > %d" % (self.off, self.hi)
        return _view(self.big[:, a:a + n], shape)

    def bf16(self, *shape):
        n = _prod(shape)
        nw = (n + 1) // 2
        a = self.off
        self.off += nw
        assert self.off <= self.hi, "SBUF arena overflow %d > %d" % (self.off, self.hi)
        v = self.big[:, a:a + nw].bitcast(BF16)
        if 2 * nw != n:
            v = v[:, 0:n]
        return _view(v, shape)


class K:
    pass


def build_nc(stop=None):
    nc = bass.Bass("TRN2", target_bir_lowering=False)
    k = K()
    k.nc = nc
    k.stop = stop

    def din(name, shape, dt=F32):
        return nc.dram_tensor(name, list(shape), dt, kind="ExternalInput").ap()

    def dout(name, shape, dt=F32):
        return nc.dram_tensor(name, list(shape), dt, kind="ExternalOutput").ap()

    def dscr(name, shape, dt=F32):
        kind = "ExternalOutput" if (stop is not None and stop.startswith("mix") and name != "XS") else "Internal"
        return nc.dram_tensor(name, list(shape), dt, kind=kind).ap()

    I = {}
    I["x_in"] = din("x_in", [NTOK, D])
    I["cvec"] = din("cvec", [2, D])
    I["ck"] = din("ck", [DEPTH, 4, PAST, 128])
    I["cv"] = din("cv", [DEPTH, 4, PAST, 128])
    I["sg"] = din("sg", [DEPTH, 2, 4, 64, 64])
    I["w_ada"] = din("w_ada", [DEPTH, D, 9 * D])
    I["b_ada"] = din("b_ada", [DEPTH, 9 * D])
    I["w_ffn1_in"] = din("w_ffn1_in", [DEPTH, D, 2 * DFF])
    I["w_ffn1_out"] = din("w_ffn1_out", [DEPTH, DFF, D])
    I["w_ffn2_in"] = din("w_ffn2_in", [DEPTH, D, 2 * DFF])
    I["w_ffn2_out"] = din("w_ffn2_out", [DEPTH, DFF, D])
    I["w_in"] = din("w_in", [DEPTH, D, INC])
    I["conv_w"] = din("conv_w", [DEPTH, 31, 256])
    I["conv_b"] = din("conv_b", [DEPTH, 256])
    I["conv_ln_g"] = din("conv_ln_g", [DEPTH, 256])
    I["conv_ln_b"] = din("conv_ln_b", [DEPTH, 256])
    I["gla_w_a2"] = din("gla_w_a2", [DEPTH, 2, 16, 256])
    I["gla_b_a"] = din("gla_b_a", [DEPTH, 2, 256])
    I["gla_norm_g"] = din("gla_norm_g", [DEPTH, 64])
    I["diff_lam"] = din("diff_lam", [DEPTH, 4, 64])
    I["diff_norm_g"] = din("diff_norm_g", [DEPTH, 128])
    I["w_out"] = din("w_out", [DEPTH, D, D])
    I["ln_g"] = din("ln_g", [DEPTH, 3, D])
    I["ln_b"] = din("ln_b", [DEPTH, 3, D])
    I["cmat"] = din("cmat", [7, 128, 128])
    I["ident"] = I["cmat"][0]
    I["rope"] = din("rope", [2, 128, T_S])
    k.I = I
    O = {}
    O["y"] = dout("y", [NTOK, D])
    O["nk"] = dout("nk", [2, DEPTH, 4, T_P, 128])
    O["nv"] = dout("nv", [2, DEPTH, 4, T_P, 128])
    O["ng"] = dout("ng", [2, DEPTH, 2, 4, 64, 64])
    k.O = O
    k.XS = dscr("XS", [128, KC, NTOK])
    S = {}
    S["YC"] = dscr("YC", [128, 2, NTOK])
    S["SG"] = dscr("SG", [128, 2, NTOK])
    for d_ in range(2):
        S["QT%d" % d_] = dscr("QT%d" % d_, [128, 2, NTOK], BF16)
        S["KT%d" % d_] = dscr("KT%d" % d_, [128, 2, NTOK], BF16)
        S["KH%d" % d_] = dscr("KH%d" % d_, [NTOK, 256], BF16)
        S["DEC%d" % d_] = dscr("DEC%d" % d_, [128, 2, NTOK // 64])
    S["VG"] = dscr("VG", [NTOK, 256], BF16)
    S["QD"] = dscr("QD", [128, 4, NTOK], BF16)
    S["KD"] = dscr("KD", [128, 4, NTOK], BF16)
    S["VD"] = dscr("VD", [NTOK, 512], BF16)
    S["YM"] = dscr("YM", [128, 8, NTOK], BF16)
    k.S = S
    k.Sb = {n: Buf("S_" + n) for n in S}
    if stop is not None:
        O["dbg"] = dout("dbg", [128, KC, NTOK])

    P = Prog()
    k.P = P
    with ExitStack() as st:
        big = st.enter_context(nc.sbuf_tensor("big", [128, BIGW], F32))
        ps = st.enter_context(nc.psum_tensor("ps", [128, 8, 512], F32))
        k.ps = ps
        k.Bps = [Buf("ps%d" % i, excl=True) for i in range(8)]
        k.cons = Arena(big, 0, 4200)
        k.ar = Arena(big, 4200, BIGW)
        k.outbufs = []
        k.XSb = {}
        _build_all(k)
        P.op("sp", None, reads=k.outbufs)
        P.emit(nc, st)
    return nc


def xs_buf(k, key):
    if key not in k.XSb:
        k.XSb[key] = Buf("XS%s" % (key,))
    return k.XSb[key]


TILES = [(i * 1024, 1024, 0) for i in range(4)] + [(T_S, 512, 1)]


def _build_all(k):
    phase_consts(k)
    if k.stop == "consts":
        return
    for l in range(DEPTH):
        phase_ffn(k, l, 0, first=(l == 0), last=False)
        if k.stop == "ffn1_%d" % l:
            dump_xs(k)
            return
        phase_mix_a(k, l)
        if k.stop == "mixa_%d" % l:
            return
        phase_gla(k, l)
        if k.stop is not None and k.stop.startswith("mixb") and k.stop.endswith("_%d" % l):
            return
        phase_attn(k, l)
        if k.stop == "mixd_%d" % l:
            return
        phase_mix_out(k, l)
        if k.stop == "mix_%d" % l:
            dump_xs(k)
            return
        phase_ffn(k, l, 2, first=False, last=(l == DEPTH - 1))


def dump_xs(k):
    P = k.P
    k.P.barrier()
    k.ar.reset()
    t = k.ar.f32(KC, 512)
    Bt = Buf("dump")
    Bd = Buf("dbgout")
    for i in range(NTOK // 512):
        sl = slice(i * 512, (i + 1) * 512)
        P.dma("sp", lambda e, sl=sl: e.dma_start(out=t, in_=k.XS[:, :, sl]),
              reads=list(k.XSb.values()), writes=[Bt], sb=Bt)
        P.dma("sp", lambda e, sl=sl: e.dma_start(out=k.O["dbg"][:, :, sl], in_=t), reads=[Bt], writes=[Bd], sb=Bt)
    k.outbufs.append(Bd)


def load_T(k, src2d, R, dst, tmp, Btmp, Bdst, bank=7):
    P = k.P
    ps = k.ps
    P.dma("sp", lambda e: e.dma_start(out=tmp[0:R, 0:128], in_=src2d), writes=[Btmp], sb=Btmp)
    P.op("pe", lambda e: e.transpose(out=ps[:, bank, 0:R], in_=tmp[0:R, 0:128], identity=k.ident[0:R, 0:R]),
         reads=[Btmp, k.Bcons], writes=[k.Bps[bank]])
    P.op("dve", lambda e: e.tensor_copy(out=dst, in_=ps[:, bank, 0:R]), reads=[k.Bps[bank]], writes=[Bdst])


def phase_consts(k):
    P, nc, I, ps = k.P, k.nc, k.I, k.ps
    c = k.cons
    k.Bcons = Buf("cons")
    Bc = k.Bcons
    k.ident = c.f32(128)
    k.ident_bf = c.bf16(128)
    k.ones_bf = c.bf16(128)
    k.modT = c.f32(DEPTH, 72, 2)
    k.s1p = c.f32(DEPTH, 2, 3, 8)
    k.gt = c.f32(DEPTH, 2, 3, 8)
    k.lng = c.f32(DEPTH * 3 * 8)
    k.lnb = c.f32(DEPTH * 3 * 8)
    k.cm = c.f32(7, 128)
    k.cmb = c.bf16(7, 128)
    P.dma("sp", lambda e: e.dma_start(out=k.ident, in_=I["ident"]), writes=[Bc], sb=Bc)
    P.op("dve", lambda e: e.tensor_copy(out=k.ident_bf, in_=k.ident), reads=[Bc], writes=[Bc])
    P.op("dve", lambda e: e.memset(k.ones_bf, 1.0), writes=[Bc])
    P.dma("sp", lambda e: e.dma_start(out=k.cm, in_=I["cmat"].rearrange("m p c -> p m c")), writes=[Bc], sb=Bc)
    P.op("dve", lambda e: e.tensor_copy(out=k.cmb, in_=k.cm), reads=[Bc], writes=[Bc])

    ar = k.ar
    ar.reset()
    tmp = ar.f32(128)
    Btmp = Buf("tmp")
    load_T(k, I["ln_g"].rearrange("l i (c p) -> (l i c) p", p=128), 48, k.lng, tmp, Btmp, Bc)
    load_T(k, I["ln_b"].rearrange("l i (c p) -> (l i c) p", p=128), 48, k.lnb, tmp, Btmp, Bc)
    cs = ar.f32(D)
    cvT = ar.f32(KC, 2)
    Bcs, BcvT = Buf("cs"), Buf("cvT")
    P.dma("sp", lambda e: e.dma_start(out=cs[0:2, :], in_=I["cvec"]), writes=[Bcs], sb=Bcs)
    P.op("act", lambda e: e.activation(out=cs[0:2, :], in_=cs[0:2, :], func=AF.Silu), reads=[Bcs], writes=[Bcs])
    for kc in range(KC):
        P.op("pe", lambda e, kc=kc: e.transpose(out=ps[:, 6, 2 * kc:2 * kc + 2], in_=cs[0:2, kc * 128:(kc + 1) * 128],
                                                 identity=k.ident[0:2, 0:2]), reads=[Bcs, Bc], writes=[k.Bps[6]])
    P.op("dve", lambda e: e.tensor_copy(out=cvT, in_=ps[:, 6, 0:16].rearrange("p (a b) -> p a b", b=2)),
         reads=[k.Bps[6]], writes=[BcvT])
    NMB = 4
    wst = [ar.f32(KC, 512) for _ in range(NMB)]
    Bwst = [Buf("wst%d" % j) for j in range(NMB)]
    wb = [ar.bf16(KC, 512) for _ in range(NMB)]
    Bwb = [Buf("wada%d" % j) for j in range(NMB)]
    cvb = ar.bf16(KC, 2)
    P.op("dve", lambda e: e.tensor_copy(out=cvb, in_=cvT), reads=[BcvT], writes=[BcvT])
    bT = ar.f32(72)
    BbT = Buf("bT")
    u = 0
    for l in range(DEPTH):
        load_T(k, I["b_ada"][l].rearrange("(r p) -> r p", p=128), 72, bT, tmp, Btmp, BbT)
        for cg in range(18):
            w = wb[u % NMB]
            Bw = Bwb[u % NMB]
            ws = wst[u % NMB]
            Bws = Bwst[u % NMB]
            src = I["w_ada"][l][:, cg * 512:(cg + 1) * 512].rearrange("(kc p) c -> p kc c", p=128)
            P.dma("sp", lambda e, ws=ws, src=src: e.dma_start(out=ws, in_=src), writes=[Bws], sb=Bws)
            if u % 2:
                P.op("act", lambda e, w=w, ws=ws: e.activation(out=w, in_=ws, func=AF.Copy), reads=[Bws], writes=[Bw])
            else:
                P.op("dve", lambda e, w=w, ws=ws: e.tensor_copy(out=w, in_=ws), reads=[Bws], writes=[Bw])
            u += 1
            for cc in range(4):
                cb = cg * 4 + cc
                for kc in range(KC):
                    P.op("pe", lambda e, w=w, cc=cc, kc=kc, cb=cb: e.matmul(
                        ps[:, 5, 2 * cb:2 * cb + 2], w[:, kc, cc * 128:(cc + 1) * 128], cvb[:, kc, :],
                        start=(kc == 0), stop=(kc == KC - 1)), reads=[Bw, BcvT], writes=[k.Bps[5]])
        P.op("dve", lambda e, l=l: e.tensor_tensor(
            out=k.modT[:, l], in0=ps[:, 5, 0:144].rearrange("p (a b) -> p a b", b=2),
            in1=bT.unsqueeze(2).broadcast_to([128, 72, 2]), op=ALU.add), reads=[k.Bps[5], BbT], writes=[Bc])
    k.convw = c.f32(DEPTH, 2, 31)
    k.convp = c.f32(3, DEPTH * 2)
    k.gnorm = c.f32(DEPTH)
    k.dnorm = c.f32(DEPTH)
    k.nlam = c.f32(DEPTH)
    k.wa2 = c.f32(DEPTH, 2, 256)
    k.lamt = c.f32(DEPTH, 4, 64)
    k.lamp = c.f32(DEPTH, 2, 64)
    k.lams = c.f32(DEPTH, 2)
    for l in range(DEPTH):
        for cc in range(2):
            load_T(k, I["conv_w"][l][:, cc * 128:(cc + 1) * 128], 31, k.convw[:, l, cc, :], tmp, Btmp, Bc)
    for wi, nm in enumerate(["conv_b", "conv_ln_g", "conv_ln_b"]):
        load_T(k, I[nm].rearrange("l (c p) -> (l c) p", p=128), DEPTH * 2, k.convp[:, wi, :], tmp, Btmp, Bc)
    P.dma("sp", [lambda e: e.dma_start(out=tmp[0:DEPTH, 0:64], in_=I["gla_norm_g"]),
                 lambda e: e.dma_start(out=tmp[0:DEPTH, 64:128], in_=I["gla_norm_g"])], writes=[Btmp], sb=Btmp)
    P.op("pe", lambda e: e.transpose(out=ps[:, 7, 0:DEPTH], in_=tmp[0:DEPTH, 0:128], identity=k.ident[0:DEPTH, 0:DEPTH]),
         reads=[Btmp, Bc], writes=[k.Bps[7]])
    P.op("dve", lambda e: e.tensor_copy(out=k.gnorm, in_=ps[:, 7, 0:DEPTH]), reads=[k.Bps[7]], writes=[Bc])
    load_T(k, I["diff_norm_g"], DEPTH, k.dnorm, tmp, Btmp, Bc)
    for l in range(DEPTH):
        lam_init = 0.8 - 0.6 * math.exp(-0.3 * l)
        P.op("dve", lambda e, l=l, li=lam_init: e.tensor_scalar(out=k.dnorm[:, l:l + 1], in0=k.dnorm[:, l:l + 1],
                                                               scalar1=1.0 - li, scalar2=None, op0=ALU.mult),
             reads=[Bc], writes=[Bc])
    P.dma("sp", lambda e: e.dma_start(out=k.lamt, in_=I["diff_lam"].partition_broadcast(128)), writes=[Bc], sb=Bc)
    for l in range(DEPTH):
        lam_init = 0.8 - 0.6 * math.exp(-0.3 * l)
        P.op("dve", lambda e, l=l: e.tensor_tensor(out=k.lamp[:, l, 0, :], in0=k.lamt[:, l, 0, :], in1=k.lamt[:, l, 1, :],
                                                   op=ALU.mult), reads=[Bc], writes=[Bc])
        P.op("dve", lambda e, l=l: e.tensor_tensor(out=k.lamp[:, l, 1, :], in0=k.lamt[:, l, 2, :], in1=k.lamt[:, l, 3, :],
                                                   op=ALU.mult), reads=[Bc], writes=[Bc])
        P.op("dve", lambda e, l=l: e.reduce_sum(out=k.lams[:, l, :], in_=k.lamp[:, l, :, :], axis=AX.X),
             reads=[Bc], writes=[Bc])
        P.op("act", lambda e, l=l: e.activation(out=k.lams[:, l, :], in_=k.lams[:, l, :], func=AF.Exp),
             reads=[Bc], writes=[Bc])
        P.op("dve", lambda e, l=l, li=lam_init: e.scalar_tensor_tensor(
            out=k.nlam[:, l:l + 1], in0=k.lams[:, l, 1:2], scalar=-li, in1=k.lams[:, l, 0:1],
            op0=ALU.add, op1=ALU.subtract), reads=[Bc], writes=[Bc])
    for l in range(DEPTH):
        for d_ in range(2):
            P.dma("sp", [lambda e, l=l, d_=d_: e.dma_start(out=k.wa2[0:16, l, d_, :], in_=I["gla_w_a2"][l, d_]),
                         lambda e, l=l, d_=d_: e.dma_start(out=k.wa2[16:17, l, d_, :], in_=I["gla_b_a"][l, d_:d_ + 1, :])],
                  writes=[Bc], sb=Bc)
    coef = [0.5 / ALPHA, 1.0 / ALPHA, 0.5 / ALPHA]
    for l in range(DEPTH):
        for g in range(2):
            for i in range(3):
                P.op("dve", lambda e, l=l, g=g, i=i: e.tensor_scalar(
                    out=k.s1p[:, l, g, i, :], in0=k.modT[:, l, (3 * i + 1) * 8:(3 * i + 2) * 8, g],
                    scalar1=1.0, scalar2=None, op0=ALU.add), reads=[Bc], writes=[Bc])
                P.op("dve", lambda e, l=l, g=g, i=i: e.tensor_scalar(
                    out=k.gt[:, l, g, i, :], in0=k.modT[:, l, (3 * i + 2) * 8:(3 * i + 3) * 8, g],
                    scalar1=coef[i], scalar2=None, op0=ALU.mult), reads=[Bc], writes=[Bc])
    P.barrier()


def ln_half(k, l, i, x, Bx, hs, tmps):
    P, ps = k.P, k.ps
    rbf, rsq, mean, var, Bt, Bs = tmps
    for m in range(KC):
        P.op("act", lambda e, m=m: e.activation(out=rbf[:, m, :], in_=x[:, m, hs], func=AF.Copy), reads=[Bx], writes=[Bt])
        P.op("act", lambda e, m=m: e.activation(out=rsq[:, m, :], in_=x[:, m, hs], func=AF.Square), reads=[Bx], writes=[Bt])
    for m in range(KC):
        P.op("pe", lambda e, m=m: e.matmul(ps[:, 0, :], k.ones_bf, rbf[:, m, :], start=(m == 0), stop=(m == KC - 1)),
             reads=[Bt, k.Bcons], writes=[k.Bps[0]])
    for m in range(KC):
        P.op("pe", lambda e, m=m: e.matmul(ps[:, 1, :], k.ones_bf, rsq[:, m, :], start=(m == 0), stop=(m == KC - 1)),
             reads=[Bt, k.Bcons], writes=[k.Bps[1]])
    P.op("dve", lambda e: e.tensor_scalar(out=mean, in0=ps[:, 0, :], scalar1=1.0 / D, scalar2=None, op0=ALU.mult),
         reads=[k.Bps[0]], writes=[Bs])
    P.op("dve", lambda e: e.tensor_tensor(out=var, in0=mean, in1=mean, op=ALU.mult), reads=[Bs], writes=[Bs])
    P.op("dve", lambda e: e.scalar_tensor_tensor(out=var, in0=ps[:, 1, :], scalar=1.0 / D, in1=var,
                                                 op0=ALU.mult, op1=ALU.subtract), reads=[k.Bps[1], Bs], writes=[Bs])
    P.op("dve", lambda e: e.tensor_scalar(out=var, in0=var, scalar1=LN_EPS / (ALPHA * ALPHA), scalar2=None, op0=ALU.add),
         reads=[Bs], writes=[Bs])
    P.op("act", lambda e: e.activation(out=var, in_=var, func=AF.Sqrt), reads=[Bs], writes=[Bs])
    P.op("dve", lambda e: e.reciprocal(out=var, in_=var), reads=[Bs], writes=[Bs])
    P.op("dve", lambda e: e.scalar_tensor_tensor(out=mean, in0=mean, scalar=-1.0, in1=var, op0=ALU.mult, op1=ALU.mult),
         reads=[Bs], writes=[Bs])
    for m in range(KC):
        P.op("dve", lambda e, m=m: e.tensor_tensor(out=x[:, m, hs], in0=x[:, m, hs], in1=var, op=ALU.mult),
             reads=[Bx, Bs], writes=[Bx])
        P.op("dve", lambda e, m=m: e.tensor_tensor(out=x[:, m, hs], in0=x[:, m, hs], in1=mean, op=ALU.add),
             reads=[Bx, Bs], writes=[Bx])
        col = (l * 3 + i) * 8 + m
        P.op("act", lambda e, m=m, col=col: e.activation(out=x[:, m, hs], in_=x[:, m, hs], func=AF.Identity,
                                                        scale=k.lng[:, col:col + 1], bias=k.lnb[:, col:col + 1]),
             reads=[Bx, k.Bcons], writes=[Bx])


def load_x_tile(k, x, Bx, tok0, nt, first, stage, Bstage):
    P, ps = k.P, k.ps
    if not first:
        P.dma("sp", lambda e: e.dma_start(out=x[:, :, 0:nt], in_=k.XS[:, :, tok0:tok0 + nt]),
              reads=[xs_buf(k, tok0)], writes=[Bx], sb=Bx)
        return
    for q in range(nt // 256):
        src = k.I["x_in"][tok0 + q * 256: tok0 + (q + 1) * 256, :].rearrange("(b p) f -> p b f", p=128)
        P.dma("sp", lambda e, src=src: e.dma_start(out=stage, in_=src), writes=[Bstage], sb=Bstage)
        for ch in range(KC):
            bank = 4 + ch % 4
            for b in range(2):
                P.op("pe", lambda e, ch=ch, b=b, bank=bank: e.transpose(
                    out=ps[:, bank, b * 128:(b + 1) * 128], in_=stage[:, b, ch * 128:(ch + 1) * 128], identity=k.ident),
                    reads=[Bstage, k.Bcons], writes=[k.Bps[bank]])
            if ch % 2:
                P.op("act", lambda e, ch=ch, bank=bank, q=q: e.activation(
                    out=x[:, ch, q * 256:(q + 1) * 256], in_=ps[:, bank, 0:256], func=AF.Copy),
                    reads=[k.Bps[bank]], writes=[Bx])
            else:
                P.op("dve", lambda e, ch=ch, bank=bank, q=q: e.tensor_copy(
                    out=x[:, ch, q * 256:(q + 1) * 256], in_=ps[:, bank, 0:256]),
                    reads=[k.Bps[bank]], writes=[Bx])


def store_x_tile(k, x, Bx, tok0, nt, last, stage, Bstage):
    P, ps = k.P, k.ps
    if not last:
        P.dma("sp", lambda e: e.dma_start(out=k.XS[:, :, tok0:tok0 + nt], in_=x[:, :, 0:nt]),
              reads=[Bx], writes=[xs_buf(k, tok0)], sb=Bx)
        return
    By = Buf("y")
    k.outbufs.append(By)
    for q in range(nt // 256):
        for b in range(2):
            for hf in range(2):
                bank = 4 + (b * 2 + hf) % 4
                for c4 in range(4):
                    ch = hf * 4 + c4
                    P.op("pe", lambda e, ch=ch, b=b, bank=bank, c4=c4, q=q: e.transpose(
                        out=ps[:, bank, c4 * 128:(c4 + 1) * 128], in_=x[:, ch, q * 256 + b * 128: q * 256 + (b + 1) * 128],
                        identity=k.ident), reads=[Bx, k.Bcons], writes=[k.Bps[bank]])
                if hf:
                    P.op("act", lambda e, b=b, bank=bank, hf=hf: e.activation(
                        out=stage[:, b, hf * 512:(hf + 1) * 512], in_=ps[:, bank, :], func=AF.Copy),
                        reads=[k.Bps[bank]], writes=[Bstage])
                else:
                    P.op("dve", lambda e, b=b, bank=bank, hf=hf: e.tensor_copy(
                        out=stage[:, b, hf * 512:(hf + 1) * 512], in_=ps[:, bank, :]),
                        reads=[k.Bps[bank]], writes=[Bstage])
        dst = k.O["y"][tok0 + q * 256: tok0 + (q + 1) * 256, :].rearrange("(b p) f -> p b f", p=128)
        P.dma("sp", lambda e, dst=dst: e.dma_start(out=dst, in_=stage), reads=[Bstage], writes=[By], sb=Bstage)


def ln_half_gen(k, l, i, x, Bx, hs, tmps, banks):
    P, ps = k.P, k.ps
    rbf, rsq, mean, var, Bt, Bs = tmps
    b0, b1 = banks
    for m in range(KC):
        P.op("act", lambda e, m=m: e.activation(out=rbf[:, m, :], in_=x[:, m, hs], func=AF.Copy), reads=[Bx], writes=[Bt])
        P.op("act", lambda e, m=m: e.activation(out=rsq[:, m, :], in_=x[:, m, hs], func=AF.Square), reads=[Bx], writes=[Bt])
        if m % 2:
            yield
    for m in range(KC):
        P.op("pe", lambda e, m=m: e.matmul(ps[:, b0, :], k.ones_bf, rbf[:, m, :], start=(m == 0), stop=(m == KC - 1)),
             reads=[Bt, k.Bcons], writes=[k.Bps[b0]])
    for m in range(KC):
        P.op("pe", lambda e, m=m: e.matmul(ps[:, b1, :], k.ones_bf, rsq[:, m, :], start=(m == 0), stop=(m == KC - 1)),
             reads=[Bt, k.Bcons], writes=[k.Bps[b1]])
    yield
    P.op("dve", lambda e: e.tensor_scalar(out=mean, in0=ps[:, b0, :], scalar1=1.0 / D, scalar2=None, op0=ALU.mult),
         reads=[k.Bps[b0]], writes=[Bs])
    P.op("dve", lambda e: e.tensor_tensor(out=var, in0=mean, in1=mean, op=ALU.mult), reads=[Bs], writes=[Bs])
    P.op("dve", lambda e: e.scalar_tensor_tensor(out=var, in0=ps[:, b1, :], scalar=1.0 / D, in1=var,
                                                 op0=ALU.mult, op1=ALU.subtract), reads=[k.Bps[b1], Bs], writes=[Bs])
    P.op("dve", lambda e: e.tensor_scalar(out=var, in0=var, scalar1=LN_EPS / (ALPHA * ALPHA), scalar2=None, op0=ALU.add),
         reads=[Bs], writes=[Bs])
    P.op("act", lambda e: e.activation(out=var, in_=var, func=AF.Sqrt), reads=[Bs], writes=[Bs])
    P.op("dve", lambda e: e.reciprocal(out=var, in_=var), reads=[Bs], writes=[Bs])
    P.op("dve", lambda e: e.scalar_tensor_tensor(out=mean, in0=mean, scalar=-1.0, in1=var, op0=ALU.mult, op1=ALU.mult),
         reads=[Bs], writes=[Bs])
    yield
    for m in range(KC):
        P.op("dve", lambda e, m=m: e.tensor_tensor(out=x[:, m, hs], in0=x[:, m, hs], in1=var, op=ALU.mult),
             reads=[Bx, Bs], writes=[Bx])
        P.op("dve", lambda e, m=m: e.tensor_tensor(out=x[:, m, hs], in0=x[:, m, hs], in1=mean, op=ALU.add),
             reads=[Bx, Bs], writes=[Bx])
        col = (l * 3 + i) * 8 + m
        P.op("act", lambda e, m=m, col=col: e.activation(out=x[:, m, hs], in_=x[:, m, hs], func=AF.Identity,
                                                        scale=k.lng[:, col:col + 1], bias=k.lnb[:, col:col + 1]),
             reads=[Bx, k.Bcons], writes=[Bx])
        yield


def ln_gen(k, l, i, x, Bx, nh, tmps, after=None, tmps2=None, Bx2=None):
    if tmps2 is None or nh == 1:
        for hf in range(nh):
            yield from ln_half_gen(k, l, i, x, Bx if Bx2 is None or hf == 0 else Bx2, slice(hf * 512, (hf + 1) * 512), tmps, (6, 7))
    else:
        g0 = ln_half_gen(k, l, i, x, Bx, slice(0, 512), tmps, (6, 7))
        g1 = ln_half_gen(k, l, i, x, Bx2 if Bx2 is not None else Bx, slice(512, 1024), tmps2, (4, 5))
        live = [g0, g1]
        while live:
            for g_ in list(live):
                if next(g_, "done") == "done":
                    live.remove(g_)
            yield
    if after is not None:
        after()
    yield


def _drain(gen):
    if gen is not None:
        for _ in gen:
            pass


def phase_ffn(k, l, i, first, last):
    P, ps, I = k.P, k.ps, k.I
    ar = k.ar
    ar.reset()
    w_in = I["w_ffn1_in" if i == 0 else "w_ffn2_in"][l]
    w_out = I["w_ffn1_out" if i == 0 else "w_ffn2_out"][l]
    xb = [ar.f32(KC, 1024) for _ in range(2)]
    Bx = [Buf("x0"), Buf("x1")]
    h = ar.bf16(KC, 1024)
    Bh = Buf("h")
    act = ar.bf16(FC, 1024)
    Bact = [Buf("act0"), Buf("act1")]
    NWB = 3
    wib = [ar.bf16(2, KC, 256) for _ in range(NWB)]
    Bwi = [Buf("wi%d" % j) for j in range(NWB)]
    wob = [ar.bf16(FC, 128) for _ in range(2)]
    Bwo = [Buf("wo0"), Buf("wo1")]
    sg = [ar.bf16(512) for _ in range(2)]
    Bsg = [Buf("sg0"), Buf("sg1")]
    rbf = ar.bf16(KC, 512)
    rsq = ar.bf16(KC, 512)
    mean = ar.f32(512)
    var = ar.f32(512)
    Bt = Buf("lntmp")
    Bstat = Buf("lnstat")
    stage = ar.f32(2, 1024) if (first or last) else None
    Bstage = Buf("stage")
    cnt = {"uw": 0, "uo": 0, "usg": 0, "grp": 0, "cacc": 0}

    def stage_a(ti):
        tok0, nt, g = TILES[ti]
        x, bx = xb[ti % 2], Bx[ti % 2]
        load_x_tile(k, x, bx, tok0, nt, first, stage, Bstage)
        for kc in range(KC):
            P.op("dve", lambda e, kc=kc, x=x, nt=nt, g=g: e.tensor_scalar(
                out=h[:, kc, 0:nt], in0=x[:, kc, 0:nt], scalar1=k.s1p[:, l, g, i, kc:kc + 1],
                scalar2=k.modT[:, l, (3 * i) * 8 + kc, g:g + 1], op0=ALU.mult, op1=ALU.add),
                reads=[bx, k.Bcons], writes=[Bh])

    def stage_b(ti, hook):
        tok0, nt, g = TILES[ti]
        nh = nt // 512
        for u in range(FC // 2):
            wb = wib[cnt["uw"] % NWB]
            Bw = Bwi[cnt["uw"] % NWB]
            cnt["uw"] += 1
            sa = w_in[:, u * 256:(u + 1) * 256].rearrange("(kc p) c -> p kc c", p=128)
            sb_ = w_in[:, DFF + u * 256: DFF + (u + 1) * 256].rearrange("(kc p) c -> p kc c", p=128)
            P.dma("pool", [lambda e, wb=wb, sa=sa: e.dma_start(out=wb[:, 0], in_=sa),
                           lambda e, wb=wb, sb_=sb_: e.dma_start(out=wb[:, 1], in_=sb_)], writes=[Bw], sb=Bw)
            for jj in range(2):
                j = 2 * u + jj
                for hf in range(nh):
                    hs = slice(hf * 512, (hf + 1) * 512)
                    slot = cnt["grp"] % 3
                    cnt["grp"] += 1
                    ba, bb = 2 * slot, 2 * slot + 1
                    for ab, bank in ((0, ba), (1, bb)):
                        for kc in range(KC):
                            P.op("pe", lambda e, wb=wb, ab=ab, kc=kc, jj=jj, hs=hs, bank=bank: e.matmul(
                                ps[:, bank, :], wb[:, ab, kc, jj * 128:(jj + 1) * 128], h[:, kc, hs],
                                start=(kc == 0), stop=(kc == KC - 1)), reads=[Bw, Bh], writes=[k.Bps[bank]])
                    s_ = sg[cnt["usg"] % 2]
                    Bs_ = Bsg[cnt["usg"] % 2]
                    cnt["usg"] += 1
                    P.op("act", lambda e, s_=s_, ba=ba: e.activation(out=s_, in_=ps[:, ba, :], func=AF.Silu),
                         reads=[k.Bps[ba]], writes=[Bs_])
                    P.op("dve", lambda e, s_=s_, bb=bb, j=j, hs=hs: e.tensor_tensor(
                        out=act[:, j, hs], in0=ps[:, bb, :], in1=s_, op=ALU.mult),
                        reads=[k.Bps[bb], Bs_], writes=[Bact[hf]])
                    hook()

    def stage_c(ti):
        tok0, nt, g = TILES[ti]
        nh = nt // 512
        x, bx = xb[ti % 2], Bx[ti % 2]
        for m in range(KC):
            wo = wob[cnt["uo"] % 2]
            Bw = Bwo[cnt["uo"] % 2]
            cnt["uo"] += 1
            so = w_out[:, m * 128:(m + 1) * 128].rearrange("(j p) c -> p j c", p=128)
            P.dma("pool", lambda e, wo=wo, so=so: e.dma_start(out=wo, in_=so), writes=[Bw], sb=Bw)
            for hf in range(nh):
                hs = slice(hf * 512, (hf + 1) * 512)
                bank = cnt["cacc"] % 6
                cnt["cacc"] += 1
                for j in range(FC):
                    P.op("pe", lambda e, wo=wo, j=j, hs=hs, bank=bank: e.matmul(
                        ps[:, bank, :], wo[:, j, :], act[:, j, hs], start=(j == 0), stop=(j == FC - 1)),
                        reads=[Bw, Bact[hf]], writes=[k.Bps[bank]])
                P.op("dve", lambda e, m=m, hs=hs, bank=bank, x=x, g=g: e.scalar_tensor_tensor(
                    out=x[:, m, hs], in0=ps[:, bank, :], scalar=k.gt[:, l, g, i, m:m + 1], in1=x[:, m, hs],
                    op0=ALU.mult, op1=ALU.add), reads=[k.Bps[bank], bx, k.Bcons], writes=[bx])

    pend = [None]

    def hook():
        if pend[0] is not None:
            if next(pend[0], "done") == "done":
                pend[0] = None

    stage_a(0)
    for ti in range(len(TILES)):
        tok0, nt, g = TILES[ti]
        stage_b(ti, hook)
        _drain(pend[0])
        pend[0] = None
        if ti + 1 < len(TILES):
            stage_a(ti + 1)
        stage_c(ti)
        x, bx = xb[ti % 2], Bx[ti % 2]
        pend[0] = ln_gen(k, l, i, x, bx, nt // 512, (rbf, rsq, mean, var, Bt, Bstat),
                         after=(lambda x=x, bx=bx, tok0=tok0, nt=nt: store_x_tile(k, x, bx, tok0, nt, last, stage, Bstage)))
    _drain(pend[0])
    P.barrier()


_W_NAMES = ["w_ada", "b_ada", "w_ffn1_in", "w_ffn1_out", "w_ffn2_in", "w_ffn2_out", "w_in", "conv_w", "conv_b",
            "conv_ln_g", "conv_ln_b", "gla_w_a2", "gla_b_a", "gla_norm_g", "diff_lam", "diff_norm_g", "w_out",
            "ln_g", "ln_b"]


def _const_mats():
    i = np.arange(128)
    same = (i[:, None] // 64) == (i[None, :] // 64)
    ident = np.eye(128, dtype=np.float32)
    tri_f = ((i[:, None] <= i[None, :]) & same).astype(np.float32)
    tri_b = ((i[:, None] >= i[None, :]) & same).astype(np.float32)
    stri_f = ((i[:, None] > i[None, :]) & same).astype(np.float32)
    stri_b = ((i[:, None] < i[None, :]) & same).astype(np.float32)
    rm = np.zeros((128, 128), np.float32)
    for m in range(128):
        if (m % 32) < 16:
            rm[m + 16, m] = -1.0
        else:
            rm[m - 16, m] = 1.0
    blk64 = same.astype(np.float32)
    return np.ascontiguousarray(np.stack([ident, tri_f, tri_b, stri_f, stri_b, rm, blk64], 0))


def _rope_tables():
    rows = T_S // 64
    row = np.repeat(np.arange(rows, dtype=np.float32), 64)
    col = np.tile(np.arange(64, dtype=np.float32), rows)
    seg = 32
    inv = (np.float32(10000.0) ** (-np.arange(0, seg, 2, dtype=np.float32) / np.float32(seg))).astype(np.float32)
    a_r = row[:, None] * inv
    a_c = col[:, None] * inv
    ang = np.concatenate([a_r, a_r, a_c, a_c], axis=-1).astype(np.float32)
    cos = np.cos(ang).astype(np.float32).T
    sin = np.sin(ang).astype(np.float32).T
    cos2 = np.concatenate([cos, cos], 0)
    sin2 = np.concatenate([sin, sin], 0)
    return np.ascontiguousarray(np.stack([cos2, sin2], 0))


def make_in_maps(inp):
    f = lambda a: np.ascontiguousarray(np.asarray(a, dtype=np.float32))
    shared = {n: f(inp[n]) for n in _W_NAMES}
    shared["cmat"] = _const_mats()
    shared["rope"] = _rope_tables()
    maps = []
    for c in range(NCORES):
        m = dict(shared)
        m["x_in"] = np.ascontiguousarray(np.concatenate(
            [f(inp["x_sample"][c]), f(inp["x_prompt"][2 * c]), f(inp["x_prompt"][2 * c + 1])], axis=0))
        m["cvec"] = np.ascontiguousarray(np.stack([f(inp["c"][c]), f(inp["c_ctx"])], axis=0))
        m["ck"] = f(inp["cache_diff_k"][c])
        m["cv"] = f(inp["cache_diff_v"][c])
        m["sg"] = f(inp["state_gla"][c])
        maps.append(m)
    return maps


def kernel(**inp):
    nc = build_nc()
    maps = make_in_maps(inp)
    res = run_bass_kernel_spmd(nc, maps, core_ids=list(range(NCORES)))
    R = res.results
    y_s = np.stack([R[c]["y"][:T_S] for c in range(NCORES)], axis=0)
    y_p = np.concatenate([R[c]["y"][T_S:].reshape(2, T_P, D) for c in range(NCORES)], axis=0)
    nk = np.concatenate([R[c]["nk"] for c in range(NCORES)], axis=0)
    nv = np.concatenate([R[c]["nv"] for c in range(NCORES)], axis=0)
    ng = np.concatenate([R[c]["ng"] for c in range(NCORES)], axis=0)
    return (y_p.astype(np.float32), y_s.astype(np.float32), nk.astype(np.float32), nv.astype(np.float32),
            ng.astype(np.float32))


SEQS = [(0, T_S, True, 0), (T_S, T_P, False, 1), (T_S + T_P, T_P, False, 1)]
MTILES = [(i * 512, 512, 0, True) for i in range(8)] + [(T_S, 512, 1, False)]
C_CONV, C_GQ, C_GK, C_GV, C_GG, C_LR, C_DQ, C_DK, C_DV = 0, 512, 768, 1024, 1280, 1536, 1568, 2080, 2592


def phase_mix_a(k, l):
    P, ps, I, S, Sb = k.P, k.ps, k.I, k.S, k.Sb
    ar = k.ar
    ar.reset()
    i = 1
    W = ar.bf16(KC, INC)
    WP = [0, 512, 1568, 2080, 2592, INC]
    BWp = [Buf("Wmix%d" % j) for j in range(5)]
    for pc in range(5):
        cols = slice(WP[pc], WP[pc + 1])
        P.dma("pool", lambda e, cols=cols: e.dma_start(
            out=W[:, :, cols], in_=I["w_in"][l][:, cols].rearrange("(kc p) c -> p kc c", p=128)), writes=[BWp[pc]], sb=BWp[pc])

    def wbufs(c0, c1):
        return [BWp[j] for j in range(5) if WP[j] < c1 and WP[j + 1] > c0]
    xb = [ar.f32(KC, 512)]
    Bx = [Buf("mx0")]
    hb = [ar.bf16(KC, 512) for _ in range(2)]
    Bhb = [Buf("mh0"), Buf("mh1")]
    hcur = [hb[0], Bhb[0]]
    tf = [ar.f32(512) for _ in range(2)]
    Btf = [Buf("tf0"), Buf("tf1")]
    YCt, SGt = ar.f32(2, 512), ar.f32(2, 512)
    BYC, BSG = Buf("YCt"), Buf("SGt")
    qf, kf = ar.f32(2, 512), ar.f32(2, 512)
    Bqf, Bkf = Buf("qf"), Buf("kf")
    lra = ar.f32(2, 512)
    Blra = Buf("lra")
    QDt, KDt = ar.bf16(4, 512), ar.bf16(4, 512)
    BQD, BKD = Buf("QDt"), Buf("KDt")
    qb = [ar.bf16(512) for _ in range(2)]
    Bqb = [Buf("qb0"), Buf("qb1")]
    rt = [ar.f32(512) for _ in range(2)]
    Brt = [Buf("rt0"), Buf("rt1")]
    cst = ar.f32(2, 512)
    Bcst = Buf("cossin")
    VGt = ar.bf16(4, 256)
    BVG = Buf("VGt")
    VDt = ar.bf16(4, 512)
    BVD = Buf("VDt")
    nkt, nvt = [ar.f32(512) for _ in range(2)], [ar.f32(512) for _ in range(2)]
    Bnk, Bnv = [Buf("nkt0"), Buf("nkt1")], [Buf("nvt0"), Buf("nvt1")]
    Bnko = Buf("nkv_out")
    k.outbufs.append(Bnko)
    ef2 = ar.f32(2, 512)
    Bef = Buf("ef")
    spf4 = ar.f32(2, 4, 256)
    Bsp = Buf("spf")
    ekh2 = ar.f32(4, 256)
    Bekh = Buf("ekh")
    ktm4 = ar.f32(4, 256)
    Bktm = Buf("ktm")
    eG4 = [ar.f32(512) for _ in range(2)]
    emG4 = [ar.f32(512) for _ in range(2)]
    BeG = [Buf("eG0"), Buf("eG1")]
    QTt = [ar.bf16(2, 512) for _ in range(2)]
    KTt = [ar.bf16(2, 512) for _ in range(2)]
    KHt = [ar.bf16(4, 256) for _ in range(2)]
    DECt = [ar.f32(2, 8) for _ in range(2)]
    BQT = [Buf("QTt0"), Buf("QTt1")]
    BKT = [Buf("KTt0"), Buf("KTt1")]
    BKH = [Buf("KHt0"), Buf("KHt1")]
    BDEC = [Buf("DECt0"), Buf("DECt1")]
    P.op("dve", lambda e: e.memset(lra[0:32], 1.0), writes=[Blra])
    bank_ctr = [0]

    def nb():
        b = bank_ctr[0] % 8
        bank_ctr[0] += 1
        return b

    tfc = [0]

    def fm(col0, bank, M=128):
        for kc in range(KC):
            P.op("pe", lambda e, kc=kc, h=hcur[0]: e.matmul(ps[0:M, bank, :], W[:, kc, col0:col0 + M], h[:, kc, :],
                                                               start=(kc == 0), stop=(kc == KC - 1)),
                 reads=wbufs(col0, col0 + M) + [hcur[1]], writes=[k.Bps[bank]])

    def gated(col_a, col_g, dst, Bdst):
        ba = nb()
        fm(col_a, ba)
        if col_g != col_a:
            bg = nb()
            fm(col_g, bg)
        else:
            bg = ba
        t = tf[tfc[0] % 2]
        Bt = Btf[tfc[0] % 2]
        tfc[0] += 1
        P.op("act", lambda e: e.activation(out=t, in_=ps[:, bg, :], func=AF.Exp, scale=-1.0), reads=[k.Bps[bg]], writes=[Bt])
        P.op("dve", lambda e: e.tensor_scalar(out=t, in0=t, scalar1=1.0, scalar2=None, op0=ALU.add), reads=[Bt], writes=[Bt])
        P.op("dve", lambda e: e.reciprocal(out=t, in_=t), reads=[Bt], writes=[Bt])
        P.op("dve", lambda e: e.tensor_tensor(out=dst, in0=ps[:, ba, :], in1=t, op=ALU.mult),
             reads=[k.Bps[ba], Bt], writes=[Bdst])

    def load_x(ti):
        tok0 = MTILES[ti][0]
        tsl = slice(tok0, tok0 + 512)
        P.dma("sp", lambda e, tsl=tsl: e.dma_start(out=xb[0], in_=k.XS[:, :, tsl]),
              reads=[xs_buf(k, (tok0 // 1024) * 1024 if tok0 < T_S else T_S)], writes=[Bx[0]], sb=Bx[0])

    def comp_h(ti):
        g = MTILES[ti][2]
        h_, Bh_ = hb[ti % 2], Bhb[ti % 2]
        for kc in range(KC):
            P.op("dve", lambda e, kc=kc, g=g, h_=h_: e.tensor_scalar(
                out=h_[:, kc, :], in0=xb[0][:, kc, :], scalar1=k.s1p[:, l, g, i, kc:kc + 1],
                scalar2=k.modT[:, l, (3 * i) * 8 + kc, g:g + 1], op0=ALU.mult, op1=ALU.add),
                reads=[Bx[0], k.Bcons], writes=[Bh_])

    load_x(0)
    comp_h(0)
    for ti, (tok0, nt, g, rope) in enumerate(MTILES):
        hcur[0], hcur[1] = hb[ti % 2], Bhb[ti % 2]
        h, Bh = hcur[0], hcur[1]
        tsl = slice(tok0, tok0 + 512)
        if ti + 1 < len(MTILES):
            load_x(ti + 1)
        if rope:
            P.dma("sp", lambda e, tsl=tsl: e.dma_start(out=cst, in_=I["rope"][:, :, tsl].rearrange("a p t -> p a t")),
                  writes=[Bcst], sb=Bcst)
        for cc in range(2):
            gated(C_CONV + cc * 128, C_CONV + 256 + cc * 128, YCt[:, cc, :], BYC)
        P.dma("sp", lambda e, tsl=tsl: e.dma_start(out=S["YC"][:, :, tsl], in_=YCt), reads=[BYC], writes=[Sb["YC"]], sb=BYC)
        for cc in range(2):
            gated(C_GG + cc * 128, C_GG + cc * 128, SGt[:, cc, :], BSG)
        P.dma("sp", lambda e, tsl=tsl: e.dma_start(out=S["SG"][:, :, tsl], in_=SGt), reads=[BSG], writes=[Sb["SG"]], sb=BSG)
        for cc in range(2):
            for col0, dst, Bd in ((C_GQ, qf, Bqf), (C_GK, kf, Bkf)):
                b = nb()
                fm(col0 + cc * 128, b)
                P.op("act", lambda e, b=b, dst=dst, cc=cc: e.activation(out=dst[:, cc, :], in_=ps[:, b, :], func=AF.Copy),
                     reads=[k.Bps[b]], writes=[Bd])
        for d_ in range(2):
            b = nb()
            fm(C_LR + d_ * 16, b, M=16)
            P.op("act", lambda e, b=b, d_=d_: e.activation(out=lra[0:16, d_, :], in_=ps[0:16, b, :], func=AF.Copy),
                 reads=[k.Bps[b]], writes=[Blra])
        if ti + 1 < len(MTILES):
            comp_h(ti + 1)
        pend_r = [None]
        for col0, dstt, Bd in ((C_DQ, QDt, BQD), (C_DK, KDt, BKD)):
            for hh in range(4):
                b = nb()
                fm(col0 + hh * 128, b)
                if not rope:
                    P.op("act", lambda e, b=b, dstt=dstt, hh=hh: e.activation(out=dstt[:, hh, :], in_=ps[:, b, :], func=AF.Copy),
                         reads=[k.Bps[b]], writes=[Bd])
                    continue
                q_ = qb[tfc[0] % 2]
                Bq = Bqb[tfc[0] % 2]
                r_ = rt[tfc[0] % 2]
                Br = Brt[tfc[0] % 2]
                t2 = tf[tfc[0] % 2]
                Bt2 = Btf[tfc[0] % 2]
                tfc[0] += 1
                P.op("act", lambda e, b=b, q_=q_: e.activation(out=q_, in_=ps[:, b, :], func=AF.Copy),
                     reads=[k.Bps[b]], writes=[Bq])
                P.op("dve", lambda e, b=b, r_=r_: e.tensor_tensor(out=r_, in0=ps[:, b, :], in1=cst[:, 0, :], op=ALU.mult),
                     reads=[k.Bps[b], Bcst], writes=[Br])

                def fin(q_=q_, Bq=Bq, r_=r_, Br=Br, t2=t2, Bt2=Bt2, dstt=dstt, hh=hh, Bd=Bd):
                    b2 = nb()
                    P.op("pe", lambda e, b2=b2: e.matmul(ps[:, b2, :], k.cmb[:, 5, :], q_, start=True, stop=True),
                         reads=[Bq, k.Bcons], writes=[k.Bps[b2]])
                    P.op("dve", lambda e, b2=b2: e.tensor_tensor(out=t2, in0=ps[:, b2, :], in1=cst[:, 1, :], op=ALU.mult),
                         reads=[k.Bps[b2], Bcst], writes=[Bt2])
                    P.op("dve", lambda e: e.tensor_tensor(out=dstt[:, hh, :], in0=r_, in1=t2, op=ALU.add),
                         reads=[Br, Bt2], writes=[Bd])

                if pend_r[0] is not None:
                    pend_r[0]()
                pend_r[0] = fin
        if pend_r[0] is not None:
            pend_r[0]()
        P.dma("sp", lambda e, tsl=tsl: e.dma_start(out=S["QD"][:, :, tsl], in_=QDt), reads=[BQD], writes=[Sb["QD"]], sb=BQD)
        P.dma("sp", lambda e, tsl=tsl: e.dma_start(out=S["KD"][:, :, tsl], in_=KDt), reads=[BKD], writes=[Sb["KD"]], sb=BKD)
        for bi in range(4):
            bsl = slice(bi * 128, (bi + 1) * 128)

            def tm(col0, bank, bsl=bsl):
                for kc in range(KC):
                    P.op("pe", lambda e, kc=kc, bsl=bsl, h=h: e.matmul(ps[:, bank, :], h[:, kc, bsl], W[:, kc, col0:col0 + 512],
                                                                        start=(kc == 0), stop=(kc == KC - 1)),
                         reads=wbufs(col0, col0 + 512) + [Bh], writes=[k.Bps[bank]])

            bkv = nb()
            tm(C_GK, bkv)
            P.op("act", lambda e, bkv=bkv, bi=bi: e.activation(out=VGt[:, bi, :], in_=ps[:, bkv, 256:512], func=AF.Copy),
                 reads=[k.Bps[bkv]], writes=[BVG])
            P.op("dve", lambda e, bkv=bkv, bi=bi: e.tensor_copy(out=ktm4[:, bi, :], in_=ps[:, bkv, 0:256]), reads=[k.Bps[bkv]], writes=[Bktm])
            bdv = nb()
            tm(C_DV, bdv)
            P.op("act", lambda e, bdv=bdv, bi=bi: e.activation(out=VDt[:, bi, :], in_=ps[:, bdv, :], func=AF.Copy),
                 reads=[k.Bps[bdv]], writes=[BVD])
            if not rope:
                sq_, b2 = bi // 2, bi % 2
                nv_, Bnv_ = nvt[bi % 2], Bnv[bi % 2]
                P.op("dve", lambda e, bdv=bdv, nv_=nv_: e.tensor_copy(out=nv_, in_=ps[:, bdv, :]),
                     reads=[k.Bps[bdv]], writes=[Bnv_])
                dst = k.O["nv"][sq_, l][:, b2 * 128:(b2 + 1) * 128, :].rearrange("h p c -> p h c")
                P.dma("sp", lambda e, dst=dst, nv_=nv_: e.dma_start(out=dst, in_=nv_.rearrange("p (h c) -> p h c", c=128)),
                      reads=[Bnv_], writes=[Bnko], sb=Bnv_)
                bdk = nb()
                tm(C_DK, bdk)
                nk_, Bnk_ = nkt[bi % 2], Bnk[bi % 2]
                P.op("dve", lambda e, bdk=bdk, nk_=nk_: e.tensor_copy(out=nk_, in_=ps[:, bdk, :]),
                     reads=[k.Bps[bdk]], writes=[Bnk_])
                dst = k.O["nk"][sq_, l][:, b2 * 128:(b2 + 1) * 128, :].rearrange("h p c -> p h c")
                P.dma("sp", lambda e, dst=dst, nk_=nk_: e.dma_start(out=dst, in_=nk_.rearrange("p (h c) -> p h c", c=128)),
                      reads=[Bnk_], writes=[Bnko], sb=Bnk_)
        for d_ in range(2):
            for bi in range(4):
                bsl = slice(bi * 128, (bi + 1) * 128)
                bank = 2 * d_ + bi // 2
                P.op("pe", lambda e, bank=bank, d_=d_, bsl=bsl, bi=bi: e.matmul(
                    ps[:, bank, (bi % 2) * 256:(bi % 2) * 256 + 256], lra[0:17, d_, bsl], k.wa2[0:17, l, d_, :], start=True, stop=True),
                    reads=[Blra, k.Bcons], writes=[k.Bps[bank]])
            P.op("act", lambda e, d_=d_: e.activation(out=ef2, in_=ps[:, 2 * d_:2 * d_ + 2, :], func=AF.Exp, scale=-1.0),
                 reads=[k.Bps[2 * d_], k.Bps[2 * d_ + 1]], writes=[Bef])
            P.op("act", lambda e, d_=d_: e.activation(out=spf4[:, d_, :, :], in_=ef2.rearrange("p a (b c) -> p (a b) c", c=256), func=AF.Ln, bias=1.0),
                 reads=[Bef], writes=[Bsp])
        for d_ in range(2):
            for bi in range(4):
                bank = 4 + 2 * d_ + bi // 2
                P.op("pe", lambda e, bank=bank, d_=d_, bi=bi: e.matmul(
                    ps[:, bank, (bi % 2) * 256:(bi % 2) * 256 + 256], k.cm[:, 3 + d_, :], spf4[:, d_, bi, :], start=True, stop=True),
                    reads=[Bsp, k.Bcons], writes=[k.Bps[bank]])
            P.op("act", lambda e, d_=d_: e.activation(out=ekh2.rearrange("p (a b) c -> p a (b c)", a=2), in_=ps[:, 4 + 2 * d_:4 + 2 * d_ + 2, :], func=AF.Exp, scale=-1.0 / 16.0),
                 reads=[k.Bps[4 + 2 * d_], k.Bps[4 + 2 * d_ + 1]], writes=[Bekh])
            P.op("dve", lambda e, d_=d_: e.tensor_tensor(out=KHt[d_], in0=ktm4, in1=ekh2, op=ALU.mult),
                 reads=[Bktm, Bekh], writes=[BKH[d_]])
        for d_ in range(2):
            for pr in range(2):
                bank = 2 * d_ + pr
                for bi in range(4):
                    P.op("pe", lambda e, bank=bank, d_=d_, pr=pr, bi=bi: e.matmul(
                        ps[:, bank, bi * 128:(bi + 1) * 128], spf4[:, d_, bi, pr * 128:(pr + 1) * 128], k.cm[:, 1 + d_, :],
                        start=True, stop=True), reads=[Bsp, k.Bcons], writes=[k.Bps[bank]])
                eg, emg, Beg = eG4[pr], emG4[pr], BeG[pr]
                P.op("act", lambda e, bank=bank, eg=eg: e.activation(out=eg, in_=ps[:, bank, :], func=AF.Exp, scale=-1.0 / 16.0),
                     reads=[k.Bps[bank]], writes=[Beg])
                P.op("act", lambda e, bank=bank, emg=emg: e.activation(out=emg, in_=ps[:, bank, :], func=AF.Exp, scale=1.0 / 16.0),
                     reads=[k.Bps[bank]], writes=[Beg])
                P.op("dve", lambda e, d_=d_, pr=pr, eg=eg: e.scalar_tensor_tensor(
                    out=QTt[d_][:, pr, :], in0=qf[:, pr, :], scalar=0.125, in1=eg, op0=ALU.mult, op1=ALU.mult),
                    reads=[Bqf, Beg], writes=[BQT[d_]])
                P.op("dve", lambda e, d_=d_, pr=pr, emg=emg: e.tensor_tensor(
                    out=KTt[d_][:, pr, :], in0=kf[:, pr, :], in1=emg, op=ALU.mult),
                    reads=[Bkf, Beg], writes=[BKT[d_]])
                c0 = 63 if d_ == 0 else 0
                P.op("dve", lambda e, d_=d_, pr=pr, eg=eg, c0=c0: e.tensor_copy(
                    out=DECt[d_][:, pr, :], in_=eg[:, c0:c0 + 449:64]),
                    reads=[Beg], writes=[BDEC[d_]])
        tb = slice(tok0, tok0 + 512)
        P.dma("sp", lambda e, tb=tb: e.dma_start(out=S["VG"][tb, :].rearrange("(b p) c -> p b c", p=128), in_=VGt),
              reads=[BVG], writes=[Sb["VG"]], sb=BVG)
        P.dma("sp", lambda e, tb=tb: e.dma_start(out=S["VD"][tb, :].rearrange("(b p) c -> p b c", p=128), in_=VDt),
              reads=[BVD], writes=[Sb["VD"]], sb=BVD)
        for d_ in range(2):
            P.dma("sp", lambda e, tb=tb, d_=d_: e.dma_start(out=S["KH%d" % d_][tb, :].rearrange("(b p) c -> p b c", p=128), in_=KHt[d_]),
                  reads=[BKH[d_]], writes=[Sb["KH%d" % d_]], sb=BKH[d_])
            P.dma("sp", lambda e, tsl=tsl, d_=d_: e.dma_start(out=S["QT%d" % d_][:, :, tsl], in_=QTt[d_]),
                  reads=[BQT[d_]], writes=[Sb["QT%d" % d_]], sb=BQT[d_])
            P.dma("sp", lambda e, tsl=tsl, d_=d_: e.dma_start(out=S["KT%d" % d_][:, :, tsl], in_=KTt[d_]),
                  reads=[BKT[d_]], writes=[Sb["KT%d" % d_]], sb=BKT[d_])
            P.dma("sp", lambda e, d_=d_, tok0=tok0: e.dma_start(out=S["DEC%d" % d_][:, :, tok0 // 64: tok0 // 64 + 8], in_=DECt[d_]),
                  reads=[BDEC[d_]], writes=[Sb["DEC%d" % d_]], sb=BDEC[d_])
    P.barrier()


def conv_alloc(k):
    ar = k.ar
    c = K()
    c.ypad = [ar.f32(2, T_S + 30), ar.f32(2, T_P + 30), ar.f32(2, T_P + 30)]
    c.acc = [ar.f32(2, T_S), ar.f32(2, T_P), ar.f32(2, T_P)]
    c.Byp = [[Buf("ypad%d_%d" % (s_, cc)) for cc in range(2)] for s_ in range(3)]
    c.Bacc = [[Buf("acc%d_%d" % (s_, cc)) for cc in range(2)] for s_ in range(3)]
    c.rbf, c.rsq = ar.bf16(2, 512), ar.bf16(2, 512)
    c.Bt = Buf("cln_t")
    c.mean, c.var = ar.f32(512), ar.f32(512)
    c.Bs = Buf("cln_s")
    c.u = ar.f32(2, 512)
    c.Bu = Buf("cln_u")
    c.ee = ar.f32(2, 512)
    c.Be = Buf("cln_e")
    c.yo = [ar.bf16(2, 512) for _ in range(2)]
    c.Byo = [Buf("cyo0"), Buf("cyo1")]
    return c


def conv_taps_gen(k, l, c):
    P, S, Sb = k.P, k.S, k.Sb
    for si, (tok0, T, ctx, g) in enumerate(SEQS):
        ypad, acc, Byp, Bacc = c.ypad[si], c.acc[si], c.Byp[si], c.Bacc[si]
        for cc in range(2):
            P.op("dve", lambda e, cc=cc, ypad=ypad: e.memset(ypad[:, cc, 0:15], 0.0), writes=[Byp[cc]])
            P.op("dve", lambda e, cc=cc, T=T, ypad=ypad: e.memset(ypad[:, cc, 15 + T:30 + T], 0.0), writes=[Byp[cc]])
            P.dma("sp", lambda e, cc=cc, T=T, tok0=tok0, ypad=ypad: e.dma_start(out=ypad[:, cc, 15:15 + T], in_=S["YC"][:, cc, tok0:tok0 + T]),
                  reads=[Sb["YC"]], writes=[Byp[cc]], sb=Byp[cc])
    for si, (tok0, T, ctx, g) in enumerate(SEQS):
        ypad, acc, Byp, Bacc = c.ypad[si], c.acc[si], c.Byp[si], c.Bacc[si]
        for cc in range(2):
            P.op("dve", lambda e, cc=cc, T=T, ypad=ypad, acc=acc: e.tensor_scalar(
                out=acc[:, cc, 0:T], in0=ypad[:, cc, 0:T], scalar1=k.convw[:, l, cc, 0:1],
                scalar2=k.convp[:, 0, l * 2 + cc:l * 2 + cc + 1], op0=ALU.mult, op1=ALU.add),
                reads=[Byp[cc], k.Bcons], writes=[Bacc[cc]])
            yield
            for j in range(1, 31):
                P.op("dve", lambda e, cc=cc, T=T, j=j, ypad=ypad, acc=acc: e.scalar_tensor_tensor(
                    out=acc[:, cc, 0:T], in0=ypad[:, cc, j:j + T], scalar=k.convw[:, l, cc, j:j + 1], in1=acc[:, cc, 0:T],
                    op0=ALU.mult, op1=ALU.add), reads=[Byp[cc], Bacc[cc], k.Bcons], writes=[Bacc[cc]])
                for _ in range(4 if T > 1024 else 1):
                    yield


def conv_ln_gen(k, l, c):
    P, ps, S, Sb = k.P, k.ps, k.S, k.Sb
    rbf, rsq, Bt, mean, var, Bs, u, Bu, ee, Be = c.rbf, c.rsq, c.Bt, c.mean, c.var, c.Bs, c.u, c.Bu, c.ee, c.Be
    nyo = 0
    for si, (tok0, T, ctx, g) in enumerate(SEQS):
        acc, Bacc = c.acc[si], c.Bacc[si]
        for t0 in range(0, T, 512):
            n = min(512, T - t0)
            sl = slice(t0, t0 + n)
            for cc in range(2):
                P.op("act", lambda e, cc=cc, sl=sl, n=n, acc=acc: e.activation(out=rbf[:, cc, 0:n], in_=acc[:, cc, sl], func=AF.Copy),
                     reads=[Bacc[cc]], writes=[Bt])
                P.op("act", lambda e, cc=cc, sl=sl, n=n, acc=acc: e.activation(out=rsq[:, cc, 0:n], in_=acc[:, cc, sl], func=AF.Square),
                     reads=[Bacc[cc]], writes=[Bt])
            yield
            for cc in range(2):
                P.op("pe", lambda e, cc=cc, n=n: e.matmul(ps[:, 7, 0:n], k.ones_bf, rbf[:, cc, 0:n], start=(cc == 0), stop=(cc == 1)),
                     reads=[Bt, k.Bcons], writes=[k.Bps[7]])
            P.op("dve", lambda e, n=n: e.tensor_scalar(out=mean[:, 0:n], in0=ps[:, 7, 0:n], scalar1=1.0 / 256, scalar2=None, op0=ALU.mult),
                 reads=[k.Bps[7]], writes=[Bs])
            for cc in range(2):
                P.op("pe", lambda e, cc=cc, n=n: e.matmul(ps[:, 7, 0:n], k.ones_bf, rsq[:, cc, 0:n], start=(cc == 0), stop=(cc == 1)),
                     reads=[Bt, k.Bcons], writes=[k.Bps[7]])
            P.op("dve", lambda e, n=n: e.tensor_tensor(out=var[:, 0:n], in0=mean[:, 0:n], in1=mean[:, 0:n], op=ALU.mult), reads=[Bs], writes=[Bs])
            P.op("dve", lambda e, n=n: e.scalar_tensor_tensor(out=var[:, 0:n], in0=ps[:, 7, 0:n], scalar=1.0 / 256, in1=var[:, 0:n],
                                                             op0=ALU.mult, op1=ALU.subtract), reads=[k.Bps[7], Bs], writes=[Bs])
            yield
            P.op("dve", lambda e, n=n: e.tensor_scalar(out=var[:, 0:n], in0=var[:, 0:n], scalar1=LN_EPS, scalar2=None, op0=ALU.add),
                 reads=[Bs], writes=[Bs])
            yield
            P.op("act", lambda e, n=n: e.activation(out=var[:, 0:n], in_=var[:, 0:n], func=AF.Ln), reads=[Bs], writes=[Bs])
            P.op("act", lambda e, n=n: e.activation(out=var[:, 0:n], in_=var[:, 0:n], func=AF.Exp, scale=-0.5), reads=[Bs], writes=[Bs])
            P.op("dve", lambda e, n=n: e.scalar_tensor_tensor(out=mean[:, 0:n], in0=mean[:, 0:n], scalar=-1.0, in1=var[:, 0:n],
                                                             op0=ALU.mult, op1=ALU.mult), reads=[Bs], writes=[Bs])
            yield
            y_ = c.yo[nyo % 2]
            By = c.Byo[nyo % 2]
            nyo += 1
            for cc in range(2):
                P.op("dve", lambda e, cc=cc, sl=sl, n=n, acc=acc: e.tensor_tensor(out=u[:, cc, 0:n], in0=acc[:, cc, sl], in1=var[:, 0:n], op=ALU.mult),
                     reads=[Bacc[cc], Bs], writes=[Bu])
                P.op("dve", lambda e, cc=cc, n=n: e.tensor_tensor(out=u[:, cc, 0:n], in0=u[:, cc, 0:n], in1=mean[:, 0:n], op=ALU.add),
                     reads=[Bu, Bs], writes=[Bu])
                P.op("act", lambda e, cc=cc, n=n: e.activation(
                    out=u[:, cc, 0:n], in_=u[:, cc, 0:n], func=AF.Identity,
                    scale=k.convp[:, 1, l * 2 + cc:l * 2 + cc + 1], bias=k.convp[:, 2, l * 2 + cc:l * 2 + cc + 1]),
                    reads=[Bu, k.Bcons], writes=[Bu])
                yield
                P.op("act", lambda e, cc=cc, n=n: e.activation(out=ee[:, cc, 0:n], in_=u[:, cc, 0:n], func=AF.Exp, scale=-1.0),
                     reads=[Bu], writes=[Be])
                yield
                P.op("dve", lambda e, cc=cc, n=n: e.tensor_scalar(out=ee[:, cc, 0:n], in0=ee[:, cc, 0:n], scalar1=1.0, scalar2=None, op0=ALU.add),
                     reads=[Be], writes=[Be])
                P.op("dve", lambda e, cc=cc, n=n: e.reciprocal(out=ee[:, cc, 0:n], in_=ee[:, cc, 0:n]), reads=[Be], writes=[Be])
                P.op("dve", lambda e, cc=cc, n=n, y_=y_: e.tensor_tensor(out=y_[:, cc, 0:n], in0=u[:, cc, 0:n], in1=ee[:, cc, 0:n], op=ALU.mult),
                     reads=[Bu, Be], writes=[By])
            P.dma("sp", lambda e, y_=y_, n=n, t0=t0, tok0=tok0: e.dma_start(
                out=S["YM"][:, 0:2, tok0 + t0:tok0 + t0 + n], in_=y_[:, :, 0:n]), reads=[By], writes=[Sb["YM"]], sb=By)


def phase_mix_out(k, l):
    P, ps, I, S, Sb = k.P, k.ps, k.I, k.S, k.Sb
    ar = k.ar
    ar.reset()
    i = 1
    Wo = ar.bf16(KC, D)
    BWo = Buf("Wo")
    BWo2 = [Buf("Wo_a"), Buf("Wo_b")]
    for j in range(2):
        P.dma("pool", lambda e, j=j: e.dma_start(
            out=Wo[:, :, j * 512:(j + 1) * 512], in_=I["w_out"][l][:, j * 512:(j + 1) * 512].rearrange("(kc p) c -> p kc c", p=128)),
            writes=[BWo2[j]], sb=BWo2[j])
    NBUF = 3
    xb = [ar.f32(KC, 1024) for _ in range(NBUF)]
    Bx = [Buf("ox%d" % j) for j in range(NBUF)]
    Bx2 = [Buf("oxh%d" % j) for j in range(NBUF)]
    NYB = 2
    ym = [ar.bf16(KC, 1024) for _ in range(NYB)]
    Bym = [Buf("ym%d" % j) for j in range(NYB)]
    tm2 = (ar.bf16(KC, 512), ar.bf16(KC, 512), ar.f32(512), ar.f32(512), Buf("lntmp2"), Buf("lnstat2"))
    rbf = ar.bf16(KC, 512)
    rsq = ar.bf16(KC, 512)
    mean = ar.f32(512)
    var = ar.f32(512)
    Bt = Buf("lntmp")
    Bstat = Buf("lnstat")
    cacc = [0]

    def loads(ti):
        tok0, nt, g = TILES[ti]
        x, bx, y_, By = xb[ti % NBUF], Bx[ti % NBUF], ym[ti % NYB], Bym[ti % NYB]
        P.dma("sp", lambda e, x=x, tok0=tok0, nt=nt: e.dma_start(out=x[:, :, 0:nt], in_=k.XS[:, :, tok0:tok0 + nt]),
              reads=[xs_buf(k, tok0)], writes=[bx, Bx2[ti % NBUF]], sb=bx)
        P.dma("sp", lambda e, y_=y_, tok0=tok0, nt=nt: e.dma_start(out=y_[:, :, 0:nt], in_=S["YM"][:, :, tok0:tok0 + nt]),
              reads=[Sb["YM"]], writes=[By], sb=By)

    pend = [None]

    def hook():
        if pend[0] is not None:
            if next(pend[0], "done") == "done":
                pend[0] = None

    loads(0)
    loads(1)
    for ti, (tok0, nt, g) in enumerate(TILES):
        x, bx, y_, By = xb[ti % NBUF], Bx[ti % NBUF], ym[ti % NYB], Bym[ti % NYB]
        bxh = [bx, Bx2[ti % NBUF]]
        nh = nt // 512
        for m in range(KC):
            for hf in range(nh):
                hs = slice(hf * 512, (hf + 1) * 512)
                bank = cacc[0] % 4
                cacc[0] += 1
                for kc in range(KC):
                    P.op("pe", lambda e, m=m, kc=kc, hs=hs, bank=bank, y_=y_: e.matmul(
                        ps[:, bank, :], Wo[:, kc, m * 128:(m + 1) * 128], y_[:, kc, hs], start=(kc == 0), stop=(kc == KC - 1)),
                        reads=[BWo2[m // 4], By], writes=[k.Bps[bank]])
                P.op("dve", lambda e, m=m, hs=hs, bank=bank, x=x, g=g: e.scalar_tensor_tensor(
                    out=x[:, m, hs], in0=ps[:, bank, :], scalar=k.gt[:, l, g, i, m:m + 1], in1=x[:, m, hs],
                    op0=ALU.mult, op1=ALU.add), reads=[k.Bps[bank], bxh[hf], k.Bcons], writes=[bxh[hf]])
                hook()
                hook()
        _drain(pend[0])
        pend[0] = None
        if ti + 2 < len(TILES):
            loads(ti + 2)

        def _store(x=x, bx=bx, bx2=Bx2[ti % NBUF], tok0=tok0, nt=nt):
            P.dma("sp", lambda e: e.dma_start(out=k.XS[:, :, tok0:tok0 + nt], in_=x[:, :, 0:nt]),
                  reads=[bx, bx2], writes=[xs_buf(k, tok0)], sb=bx)

        pend[0] = ln_gen(k, l, i, x, bx, nh, (rbf, rsq, mean, var, Bt, Bstat), after=_store, tmps2=tm2, Bx2=Bx2[ti % NBUF])
    _drain(pend[0])
    P.barrier()


def phase_gla(k, l):
    P, ps, I, S, Sb = k.P, k.ps, k.I, k.S, k.Sb
    ar = k.ar
    ar.reset()
    NBmax, NCmax = NTOK // 128, T_S // 64
    QTa = [ar.bf16(2, NTOK) for _ in range(2)]
    KTa = [ar.bf16(2, NTOK) for _ in range(2)]
    KHa = [ar.bf16(NBmax, 256) for _ in range(2)]
    VGa = ar.bf16(NBmax, 256)
    DECa = [ar.f32(2, NTOK // 64) for _ in range(2)]
    SA = [ar.bf16(NCmax, 2, 64) for _ in range(2)]
    Sf = [[ar.f32(2, 64) for _ in range(2)] for _ in range(2)]
    BQT, BKT, BKH, BDEC, BSA = ([Buf("gq%d" % d) for d in range(2)], [Buf("gk%d" % d) for d in range(2)],
                                [Buf("gkh%d" % d) for d in range(2)], [Buf("gdec%d" % d) for d in range(2)],
                                [Buf("gsa%d" % d) for d in range(2)])
    BVG = Buf("gvg")
    BSf = [[Buf("sf%d%d" % (d, j)) for j in range(2)] for d in range(2)]
    mask4 = ar.f32(4, 128)
    Bm = Buf("mask4")
    for q_ in range(4):
        P.op("dve", lambda e, q_=q_: e.tensor_copy(out=mask4[:, q_, :], in_=k.cm[:, 1 + q_ % 2, :]), reads=[k.Bcons], writes=[Bm])
    Am = [ar.bf16(2, 4, 128) for _ in range(2)]
    BAm = [Buf("Am0"), Buf("Am1")]
    pend_epi = [None]
    sq = ar.bf16(512)
    Bsq = Buf("gsq")
    lnv = ar.f32(512)
    Bln = Buf("glnv")
    tt = ar.f32(512)
    Btt = Buf("gtt")
    sgt = [ar.f32(2, 512) for _ in range(2)]
    Bsgt = [Buf("sgt0"), Buf("sgt1")]
    yo = [ar.bf16(2, 512) for _ in range(2)]
    Byo = [Buf("gyo0"), Buf("gyo1")]
    Bout = Buf("ng_out")
    k.outbufs.append(Bout)
    nA = 0
    nG = 0
    for d in range(2):
        P.dma("sp", lambda e, d=d: e.dma_start(out=KHa[d], in_=S["KH%d" % d].rearrange("(b p) c -> p b c", p=128)),
              reads=[Sb["KH%d" % d]], writes=[BKH[d]], sb=BKH[d])
    P.dma("sp", lambda e: e.dma_start(out=VGa, in_=S["VG"].rearrange("(b p) c -> p b c", p=128)),
          reads=[Sb["VG"]], writes=[BVG], sb=BVG)
    for d in range(2):
        P.dma("sp", lambda e, d=d: e.dma_start(out=DECa[d], in_=S["DEC%d" % d]), reads=[Sb["DEC%d" % d]], writes=[BDEC[d]], sb=BDEC[d])
    for d in range(2):
        P.dma("sp", lambda e, d=d: e.dma_start(out=QTa[d], in_=S["QT%d" % d]), reads=[Sb["QT%d" % d]], writes=[BQT[d]], sb=BQT[d])
        P.dma("sp", lambda e, d=d: e.dma_start(out=KTa[d], in_=S["KT%d" % d]), reads=[Sb["KT%d" % d]], writes=[BKT[d]], sb=BKT[d])
    def do_seq(si, tok0, T, ctx, g):
        nonlocal nA, nG
        NB, NCH = T // 128, T // 64
        tsl = slice(tok0, tok0 + T)
        QT = [QTa[d][:, :, tsl] for d in range(2)]
        KT = [KTa[d][:, :, tsl] for d in range(2)]
        KH = [KHa[d][:, tok0 // 128: tok0 // 128 + NB, :] for d in range(2)]
        VG = VGa[:, tok0 // 128: tok0 // 128 + NB, :]
        DEC = [DECa[d][:, :, tok0 // 64: tok0 // 64 + NCH] for d in range(2)]
        for d in range(2):
            if ctx:
                fns = []
                for hh in range(2):
                    src = I["sg"][l, d].rearrange("(pr hh) dk dv -> hh dk pr dv", hh=2)[hh]
                    fns.append(lambda e, d=d, hh=hh, src=src: e.dma_start(out=Sf[d][0][hh * 64:(hh + 1) * 64, :, :], in_=src))
                P.dma("sp", fns, writes=[BSf[d][0]], sb=BSf[d][0])
            else:
                P.op("dve", lambda e, d=d: e.memset(Sf[d][0], 0.0), writes=[BSf[d][0]])
        for step in range(NCH):
            for d in range(2):
                n = step if d == 0 else NCH - 1 - step
                blk, half = n // 2, n % 2
                s_ = 2 * step + d
                bank, slot = s_ % 4, (s_ // 4) % 4
                for h in range(4):
                    hh, pr = h % 2, h // 2
                    P.op("pe", lambda e, d=d, blk=blk, half=half, h=h, hh=hh, pr=pr, bank=bank, slot=slot: e.matmul(
                        ps[hh * 64:(hh + 1) * 64, bank, slot * 128 + pr * 64: slot * 128 + pr * 64 + 64],
                        KH[d][half * 64:(half + 1) * 64, blk, h * 64:(h + 1) * 64],
                        VG[half * 64:(half + 1) * 64, blk, h * 64:(h + 1) * 64], start=True, stop=True),
                        reads=[BKH[d], BVG], writes=[k.Bps[bank]])
                cur, nxt = step % 2, (step + 1) % 2
                P.op("act", lambda e, d=d, n=n, cur=cur: e.activation(
                    out=SA[d][:, n, :, :], in_=Sf[d][cur], func=AF.Copy), reads=[BSf[d][cur]], writes=[BSA[d]])
                for pr in range(2):
                    P.op("dve", lambda e, d=d, n=n, pr=pr, cur=cur, nxt=nxt, bank=bank, slot=slot: e.scalar_tensor_tensor(
                        out=Sf[d][nxt][:, pr, :], in0=Sf[d][cur][:, pr, :], scalar=DEC[d][:, pr, n:n + 1],
                        in1=ps[:, bank, slot * 128 + pr * 64: slot * 128 + pr * 64 + 64], op0=ALU.mult, op1=ALU.add),
                        reads=[BSf[d][cur], BDEC[d], k.Bps[bank]], writes=[BSf[d][nxt]])
        fin = NCH % 2
        if not ctx:
            for d in range(2):
                fns = []
                for hh in range(2):
                    dst = k.O["ng"][si - 1, l, d].rearrange("(pr hh) dk dv -> hh dk pr dv", hh=2)[hh]
                    fns.append(lambda e, d=d, hh=hh, dst=dst, fin=fin: e.dma_start(out=dst, in_=Sf[d][fin][hh * 64:(hh + 1) * 64, :, :]))
                P.dma("sp", fns, reads=[BSf[d][fin]], writes=[Bout], sb=BSf[d][fin])
        GB = min(4, NB)
        for gi in range(NB // GB):
            gt0 = gi * GB * 128
            ncol = GB * 128
            sg_ = sgt[nG % 2]
            Bsg_ = Bsgt[nG % 2]
            y_ = yo[nG % 2]
            By = Byo[nG % 2]
            nG += 1
            P.dma("sp", lambda e, sg_=sg_, ncol=ncol, a=tok0 + gt0: e.dma_start(out=sg_[:, :, 0:ncol], in_=S["SG"][:, :, a:a + ncol]),
                  reads=[Sb["SG"]], writes=[Bsg_], sb=Bsg_)
            for pr in range(2):
                bO = [2 * pr, 2 * pr + 1]
                for rd in range((GB + 1) // 2):
                    nb2 = min(2, GB - 2 * rd)
                    am = Am[nA % 2]
                    Bam = BAm[nA % 2]
                    nA += 1
                    for b2 in range(nb2):
                        blk = gi * GB + rd * 2 + b2
                        bt = slice(blk * 128, (blk + 1) * 128)
                        for d in range(2):
                            for hh in range(2):
                                P.op("pe", lambda e, d=d, hh=hh, pr=pr, bt=bt, b2=b2: e.matmul(
                                    ps[:, 4 + hh, (b2 * 2 + d) * 128:(b2 * 2 + d + 1) * 128],
                                    KT[d][hh * 64:(hh + 1) * 64, pr, bt], QT[d][hh * 64:(hh + 1) * 64, pr, bt], start=True, stop=True),
                                    reads=[BKT[d], BQT[d]], writes=[k.Bps[4 + hh]])
                    for hh in range(2):
                        P.op("dve", lambda e, am=am, hh=hh, nb2=nb2: e.tensor_tensor(
                            out=am[:, hh, 0:2 * nb2, :], in0=ps[:, 4 + hh, 0:256 * nb2].rearrange("p (a b) -> p a b", b=128),
                            in1=mask4[:, 0:2 * nb2, :], op=ALU.mult),
                            reads=[k.Bps[4 + hh], Bm], writes=[Bam])
                    if rd == 0 and pend_epi[0] is not None:
                        pend_epi[0]()
                        pend_epi[0] = None
                    for b2 in range(nb2):
                        bl = rd * 2 + b2
                        blk = gi * GB + bl
                        cols = slice(bl * 128, (bl + 1) * 128)
                        for hh in range(2):
                            h = pr * 2 + hh
                            for d in range(2):
                                P.op("pe", lambda e, d=d, hh=hh, h=h, blk=blk, am=am, cols=cols, bO=bO, b2=b2: e.matmul(
                                    ps[hh * 64:(hh + 1) * 64, bO[hh], cols], VG[:, blk, h * 64:(h + 1) * 64], am[:, hh, b2 * 2 + d, :],
                                    start=(d == 0), stop=False), reads=[BVG, Bam], writes=[k.Bps[bO[hh]]])
                                for cc in range(2):
                                    n = 2 * blk + cc
                                    ct = slice(n * 64, (n + 1) * 64)
                                    oc = slice(bl * 128 + cc * 64, bl * 128 + cc * 64 + 64)
                                    P.op("pe", lambda e, d=d, hh=hh, n=n, pr=pr, ct=ct, oc=oc, cc=cc, bO=bO: e.matmul(
                                        ps[hh * 64:(hh + 1) * 64, bO[hh], oc], SA[d][hh * 64:(hh + 1) * 64, n, pr, :],
                                        QT[d][hh * 64:(hh + 1) * 64, pr, ct], start=False, stop=(d == 1 and cc == 1)),
                                        reads=[BSA[d], BQT[d]], writes=[k.Bps[bO[hh]]])

                def epi(pr=pr, bO=bO, ncol=ncol, y_=y_, sg_=sg_, By=By, Bsg_=Bsg_, last=(pr == 1), a=tok0 + gt0):
                    for hh in range(2):
                        P.op("act", lambda e, hh=hh: e.activation(
                            out=sq[hh * 64:(hh + 1) * 64, 0:ncol], in_=ps[hh * 64:(hh + 1) * 64, bO[hh], 0:ncol], func=AF.Square),
                            reads=[k.Bps[bO[hh]]], writes=[Bsq])
                    bankR = 6 + pr
                    P.op("pe", lambda e: e.matmul(ps[:, bankR, 0:ncol], k.cmb[:, 6, :], sq[:, 0:ncol], start=True, stop=True),
                         reads=[Bsq, k.Bcons], writes=[k.Bps[bankR]])
                    P.op("dve", lambda e: e.tensor_scalar(
                        out=lnv[:, 0:ncol], in0=ps[:, bankR, 0:ncol], scalar1=1.0 / 64, scalar2=LN_EPS, op0=ALU.mult, op1=ALU.add),
                        reads=[k.Bps[bankR]], writes=[Bln])
                    P.op("act", lambda e: e.activation(out=lnv[:, 0:ncol], in_=lnv[:, 0:ncol], func=AF.Ln), reads=[Bln], writes=[Bln])
                    P.op("act", lambda e: e.activation(out=lnv[:, 0:ncol], in_=lnv[:, 0:ncol], func=AF.Exp, scale=-0.5),
                         reads=[Bln], writes=[Bln])
                    for hh in range(2):
                        P.op("dve", lambda e, hh=hh: e.tensor_tensor(
                            out=tt[hh * 64:(hh + 1) * 64, 0:ncol], in0=ps[hh * 64:(hh + 1) * 64, bO[hh], 0:ncol],
                            in1=lnv[hh * 64:(hh + 1) * 64, 0:ncol], op=ALU.mult),
                            reads=[k.Bps[bO[hh]], Bln], writes=[Btt])
                    P.op("dve", lambda e: e.scalar_tensor_tensor(
                        out=y_[:, pr, 0:ncol], in0=tt[:, 0:ncol], scalar=k.gnorm[:, l:l + 1], in1=sg_[:, pr, 0:ncol],
                        op0=ALU.mult, op1=ALU.mult), reads=[Btt, Bsg_, k.Bcons], writes=[By])
                    if last:
                        P.dma("sp", lambda e: e.dma_start(out=S["YM"][:, 2:4, a:a + ncol], in_=y_[:, :, 0:ncol]),
                              reads=[By], writes=[Sb["YM"]], sb=By)

                if pend_epi[0] is not None:
                    pend_epi[0]()
                pend_epi[0] = epi
        if pend_epi[0] is not None:
            pend_epi[0]()
            pend_epi[0] = None

    for si, (tok0, T, ctx, g) in enumerate(SEQS):
        do_seq(si, tok0, T, ctx, g)
    if pend_epi[0] is not None:
        pend_epi[0]()
    P.barrier()


def phase_attn(k, l):
    P, ps, I, S, Sb = k.P, k.ps, k.I, k.S, k.Sb
    ar = k.ar
    ar.reset()
    NKmax = T_S + PAST
    KTb = [ar.bf16(NKmax) for _ in range(2)]
    Vb = [ar.bf16(NKmax // 128, 130) for _ in range(2)]
    Qb = [ar.bf16(T_S) for _ in range(2)]
    BKTb = [Buf("aK0"), Buf("aK1")]
    BVb = [Buf("aV0"), Buf("aV1")]
    BQb = [Buf("aQ0"), Buf("aQ1")]
    ckst = ar.f32(4, 128)
    Bck = Buf("ckst")
    E = [ar.bf16(2, 512) for _ in range(3)]
    BE = [Buf("E%d" % j) for j in range(3)]
    rz = ar.f32(8)
    Brz = Buf("rz")
    tO = ar.f32(8, 128)
    BtO = Buf("tO")
    od = ar.f32(4, 128)
    Bod = Buf("od")
    sq = ar.f32(4, 128)
    Bsq = Buf("asq")
    ss = ar.f32(4)
    Bss = Buf("ass")
    yb = ar.bf16(4, 128)
    Byb = Buf("ayb")
    yT = [ar.bf16(512) for _ in range(2)]
    ByT = [Buf("yT0"), Buf("yT1")]
    for j in range(2):
        P.op("dve", lambda e, j=j: e.memset(Vb[j][:, :, 128:130], 1.0), writes=[BVb[j]])
    cv_ = conv_alloc(k)
    def _chain():
        yield from conv_taps_gen(k, l, cv_)
        yield from conv_ln_gen(k, l, cv_)
    cgen = [_chain()]

    def chook():
        if cgen[0] is not None:
            if next(cgen[0], "done") == "done":
                cgen[0] = None
    nE = [0]
    nY = [0]
    heads = []
    for si, (tok0, T, ctx, g) in enumerate(SEQS):
        for hh in range(4):
            hc = K()
            hc.si, hc.hh, hc.tok0, hc.T, hc.ctx = si, hh, tok0, T, ctx
            hc.NKT = (T + (PAST if ctx else 0)) // 128
            hc.QB = min(512, T)
            hc.QS = hc.QB // 128
            j = len(heads) % 2
            hc.Kt, hc.Vt, hc.Qt, hc.BK, hc.BV, hc.BQ = KTb[j], Vb[j], Qb[j], BKTb[j], BVb[j], BQb[j]
            heads.append(hc)
    units = [(hi_, qb) for hi_, hc in enumerate(heads) for qb in range(hc.T // hc.QB)]

    def emit_loads(hc):
        Kt, Vt, Qt, BK, BV, BQ, hh, T = hc.Kt, hc.Vt, hc.Qt, hc.BK, hc.BV, hc.BQ, hc.hh, hc.T
        tsl = slice(hc.tok0, hc.tok0 + T)
        P.dma("sp", lambda e: e.dma_start(out=Qt[:, 0:T], in_=S["QD"][:, hh, tsl]), reads=[Sb["QD"]], writes=[BQ], sb=BQ)
        P.dma("sp", lambda e: e.dma_start(out=Kt[:, 0:T], in_=S["KD"][:, hh, tsl]), reads=[Sb["KD"]], writes=[BK], sb=BK)
        P.dma("sp", lambda e: e.dma_start(
            out=Vt[:, 0:T // 128, 0:128], in_=S["VD"][tsl, hh * 128:(hh + 1) * 128].rearrange("(b p) c -> p b c", p=128)),
            reads=[Sb["VD"]], writes=[BV], sb=BV)
        if hc.ctx:
            P.dma("pool", lambda e: e.dma_start(
                out=Vt[:, T // 128:T // 128 + 4, 0:128], in_=I["cv"][l, hh].rearrange("(b p) c -> p b c", p=128)),
                writes=[BV], sb=BV)
            P.dma("sp", lambda e: e.dma_start(out=ckst, in_=I["ck"][l, hh].rearrange("(b p) c -> p b c", p=128)),
                  writes=[Bck], sb=Bck)
            for b4 in range(4):
                P.op("pe", lambda e, b4=b4: e.transpose(out=ps[:, 7, b4 * 128:(b4 + 1) * 128], in_=ckst[:, b4, :], identity=k.ident),
                     reads=[Bck, k.Bcons], writes=[k.Bps[7]])
            P.op("dve", lambda e: e.tensor_copy(out=Kt[:, T:T + 512], in_=ps[:, 7, :]), reads=[k.Bps[7]], writes=[BK])

    def qk(ui, kt):
        hc = heads[units[ui][0]]
        qb = units[ui][1]
        Kt, Qt, QB = hc.Kt, hc.Qt, hc.QB
        qsl = slice(qb * QB, (qb + 1) * QB)
        sb_ = kt % 2
        for mp in range(2):
            bank = sb_ * 2 + mp
            P.op("pe", lambda e, mp=mp, bank=bank: e.matmul(
                ps[:, bank, 0:QB], Kt[mp * 64:(mp + 1) * 64, kt * 128:(kt + 1) * 128], Qt[mp * 64:(mp + 1) * 64, qsl],
                start=True, stop=True), reads=[hc.BK, hc.BQ], writes=[k.Bps[bank]])

    def epi_gen(hc, qb):
        QB, QS, hh, tok0 = hc.QB, hc.QS, hc.hh, hc.tok0
        ns = 2 * QS
        for b3 in range((ns + 2) // 3):
            nsb = min(3, ns - 3 * b3)
            pv = ps[:, 4 + b3, 0:480].rearrange("p (s c) -> p s c", c=160)
            P.op("dve", lambda e, b3=b3, nsb=nsb, pv=pv: e.reciprocal(out=rz[:, 3 * b3:3 * b3 + nsb], in_=pv[:, 0:nsb, 128]),
                 reads=[k.Bps[4 + b3]], writes=[Brz])
            P.op("dve", lambda e, b3=b3, nsb=nsb, pv=pv: e.tensor_tensor(
                out=tO[:, 3 * b3:3 * b3 + nsb, :], in0=pv[:, 0:nsb, 0:128],
                in1=rz[:, 3 * b3:3 * b3 + nsb].unsqueeze(2).broadcast_to([128, nsb, 128]), op=ALU.mult),
                reads=[k.Bps[4 + b3], Brz], writes=[BtO])
        tv = tO.rearrange("p (q m) c -> p q m c", m=2)
        P.op("dve", lambda e: e.scalar_tensor_tensor(
            out=od[:, 0:QS, :], in0=tv[:, 0:QS, 1, :], scalar=k.nlam[:, l:l + 1], in1=tv[:, 0:QS, 0, :],
            op0=ALU.mult, op1=ALU.add), reads=[BtO, k.Bcons], writes=[Bod])
        P.op("dve", lambda e: e.tensor_tensor(out=sq[:, 0:QS, :], in0=od[:, 0:QS, :], in1=od[:, 0:QS, :], op=ALU.mult),
             reads=[Bod], writes=[Bsq])
        P.op("dve", lambda e: e.reduce_sum(out=ss[:, 0:QS], in_=sq[:, 0:QS, :], axis=AX.X), reads=[Bsq], writes=[Bss])
        P.op("dve", lambda e: e.tensor_scalar(out=ss[:, 0:QS], in0=ss[:, 0:QS], scalar1=1.0 / 128, scalar2=LN_EPS,
                                              op0=ALU.mult, op1=ALU.add), reads=[Bss], writes=[Bss])
        yield
        P.op("act", lambda e: e.activation(out=ss[:, 0:QS], in_=ss[:, 0:QS], func=AF.Ln), reads=[Bss], writes=[Bss])
        P.op("act", lambda e: e.activation(out=ss[:, 0:QS], in_=ss[:, 0:QS], func=AF.Exp, scale=-0.5), reads=[Bss], writes=[Bss])
        yield
        P.op("dve", lambda e: e.tensor_tensor(
            out=yb[:, 0:QS, :], in0=od[:, 0:QS, :], in1=ss[:, 0:QS].unsqueeze(2).broadcast_to([128, QS, 128]), op=ALU.mult),
            reads=[Bod, Bss], writes=[Byb])
        for qs in range(QS):
            P.op("pe", lambda e, qs=qs: e.matmul(ps[:, 7, qs * 128:(qs + 1) * 128], yb[:, qs, :], k.ident_bf, start=True, stop=True),
                 reads=[Byb, k.Bcons], writes=[k.Bps[7]])
        y_ = yT[nY[0] % 2]
        By = ByT[nY[0] % 2]
        nY[0] += 1
        P.op("act", lambda e: e.activation(out=y_[:, 0:QB], in_=ps[:, 7, 0:QB], func=AF.Copy, scale=k.dnorm[:, l:l + 1]),
             reads=[k.Bps[7], k.Bcons], writes=[By])
        a_ = tok0 + qb * QB
        P.dma("sp", lambda e: e.dma_start(out=S["YM"][:, 4 + hh, a_:a_ + QB], in_=y_[:, 0:QB]),
              reads=[By], writes=[Sb["YM"]], sb=By)
        yield

    pend = [None]

    def step_epi():
        if pend[0] is not None:
            if next(pend[0], "done") == "done":
                pend[0] = None

    emit_loads(heads[0])
    qk(0, 0)
    for ui, (hi_, qb) in enumerate(units):
        hc = heads[hi_]
        NKT, QB, QS, Vt = hc.NKT, hc.QB, hc.QS, hc.Vt
        if qb == 0 and hi_ + 1 < len(heads):
            emit_loads(heads[hi_ + 1])
        for kt in range(NKT):
            sb_ = kt % 2
            e_ = E[nE[0] % 3]
            Be_ = BE[nE[0] % 3]
            nE[0] += 1
            P.op("act", lambda e, e_=e_, sb_=sb_, QB=QB: e.activation(
                out=e_[:, :, 0:QB], in_=ps[:, sb_ * 2:sb_ * 2 + 2, 0:QB], func=AF.Exp, scale=0.125),
                reads=[k.Bps[sb_ * 2], k.Bps[sb_ * 2 + 1]], writes=[Be_])
            if kt + 1 < NKT:
                qk(ui, kt + 1)
            elif ui + 1 < len(units):
                qk(ui + 1, 0)
            if kt == 0:
                step_epi()
            elif kt in (2, 3, 5):
                step_epi()
            for qs in range(QS):
                for mp in range(2):
                    slot = qs * 2 + mp
                    bank = 4 + slot // 3
                    off = (slot % 3) * 160
                    P.op("pe", lambda e, e_=e_, kt=kt, qs=qs, mp=mp, bank=bank, off=off, slot=slot, Vt=Vt, NKT=NKT: e.matmul(
                        ps[:, bank, off:off + 129], e_[:, mp, qs * 128:(qs + 1) * 128], Vt[:, kt, 0:129],
                        start=(kt == 0 and slot % 3 == 0), stop=(kt == NKT - 1), skip_group_check=True),
                        reads=[Be_, hc.BV], writes=[k.Bps[bank]])
            chook()
        _drain(pend[0])
        pend[0] = epi_gen(hc, qb)
    _drain(pend[0])
    _drain(cgen[0])
    P.barrier()
```

```python
import math
from contextlib import ExitStack

import numpy as np
import concourse.bass as bass
import concourse.mybir as mybir
from concourse.bass_utils import run_bass_kernel_spmd

F32 = mybir.dt.float32
BF16 = mybir.dt.bfloat16
AF = mybir.ActivationFunctionType
ALU = mybir.AluOpType
AX = mybir.AxisListType

ENGS = ("pe", "act", "dve", "pool", "sp")
SEM_LIMIT = 30000


class Buf:
    __slots__ = ("name", "w", "r", "dsem", "excl", "epoch")

    def __init__(self, name="", excl=False):
        self.name = name
        self.w = None
        self.r = {}
        self.dsem = None
        self.epoch = -1
        self.excl = excl


class DSem:
    _n = 0

    def __init__(self):
        self.id = "d%d" % DSem._n
        DSem._n += 1
        self.total = 0
        self.last = None
        self.handle = None
        self.kind = None


class Op:
    __slots__ = ("eng", "idx", "fn", "deps", "dma", "need_inc", "waits", "known", "incval")

    def __init__(self, eng, idx, fn, deps, dma):
        self.eng = eng
        self.idx = idx
        self.fn = fn
        self.deps = deps
        self.dma = dma
        self.need_inc = False
        self.waits = []
        self.known = None
        self.incval = 0


class Prog:
    def __init__(self):
        self.ops = {e: [] for e in ENGS}
        self.order = []
        self.dsems = []
        self.free = []
        self.epoch = 0

    def _add(self, eng, fn, reads, writes, dma_sem=None, ndma=1):
        deps = {}

        def add_dep(t):
            if t is None:
                return
            key = t[1] if t[0] == "c" else t[1].id
            if key not in deps or deps[key][2] < t[2]:
                deps[key] = t

        for b in reads:
            add_dep(b.w)
            if b.excl:
                for t in b.r.values():
                    add_dep(t)
        for b in writes:
            add_dep(b.w)
            for t in b.r.values():
                add_dep(t)
        raw_same = set()
        for b in reads:
            if b.w is not None and b.w[0] == "c" and b.w[1] == eng:
                raw_same.add(b.w[2])
        if eng in deps and eng == "pe":
            del deps[eng]
        idx = len(self.ops[eng])
        dma = None
        if dma_sem is not None:
            if dma_sem.last is not None:
                add_dep(dma_sem.last)
            dma_sem.total += 16 * ndma
            dma = (dma_sem, dma_sem.total, ndma)
        op = Op(eng, idx, fn, list(deps.values()), dma)
        self.ops[eng].append(op)
        self.order.append(op)
        if dma is not None:
            tok = ("d", dma_sem, dma_sem.total, op)
            dma_sem.last = tok
            rkey = dma_sem.id
        else:
            tok = ("c", eng, idx)
            rkey = eng
        for b in reads:
            b.r[rkey] = tok
        for b in writes:
            b.w = tok
            b.r = {}
        return op

    def op(self, eng, fn, reads=(), writes=()):
        return self._add(eng, fn, list(reads), list(writes))

    def dma(self, eng, fns, reads=(), writes=(), sb=None):
        if not isinstance(fns, (list, tuple)):
            fns = [fns]
        b = sb
        kind = "sw" if eng == "pool" else "hw"
        if b.dsem is None or b.epoch != self.epoch or b.dsem.kind != kind or b.dsem.total + 16 * len(fns) > SEM_LIMIT:
            ds = None
            rest = []
            while self.free:
                c_ = self.free.pop()
                if c_.kind == kind and c_.total + 16 * len(fns) + 2000 <= SEM_LIMIT:
                    ds = c_
                    break
                rest.append(c_)
            self.free.extend(rest)
            if ds is None:
                ds = DSem()
                ds.kind = kind
                self.dsems.append(ds)
            b.dsem = ds
            b.epoch = self.epoch
        return self._add(eng, list(fns), list(reads), list(writes), dma_sem=b.dsem, ndma=len(fns))

    def barrier(self):
        self.epoch += 1
        self.free = list(self.dsems)
        toks = []
        for e in ENGS:
            if self.ops[e]:
                o = self.ops[e][-1]
                if o.dma is None:
                    toks.append(("c", e, o.idx))
        for s in self.dsems:
            if s.last is not None:
                toks.append(s.last)
        for e in ENGS:
            deps = {}
            for t in toks:
                if t[0] == "c":
                    if t[1] == e:
                        continue
                    deps[t[1]] = t
                else:
                    deps[t[1].id] = t
            idx = len(self.ops[e])
            op = Op(e, idx, None, list(deps.values()), None)
            self.ops[e].append(op)
            self.order.append(op)

    def resolve(self):
        cur = {e: {} for e in ENGS}
        for op in self.order:
            k = cur[op.eng]
            newk = None
            for t in op.deps:
                base = newk if newk is not None else k
                if t[0] == "c":
                    _, e, i = t
                    if base.get(e, -1) >= i:
                        continue
                    src = self.ops[e][i]
                    if src.fn is None:
                        pass
                    src.need_inc = True
                    key, val = e, i
                else:
                    _, s, v, src = t
                    if base.get(s.id, -1) >= v:
                        continue
                    key, val = s.id, v
                op.waits.append(t)
                if newk is None:
                    newk = dict(k)
                if src.known:
                    for kk, vv in src.known.items():
                        if newk.get(kk, -1) < vv:
                            newk[kk] = vv
                if newk.get(key, -1) < val:
                    newk[key] = val
            if newk is not None:
                cur[op.eng] = newk
                k = newk
            op.known = k
        self.ninc = {}
        for e in ENGS:
            c = 0
            for op in self.ops[e]:
                if op.need_inc:
                    c += 1
                op.incval = c
            self.ninc[e] = c

    def emit(self, nc, stack):
        self.resolve()
        esems = {}
        for e in ENGS:
            n = self.ninc[e] // SEM_LIMIT + 1
            esems[e] = [stack.enter_context(nc.semaphore("s_%s_%d" % (e, j))) for j in range(n)]
        for s in self.dsems:
            s.handle = stack.enter_context(nc.semaphore("s_" + s.id))
        prog = self

        def emit_wait(eng, t):
            if t[0] == "c":
                v = prog.ops[t[1]][t[2]].incval
                j = (v - 1) // SEM_LIMIT
                eng.wait_ge(esems[t[1]][j], v - j * SEM_LIMIT)
            else:
                eng.wait_ge(t[1].handle, t[2])

        def run(ename):
            def body(eng):
                for op in prog.ops[ename]:
                    for t in op.waits:
                        emit_wait(eng, t)
                    if op.fn is None:
                        if op.need_inc:
                            j = (op.incval - 1) // SEM_LIMIT
                            eng.nop().then_inc(esems[ename][j], 1)
                        continue
                    if op.dma is not None:
                        for f in op.fn:
                            f(eng).then_inc(op.dma[0].handle, 16)
                    else:
                        ins = op.fn(eng)
                        if op.need_inc:
                            j = (op.incval - 1) // SEM_LIMIT
                            ins.then_inc(esems[ename][j], 1)
            return body

        with nc.Block() as block:
            block.tensor(run("pe"))
            block.scalar(run("act"))
            block.vector(run("dve"))
            block.gpsimd(run("pool"))
            block.sync(run("sp"))


D = 1024
KC = 8
DFF = 2816
FC = 22
DEPTH = 2
T_S = 4096
T_P = 256
NTOK = T_S + 2 * T_P
PAST = 512
INC = 3104
ALPHA = (2 * DEPTH) ** 0.25
LN_EPS = 1e-5
NCORES = 8

BIGW = 53000


def _prod(s):
    n = 1
    for v in s:
        n *= v
    return n


def _view(v, shape):
    if len(shape) == 1:
        return v
    if len(shape) == 2:
        return v.rearrange("p (a b) -> p a b", b=shape[1])
    if len(shape) == 3:
        return v.rearrange("p (a b c) -> p a b c", b=shape[1], c=shape[2])
    if len(shape) == 4:
        return v.rearrange("p (a b c d) -> p a b c d", b=shape[1], c=shape[2], d=shape[3])
    raise ValueError


class Arena:
    def __init__(self, big, lo, hi):
        self.big = big
        self.lo = lo
        self.hi = hi
        self.off = lo

    def reset(self):
        self.off = self.lo

    def f32(self, *shape):
        n = _prod(shape)
        a = self.off
        self.off += n
        assert self.off <= self.hi, "SBUF arena overflow %d > %d" % (self.off, self.hi)
        return _view(self.big[:, a:a + n], shape)

    def bf16(self, *shape):
        n = _prod(shape)
        nw = (n + 1) // 2
        a = self.off
        self.off += nw
        assert self.off <= self.hi, "SBUF arena overflow %d > %d" % (self.off, self.hi)
        v = self.big[:, a:a + nw].bitcast(BF16)
        if 2 * nw != n:
            v = v[:, 0:n]
        return _view(v, shape)


class K:
    pass


def build_nc(stop=None):
    nc = bass.Bass("TRN2", target_bir_lowering=False)
    k = K()
    k.nc = nc
    k.stop = stop

    def din(name, shape, dt=F32):
        return nc.dram_tensor(name, list(shape), dt, kind="ExternalInput").ap()

    def dout(name, shape, dt=F32):
        return nc.dram_tensor(name, list(shape), dt, kind="ExternalOutput").ap()

    def dscr(name, shape, dt=F32):
        kind = "ExternalOutput" if (stop is not None and stop.startswith("mix") and name != "XS") else "Internal"
        return nc.dram_tensor(name, list(shape), dt, kind=kind).ap()

    I = {}
    I["x_in"] = din("x_in", [NTOK, D])
    I["cvec"] = din("cvec", [2, D])
    I["ck"] = din("ck", [DEPTH, 4, PAST, 128])
    I["cv"] = din("cv", [DEPTH, 4, PAST, 128])
    I["sg"] = din("sg", [DEPTH, 2, 4, 64, 64])
    I["w_ada"] = din("w_ada", [DEPTH, D, 9 * D])
    I["b_ada"] = din("b_ada", [DEPTH, 9 * D])
    I["w_ffn1_in"] = din("w_ffn1_in", [DEPTH, D, 2 * DFF])
    I["w_ffn1_out"] = din("w_ffn1_out", [DEPTH, DFF, D])
    I["w_ffn2_in"] = din("w_ffn2_in", [DEPTH, D, 2 * DFF])
    I["w_ffn2_out"] = din("w_ffn2_out", [DEPTH, DFF, D])
    I["w_in"] = din("w_in", [DEPTH, D, INC])
    I["conv_w"] = din("conv_w", [DEPTH, 31, 256])
    I["conv_b"] = din("conv_b", [DEPTH, 256])
    I["conv_ln_g"] = din("conv_ln_g", [DEPTH, 256])
    I["conv_ln_b"] = din("conv_ln_b", [DEPTH, 256])
    I["gla_w_a2"] = din("gla_w_a2", [DEPTH, 2, 16, 256])
    I["gla_b_a"] = din("gla_b_a", [DEPTH, 2, 256])
    I["gla_norm_g"] = din("gla_norm_g", [DEPTH, 64])
    I["diff_lam"] = din("diff_lam", [DEPTH, 4, 64])
    I["diff_norm_g"] = din("diff_norm_g", [DEPTH, 128])
    I["w_out"] = din("w_out", [DEPTH, D, D])
    I["ln_g"] = din("ln_g", [DEPTH, 3, D])
    I["ln_b"] = din("ln_b", [DEPTH, 3, D])
    I["cmat"] = din("cmat", [7, 128, 128])
    I["ident"] = I["cmat"][0]
    I["rope"] = din("rope", [2, 128, T_S])
    k.I = I
    O = {}
    O["y"] = dout("y", [NTOK, D])
    O["nk"] = dout("nk", [2, DEPTH, 4, T_P, 128])
    O["nv"] = dout("nv", [2, DEPTH, 4, T_P, 128])
    O["ng"] = dout("ng", [2, DEPTH, 2, 4, 64, 64])
    k.O = O
    k.XS = dscr("XS", [128, KC, NTOK])
    S = {}
    S["YC"] = dscr("YC", [128, 2, NTOK])
    S["SG"] = dscr("SG", [128, 2, NTOK])
    for d_ in range(2):
        S["QT%d" % d_] = dscr("QT%d" % d_, [128, 2, NTOK], BF16)
        S["KT%d" % d_] = dscr("KT%d" % d_, [128, 2, NTOK], BF16)
        S["KH%d" % d_] = dscr("KH%d" % d_, [NTOK, 256], BF16)
        S["DEC%d" % d_] = dscr("DEC%d" % d_, [128, 2, NTOK // 64])
    S["VG"] = dscr("VG", [NTOK, 256], BF16)
    S["QD"] = dscr("QD", [128, 4, NTOK], BF16)
    S["KD"] = dscr("KD", [128, 4, NTOK], BF16)
    S["VD"] = dscr("VD", [NTOK, 512], BF16)
    S["YM"] = dscr("YM", [128, 8, NTOK], BF16)
    k.S = S
    k.Sb = {n: Buf("S_" + n) for n in S}
    if stop is not None:
        O["dbg"] = dout("dbg", [128, KC, NTOK])

    P = Prog()
    k.P = P
    with ExitStack() as st:
        big = st.enter_context(nc.sbuf_tensor("big", [128, BIGW], F32))
        ps = st.enter_context(nc.psum_tensor("ps", [128, 8, 512], F32))
        k.ps = ps
        k.Bps = [Buf("ps%d" % i, excl=True) for i in range(8)]
        k.cons = Arena(big, 0, 4200)
        k.ar = Arena(big, 4200, BIGW)
        k.outbufs = []
        k.XSb = {}
        _build_all(k)
        P.op("sp", None, reads=k.outbufs)
        P.emit(nc, st)
    return nc


def xs_buf(k, key):
    if key not in k.XSb:
        k.XSb[key] = Buf("XS%s" % (key,))
    return k.XSb[key]


TILES = [(i * 1024, 1024, 0) for i in range(4)] + [(T_S, 512, 1)]


def _build_all(k):
    phase_consts(k)
    if k.stop == "consts":
        return
    for l in range(DEPTH):
        phase_ffn(k, l, 0, first=(l == 0), last=False)
        if k.stop == "ffn1_%d" % l:
            dump_xs(k)
            return
        phase_mix_a(k, l)
        if k.stop == "mixa_%d" % l:
            return
        phase_gla(k, l)
        if k.stop is not None and k.stop.startswith("mixb") and k.stop.endswith("_%d" % l):
            return
        phase_attn(k, l)
        if k.stop == "mixd_%d" % l:
            return
        phase_mix_out(k, l)
        if k.stop == "mix_%d" % l:
            dump_xs(k)
            return
        phase_ffn(k, l, 2, first=False, last=(l == DEPTH - 1))


def dump_xs(k):
    P = k.P
    k.P.barrier()
    k.ar.reset()
    t = k.ar.f32(KC, 512)
    Bt = Buf("dump")
    Bd = Buf("dbgout")
    for i in range(NTOK // 512):
        sl = slice(i * 512, (i + 1) * 512)
        P.dma("sp", lambda e, sl=sl: e.dma_start(out=t, in_=k.XS[:, :, sl]),
              reads=list(k.XSb.values()), writes=[Bt], sb=Bt)
        P.dma("sp", lambda e, sl=sl: e.dma_start(out=k.O["dbg"][:, :, sl], in_=t), reads=[Bt], writes=[Bd], sb=Bt)
    k.outbufs.append(Bd)


def load_T(k, src2d, R, dst, tmp, Btmp, Bdst, bank=7):
    P = k.P
    ps = k.ps
    P.dma("sp", lambda e: e.dma_start(out=tmp[0:R, 0:128], in_=src2d), writes=[Btmp], sb=Btmp)
    P.op("pe", lambda e: e.transpose(out=ps[:, bank, 0:R], in_=tmp[0:R, 0:128], identity=k.ident[0:R, 0:R]),
         reads=[Btmp, k.Bcons], writes=[k.Bps[bank]])
    P.op("dve", lambda e: e.tensor_copy(out=dst, in_=ps[:, bank, 0:R]), reads=[k.Bps[bank]], writes=[Bdst])


def phase_consts(k):
    P, nc, I, ps = k.P, k.nc, k.I, k.ps
    c = k.cons
    k.Bcons = Buf("cons")
    Bc = k.Bcons
    k.ident = c.f32(128)
    k.ident_bf = c.bf16(128)
    k.ones_bf = c.bf16(128)
    k.modT = c.f32(DEPTH, 72, 2)
    k.s1p = c.f32(DEPTH, 2, 3, 8)
    k.gt = c.f32(DEPTH, 2, 3, 8)
    k.lng = c.f32(DEPTH * 3 * 8)
    k.lnb = c.f32(DEPTH * 3 * 8)
    k.cm = c.f32(7, 128)
    k.cmb = c.bf16(7, 128)
    P.dma("sp", lambda e: e.dma_start(out=k.ident, in_=I["ident"]), writes=[Bc], sb=Bc)
    P.op("dve", lambda e: e.tensor_copy(out=k.ident_bf, in_=k.ident), reads=[Bc], writes=[Bc])
    P.op("dve", lambda e: e.memset(k.ones_bf, 1.0), writes=[Bc])
    P.dma("sp", lambda e: e.dma_start(out=k.cm, in_=I["cmat"].rearrange("m p c -> p m c")), writes=[Bc], sb=Bc)
    P.op("dve", lambda e: e.tensor_copy(out=k.cmb, in_=k.cm), reads=[Bc], writes=[Bc])

    ar = k.ar
    ar.reset()
    tmp = ar.f32(128)
    Btmp = Buf("tmp")
    load_T(k, I["ln_g"].rearrange("l i (c p) -> (l i c) p", p=128), 48, k.lng, tmp, Btmp, Bc)
    load_T(k, I["ln_b"].rearrange("l i (c p) -> (l i c) p", p=128), 48, k.lnb, tmp, Btmp, Bc)
    cs = ar.f32(D)
    cvT = ar.f32(KC, 2)
    Bcs, BcvT = Buf("cs"), Buf("cvT")
    P.dma("sp", lambda e: e.dma_start(out=cs[0:2, :], in_=I["cvec"]), writes=[Bcs], sb=Bcs)
    P.op("act", lambda e: e.activation(out=cs[0:2, :], in_=cs[0:2, :], func=AF.Silu), reads=[Bcs], writes=[Bcs])
    for kc in range(KC):
        P.op("pe", lambda e, kc=kc: e.transpose(out=ps[:, 6, 2 * kc:2 * kc + 2], in_=cs[0:2, kc * 128:(kc + 1) * 128],
                                                 identity=k.ident[0:2, 0:2]), reads=[Bcs, Bc], writes=[k.Bps[6]])
    P.op("dve", lambda e: e.tensor_copy(out=cvT, in_=ps[:, 6, 0:16].rearrange("p (a b) -> p a b", b=2)),
         reads=[k.Bps[6]], writes=[BcvT])
    wst = [ar.f32(KC, 512) for _ in range(2)]
    Bwst = [Buf("wst0"), Buf("wst1")]
    wb = [ar.bf16(KC, 512) for _ in range(2)]
    Bwb = [Buf("wada0"), Buf("wada1")]
    cvb = ar.bf16(KC, 2)
    P.op("dve", lambda e: e.tensor_copy(out=cvb, in_=cvT), reads=[BcvT], writes=[BcvT])
    bT = ar.f32(72)
    BbT = Buf("bT")
    u = 0
    for l in range(DEPTH):
        load_T(k, I["b_ada"][l].rearrange("(r p) -> r p", p=128), 72, bT, tmp, Btmp, BbT)
        for cg in range(18):
            w = wb[u % 2]
            Bw = Bwb[u % 2]
            ws = wst[u % 2]
            Bws = Bwst[u % 2]
            src = I["w_ada"][l][:, cg * 512:(cg + 1) * 512].rearrange("(kc p) c -> p kc c", p=128)
            P.dma("sp", lambda e, ws=ws, src=src: e.dma_start(out=ws, in_=src), writes=[Bws], sb=Bws)
            if u % 2:
                P.op("act", lambda e, w=w, ws=ws: e.activation(out=w, in_=ws, func=AF.Copy), reads=[Bws], writes=[Bw])
            else:
                P.op("dve", lambda e, w=w, ws=ws: e.tensor_copy(out=w, in_=ws), reads=[Bws], writes=[Bw])
            u += 1
            for cc in range(4):
                cb = cg * 4 + cc
                for kc in range(KC):
                    P.op("pe", lambda e, w=w, cc=cc, kc=kc, cb=cb: e.matmul(
                        ps[:, 5, 2 * cb:2 * cb + 2], w[:, kc, cc * 128:(cc + 1) * 128], cvb[:, kc, :],
                        start=(kc == 0), stop=(kc == KC - 1)), reads=[Bw, BcvT], writes=[k.Bps[5]])
        P.op("dve", lambda e, l=l: e.tensor_tensor(
            out=k.modT[:, l], in0=ps[:, 5, 0:144].rearrange("p (a b) -> p a b", b=2),
            in1=bT.unsqueeze(2).broadcast_to([128, 72, 2]), op=ALU.add), reads=[k.Bps[5], BbT], writes=[Bc])
    k.convw = c.f32(DEPTH, 2, 31)
    k.convp = c.f32(3, DEPTH * 2)
    k.gnorm = c.f32(DEPTH)
    k.dnorm = c.f32(DEPTH)
    k.nlam = c.f32(DEPTH)
    k.wa2 = c.f32(DEPTH, 2, 256)
    k.lamt = c.f32(DEPTH, 4, 64)
    k.lamp = c.f32(DEPTH, 2, 64)
    k.lams = c.f32(DEPTH, 2)
    for l in range(DEPTH):
        for cc in range(2):
            load_T(k, I["conv_w"][l][:, cc * 128:(cc + 1) * 128], 31, k.convw[:, l, cc, :], tmp, Btmp, Bc)
    for wi, nm in enumerate(["conv_b", "conv_ln_g", "conv_ln_b"]):
        load_T(k, I[nm].rearrange("l (c p) -> (l c) p", p=128), DEPTH * 2, k.convp[:, wi, :], tmp, Btmp, Bc)
    P.dma("sp", [lambda e: e.dma_start(out=tmp[0:DEPTH, 0:64], in_=I["gla_norm_g"]),
                 lambda e: e.dma_start(out=tmp[0:DEPTH, 64:128], in_=I["gla_norm_g"])], writes=[Btmp], sb=Btmp)
    P.op("pe", lambda e: e.transpose(out=ps[:, 7, 0:DEPTH], in_=tmp[0:DEPTH, 0:128], identity=k.ident[0:DEPTH, 0:DEPTH]),
         reads=[Btmp, Bc], writes=[k.Bps[7]])
    P.op("dve", lambda e: e.tensor_copy(out=k.gnorm, in_=ps[:, 7, 0:DEPTH]), reads=[k.Bps[7]], writes=[Bc])
    load_T(k, I["diff_norm_g"], DEPTH, k.dnorm, tmp, Btmp, Bc)
    for l in range(DEPTH):
        lam_init = 0.8 - 0.6 * math.exp(-0.3 * l)
        P.op("dve", lambda e, l=l, li=lam_init: e.tensor_scalar(out=k.dnorm[:, l:l + 1], in0=k.dnorm[:, l:l + 1],
                                                               scalar1=1.0 - li, scalar2=None, op0=ALU.mult),
             reads=[Bc], writes=[Bc])
    P.dma("sp", lambda e: e.dma_start(out=k.lamt, in_=I["diff_lam"].partition_broadcast(128)), writes=[Bc], sb=Bc)
    for l in range(DEPTH):
        lam_init = 0.8 - 0.6 * math.exp(-0.3 * l)
        P.op("dve", lambda e, l=l: e.tensor_tensor(out=k.lamp[:, l, 0, :], in0=k.lamt[:, l, 0, :], in1=k.lamt[:, l, 1, :],
                                                   op=ALU.mult), reads=[Bc], writes=[Bc])
        P.op("dve", lambda e, l=l: e.tensor_tensor(out=k.lamp[:, l, 1, :], in0=k.lamt[:, l, 2, :], in1=k.lamt[:, l, 3, :],
                                                   op=ALU.mult), reads=[Bc], writes=[Bc])
        P.op("dve", lambda e, l=l: e.reduce_sum(out=k.lams[:, l, :], in_=k.lamp[:, l, :, :], axis=AX.X),
             reads=[Bc], writes=[Bc])
        P.op("act", lambda e, l=l: e.activation(out=k.lams[:, l, :], in_=k.lams[:, l, :], func=AF.Exp),
             reads=[Bc], writes=[Bc])
        P.op("dve", lambda e, l=l, li=lam_init: e.scalar_tensor_tensor(
            out=k.nlam[:, l:l + 1], in0=k.lams[:, l, 1:2], scalar=-li, in1=k.lams[:, l, 0:1],
            op0=ALU.add, op1=ALU.subtract), reads=[Bc], writes=[Bc])
    for l in range(DEPTH):
        for d_ in range(2):
            P.dma("sp", [lambda e, l=l, d_=d_: e.dma_start(out=k.wa2[0:16, l, d_, :], in_=I["gla_w_a2"][l, d_]),
                         lambda e, l=l, d_=d_: e.dma_start(out=k.wa2[16:17, l, d_, :], in_=I["gla_b_a"][l, d_:d_ + 1, :])],
                  writes=[Bc], sb=Bc)
    coef = [0.5 / ALPHA, 1.0 / ALPHA, 0.5 / ALPHA]
    for l in range(DEPTH):
        for g in range(2):
            for i in range(3):
                P.op("dve", lambda e, l=l, g=g, i=i: e.tensor_scalar(
                    out=k.s1p[:, l, g, i, :], in0=k.modT[:, l, (3 * i + 1) * 8:(3 * i + 2) * 8, g],
                    scalar1=1.0, scalar2=None, op0=ALU.add), reads=[Bc], writes=[Bc])
                P.op("dve", lambda e, l=l, g=g, i=i: e.tensor_scalar(
                    out=k.gt[:, l, g, i, :], in0=k.modT[:, l, (3 * i + 2) * 8:(3 * i + 3) * 8, g],
                    scalar1=coef[i], scalar2=None, op0=ALU.mult), reads=[Bc], writes=[Bc])
    P.barrier()


def ln_half(k, l, i, x, Bx, hs, tmps):
    P, ps = k.P, k.ps
    rbf, rsq, mean, var, Bt, Bs = tmps
    for m in range(KC):
        P.op("act", lambda e, m=m: e.activation(out=rbf[:, m, :], in_=x[:, m, hs], func=AF.Copy), reads=[Bx], writes=[Bt])
        P.op("act", lambda e, m=m: e.activation(out=rsq[:, m, :], in_=x[:, m, hs], func=AF.Square), reads=[Bx], writes=[Bt])
    for m in range(KC):
        P.op("pe", lambda e, m=m: e.matmul(ps[:, 0, :], k.ones_bf, rbf[:, m, :], start=(m == 0), stop=(m == KC - 1)),
             reads=[Bt, k.Bcons], writes=[k.Bps[0]])
    for m in range(KC):
        P.op("pe", lambda e, m=m: e.matmul(ps[:, 1, :], k.ones_bf, rsq[:, m, :], start=(m == 0), stop=(m == KC - 1)),
             reads=[Bt, k.Bcons], writes=[k.Bps[1]])
    P.op("dve", lambda e: e.tensor_scalar(out=mean, in0=ps[:, 0, :], scalar1=1.0 / D, scalar2=None, op0=ALU.mult),
         reads=[k.Bps[0]], writes=[Bs])
    P.op("dve", lambda e: e.tensor_tensor(out=var, in0=mean, in1=mean, op=ALU.mult), reads=[Bs], writes=[Bs])
    P.op("dve", lambda e: e.scalar_tensor_tensor(out=var, in0=ps[:, 1, :], scalar=1.0 / D, in1=var,
                                                 op0=ALU.mult, op1=ALU.subtract), reads=[k.Bps[1], Bs], writes=[Bs])
    P.op("dve", lambda e: e.tensor_scalar(out=var, in0=var, scalar1=LN_EPS / (ALPHA * ALPHA), scalar2=None, op0=ALU.add),
         reads=[Bs], writes=[Bs])
    P.op("act", lambda e: e.activation(out=var, in_=var, func=AF.Sqrt), reads=[Bs], writes=[Bs])
    P.op("dve", lambda e: e.reciprocal(out=var, in_=var), reads=[Bs], writes=[Bs])
    P.op("dve", lambda e: e.scalar_tensor_tensor(out=mean, in0=mean, scalar=-1.0, in1=var, op0=ALU.mult, op1=ALU.mult),
         reads=[Bs], writes=[Bs])
    for m in range(KC):
        P.op("dve", lambda e, m=m: e.tensor_tensor(out=x[:, m, hs], in0=x[:, m, hs], in1=var, op=ALU.mult),
             reads=[Bx, Bs], writes=[Bx])
        P.op("dve", lambda e, m=m: e.tensor_tensor(out=x[:, m, hs], in0=x[:, m, hs], in1=mean, op=ALU.add),
             reads=[Bx, Bs], writes=[Bx])
        col = (l * 3 + i) * 8 + m
        P.op("act", lambda e, m=m, col=col: e.activation(out=x[:, m, hs], in_=x[:, m, hs], func=AF.Identity,
                                                        scale=k.lng[:, col:col + 1], bias=k.lnb[:, col:col + 1]),
             reads=[Bx, k.Bcons], writes=[Bx])


def load_x_tile(k, x, Bx, tok0, nt, first, stage, Bstage):
    P, ps = k.P, k.ps
    if not first:
        P.dma("sp", lambda e: e.dma_start(out=x[:, :, 0:nt], in_=k.XS[:, :, tok0:tok0 + nt]),
              reads=[xs_buf(k, tok0)], writes=[Bx], sb=Bx)
        return
    for q in range(nt // 256):
        src = k.I["x_in"][tok0 + q * 256: tok0 + (q + 1) * 256, :].rearrange("(b p) f -> p b f", p=128)
        P.dma("sp", lambda e, src=src: e.dma_start(out=stage, in_=src), writes=[Bstage], sb=Bstage)
        for ch in range(KC):
            bank = 4 + ch % 4
            for b in range(2):
                P.op("pe", lambda e, ch=ch, b=b, bank=bank: e.transpose(
                    out=ps[:, bank, b * 128:(b + 1) * 128], in_=stage[:, b, ch * 128:(ch + 1) * 128], identity=k.ident),
                    reads=[Bstage, k.Bcons], writes=[k.Bps[bank]])
            if ch % 2:
                P.op("act", lambda e, ch=ch, bank=bank, q=q: e.activation(
                    out=x[:, ch, q * 256:(q + 1) * 256], in_=ps[:, bank, 0:256], func=AF.Copy),
                    reads=[k.Bps[bank]], writes=[Bx])
            else:
                P.op("dve", lambda e, ch=ch, bank=bank, q=q: e.tensor_copy(
                    out=x[:, ch, q * 256:(q + 1) * 256], in_=ps[:, bank, 0:256]),
                    reads=[k.Bps[bank]], writes=[Bx])


def store_x_tile(k, x, Bx, tok0, nt, last, stage, Bstage):
    P, ps = k.P, k.ps
    if not last:
        P.dma("sp", lambda e: e.dma_start(out=k.XS[:, :, tok0:tok0 + nt], in_=x[:, :, 0:nt]),
              reads=[Bx], writes=[xs_buf(k, tok0)], sb=Bx)
        return
    By = Buf("y")
    k.outbufs.append(By)
    for q in range(nt // 256):
        for b in range(2):
            for hf in range(2):
                bank = 4 + (b * 2 + hf) % 4
                for c4 in range(4):
                    ch = hf * 4 + c4
                    P.op("pe", lambda e, ch=ch, b=b, bank=bank, c4=c4, q=q: e.transpose(
                        out=ps[:, bank, c4 * 128:(c4 + 1) * 128], in_=x[:, ch, q * 256 + b * 128: q * 256 + (b + 1) * 128],
                        identity=k.ident), reads=[Bx, k.Bcons], writes=[k.Bps[bank]])
                if hf:
                    P.op("act", lambda e, b=b, bank=bank, hf=hf: e.activation(
                        out=stage[:, b, hf * 512:(hf + 1) * 512], in_=ps[:, bank, :], func=AF.Copy),
                        reads=[k.Bps[bank]], writes=[Bstage])
                else:
                    P.op("dve", lambda e, b=b, bank=bank, hf=hf: e.tensor_copy(
                        out=stage[:, b, hf * 512:(hf + 1) * 512], in_=ps[:, bank, :]),
                        reads=[k.Bps[bank]], writes=[Bstage])
        dst = k.O["y"][tok0 + q * 256: tok0 + (q + 1) * 256, :].rearrange("(b p) f -> p b f", p=128)
        P.dma("sp", lambda e, dst=dst: e.dma_start(out=dst, in_=stage), reads=[Bstage], writes=[By], sb=Bstage)


def ln_half_gen(k, l, i, x, Bx, hs, tmps, banks):
    P, ps = k.P, k.ps
    rbf, rsq, mean, var, Bt, Bs = tmps
    b0, b1 = banks
    for m in range(KC):
        P.op("act", lambda e, m=m: e.activation(out=rbf[:, m, :], in_=x[:, m, hs], func=AF.Copy), reads=[Bx], writes=[Bt])
        P.op("act", lambda e, m=m: e.activation(out=rsq[:, m, :], in_=x[:, m, hs], func=AF.Square), reads=[Bx], writes=[Bt])
        if m % 2:
            yield
    for m in range(KC):
        P.op("pe", lambda e, m=m: e.matmul(ps[:, b0, :], k.ones_bf, rbf[:, m, :], start=(m == 0), stop=(m == KC - 1)),
             reads=[Bt, k.Bcons], writes=[k.Bps[b0]])
    for m in range(KC):
        P.op("pe", lambda e, m=m: e.matmul(ps[:, b1, :], k.ones_bf, rsq[:, m, :], start=(m == 0), stop=(m == KC - 1)),
             reads=[Bt, k.Bcons], writes=[k.Bps[b1]])
    yield
    P.op("dve", lambda e: e.tensor_scalar(out=mean, in0=ps[:, b0, :], scalar1=1.0 / D, scalar2=None, op0=ALU.mult),
         reads=[k.Bps[b0]], writes=[Bs])
    P.op("dve", lambda e: e.tensor_tensor(out=var, in0=mean, in1=mean, op=ALU.mult), reads=[Bs], writes=[Bs])
    P.op("dve", lambda e: e.scalar_tensor_tensor(out=var, in0=ps[:, b1, :], scalar=1.0 / D, in1=var,
                                                 op0=ALU.mult, op1=ALU.subtract), reads=[k.Bps[b1], Bs], writes=[Bs])
    P.op("dve", lambda e: e.tensor_scalar(out=var, in0=var, scalar1=LN_EPS / (ALPHA * ALPHA), scalar2=None, op0=ALU.add),
         reads=[Bs], writes=[Bs])
    P.op("act", lambda e: e.activation(out=var, in_=var, func=AF.Sqrt), reads=[Bs], writes=[Bs])
    P.op("dve", lambda e: e.reciprocal(out=var, in_=var), reads=[Bs], writes=[Bs])
    P.op("dve", lambda e: e.scalar_tensor_tensor(out=mean, in0=mean, scalar=-1.0, in1=var, op0=ALU.mult, op1=ALU.mult),
         reads=[Bs], writes=[Bs])
    yield
    for m in range(KC):
        P.op("dve", lambda e, m=m: e.tensor_tensor(out=x[:, m, hs], in0=x[:, m, hs], in1=var, op=ALU.mult),
             reads=[Bx, Bs], writes=[Bx])
        P.op("dve", lambda e, m=m: e.tensor_tensor(out=x[:, m, hs], in0=x[:, m, hs], in1=mean, op=ALU.add),
             reads=[Bx, Bs], writes=[Bx])
        col = (l * 3 + i) * 8 + m
        P.op("act", lambda e, m=m, col=col: e.activation(out=x[:, m, hs], in_=x[:, m, hs], func=AF.Identity,
                                                        scale=k.lng[:, col:col + 1], bias=k.lnb[:, col:col + 1]),
             reads=[Bx, k.Bcons], writes=[Bx])
        yield


def ln_gen(k, l, i, x, Bx, nh, tmps, after=None, tmps2=None, Bx2=None):
    if tmps2 is None or nh == 1:
        for hf in range(nh):
            yield from ln_half_gen(k, l, i, x, Bx if Bx2 is None or hf == 0 else Bx2, slice(hf * 512, (hf + 1) * 512), tmps, (6, 7))
    else:
        g0 = ln_half_gen(k, l, i, x, Bx, slice(0, 512), tmps, (6, 7))
        g1 = ln_half_gen(k, l, i, x, Bx2 if Bx2 is not None else Bx, slice(512, 1024), tmps2, (4, 5))
        live = [g0, g1]
        while live:
            for g_ in list(live):
                if next(g_, "done") == "done":
                    live.remove(g_)
            yield
    if after is not None:
        after()
    yield


def _drain(gen):
    if gen is not None:
        for _ in gen:
            pass


def phase_ffn(k, l, i, first, last):
    P, ps, I = k.P, k.ps, k.I
    ar = k.ar
    ar.reset()
    w_in = I["w_ffn1_in" if i == 0 else "w_ffn2_in"][l]
    w_out = I["w_ffn1_out" if i == 0 else "w_ffn2_out"][l]
    xb = [ar.f32(KC, 1024) for _ in range(2)]
    Bx = [Buf("x0"), Buf("x1")]
    h = ar.bf16(KC, 1024)
    Bh = Buf("h")
    act = ar.bf16(FC, 1024)
    Bact = [Buf("act0"), Buf("act1")]
    NWB = 3
    wib = [ar.bf16(2, KC, 256) for _ in range(NWB)]
    Bwi = [Buf("wi%d" % j) for j in range(NWB)]
    wob = [ar.bf16(FC, 128) for _ in range(2)]
    Bwo = [Buf("wo0"), Buf("wo1")]
    sg = [ar.bf16(512) for _ in range(2)]
    Bsg = [Buf("sg0"), Buf("sg1")]
    rbf = ar.bf16(KC, 512)
    rsq = ar.bf16(KC, 512)
    mean = ar.f32(512)
    var = ar.f32(512)
    Bt = Buf("lntmp")
    Bstat = Buf("lnstat")
    stage = ar.f32(2, 1024) if (first or last) else None
    Bstage = Buf("stage")
    cnt = {"uw": 0, "uo": 0, "usg": 0, "grp": 0, "cacc": 0}

    def stage_a(ti):
        tok0, nt, g = TILES[ti]
        x, bx = xb[ti % 2], Bx[ti % 2]
        load_x_tile(k, x, bx, tok0, nt, first, stage, Bstage)
        for kc in range(KC):
            P.op("dve", lambda e, kc=kc, x=x, nt=nt, g=g: e.tensor_scalar(
                out=h[:, kc, 0:nt], in0=x[:, kc, 0:nt], scalar1=k.s1p[:, l, g, i, kc:kc + 1],
                scalar2=k.modT[:, l, (3 * i) * 8 + kc, g:g + 1], op0=ALU.mult, op1=ALU.add),
                reads=[bx, k.Bcons], writes=[Bh])

    def stage_b(ti, hook):
        tok0, nt, g = TILES[ti]
        nh = nt // 512
        for u in range(FC // 2):
            wb = wib[cnt["uw"] % NWB]
            Bw = Bwi[cnt["uw"] % NWB]
            cnt["uw"] += 1
            sa = w_in[:, u * 256:(u + 1) * 256].rearrange("(kc p) c -> p kc c", p=128)
            sb_ = w_in[:, DFF + u * 256: DFF + (u + 1) * 256].rearrange("(kc p) c -> p kc c", p=128)
            P.dma("pool", [lambda e, wb=wb, sa=sa: e.dma_start(out=wb[:, 0], in_=sa),
                           lambda e, wb=wb, sb_=sb_: e.dma_start(out=wb[:, 1], in_=sb_)], writes=[Bw], sb=Bw)
            for jj in range(2):
                j = 2 * u + jj
                for hf in range(nh):
                    hs = slice(hf * 512, (hf + 1) * 512)
                    slot = cnt["grp"] % 3
                    cnt["grp"] += 1
                    ba, bb = 2 * slot, 2 * slot + 1
                    for ab, bank in ((0, ba), (1, bb)):
                        for kc in range(KC):
                            P.op("pe", lambda e, wb=wb, ab=ab, kc=kc, jj=jj, hs=hs, bank=bank: e.matmul(
                                ps[:, bank, :], wb[:, ab, kc, jj * 128:(jj + 1) * 128], h[:, kc, hs],
                                start=(kc == 0), stop=(kc == KC - 1)), reads=[Bw, Bh], writes=[k.Bps[bank]])
                    s_ = sg[cnt["usg"] % 2]
                    Bs_ = Bsg[cnt["usg"] % 2]
                    cnt["usg"] += 1
                    P.op("act", lambda e, s_=s_, ba=ba: e.activation(out=s_, in_=ps[:, ba, :], func=AF.Silu),
                         reads=[k.Bps[ba]], writes=[Bs_])
                    P.op("dve", lambda e, s_=s_, bb=bb, j=j, hs=hs: e.tensor_tensor(
                        out=act[:, j, hs], in0=ps[:, bb, :], in1=s_, op=ALU.mult),
                        reads=[k.Bps[bb], Bs_], writes=[Bact[hf]])
                    hook()

    def stage_c(ti):
        tok0, nt, g = TILES[ti]
        nh = nt // 512
        x, bx = xb[ti % 2], Bx[ti % 2]
        for m in range(KC):
            wo = wob[cnt["uo"] % 2]
            Bw = Bwo[cnt["uo"] % 2]
            cnt["uo"] += 1
            so = w_out[:, m * 128:(m + 1) * 128].rearrange("(j p) c -> p j c", p=128)
            P.dma("pool", lambda e, wo=wo, so=so: e.dma_start(out=wo, in_=so), writes=[Bw], sb=Bw)
            for hf in range(nh):
                hs = slice(hf * 512, (hf + 1) * 512)
                bank = cnt["cacc"] % 6
                cnt["cacc"] += 1
                for j in range(FC):
                    P.op("pe", lambda e, wo=wo, j=j, hs=hs, bank=bank: e.matmul(
                        ps[:, bank, :], wo[:, j, :], act[:, j, hs], start=(j == 0), stop=(j == FC - 1)),
                        reads=[Bw, Bact[hf]], writes=[k.Bps[bank]])
                P.op("dve", lambda e, m=m, hs=hs, bank=bank, x=x, g=g: e.scalar_tensor_tensor(
                    out=x[:, m, hs], in0=ps[:, bank, :], scalar=k.gt[:, l, g, i, m:m + 1], in1=x[:, m, hs],
                    op0=ALU.mult, op1=ALU.add), reads=[k.Bps[bank], bx, k.Bcons], writes=[bx])

    pend = [None]

    def hook():
        if pend[0] is not None:
            if next(pend[0], "done") == "done":
                pend[0] = None

    stage_a(0)
    for ti in range(len(TILES)):
        tok0, nt, g = TILES[ti]
        stage_b(ti, hook)
        _drain(pend[0])
        pend[0] = None
        if ti + 1 < len(TILES):
            stage_a(ti + 1)
        stage_c(ti)
        x, bx = xb[ti % 2], Bx[ti % 2]
        pend[0] = ln_gen(k, l, i, x, bx, nt // 512, (rbf, rsq, mean, var, Bt, Bstat),
                         after=(lambda x=x, bx=bx, tok0=tok0, nt=nt: store_x_tile(k, x, bx, tok0, nt, last, stage, Bstage)))
    _drain(pend[0])
    P.barrier()


_W_NAMES = ["w_ada", "b_ada", "w_ffn1_in", "w_ffn1_out", "w_ffn2_in", "w_ffn2_out", "w_in", "conv_w", "conv_b",
            "conv_ln_g", "conv_ln_b", "gla_w_a2", "gla_b_a", "gla_norm_g", "diff_lam", "diff_norm_g", "w_out",
            "ln_g", "ln_b"]


def _const_mats():
    i = np.arange(128)
    same = (i[:, None] // 64) == (i[None, :] // 64)
    ident = np.eye(128, dtype=np.float32)
    tri_f = ((i[:, None] <= i[None, :]) & same).astype(np.float32)
    tri_b = ((i[:, None] >= i[None, :]) & same).astype(np.float32)
    stri_f = ((i[:, None] > i[None, :]) & same).astype(np.float32)
    stri_b = ((i[:, None] < i[None, :]) & same).astype(np.float32)
    rm = np.zeros((128, 128), np.float32)
    for m in range(128):
        if (m % 32) < 16:
            rm[m + 16, m] = -1.0
        else:
            rm[m - 16, m] = 1.0
    blk64 = same.astype(np.float32)
    return np.ascontiguousarray(np.stack([ident, tri_f, tri_b, stri_f, stri_b, rm, blk64], 0))


def _rope_tables():
    rows = T_S // 64
    row = np.repeat(np.arange(rows, dtype=np.float32), 64)
    col = np.tile(np.arange(64, dtype=np.float32), rows)
    seg = 32
    inv = (np.float32(10000.0) ** (-np.arange(0, seg, 2, dtype=np.float32) / np.float32(seg))).astype(np.float32)
    a_r = row[:, None] * inv
    a_c = col[:, None] * inv
    ang = np.concatenate([a_r, a_r, a_c, a_c], axis=-1).astype(np.float32)
    cos = np.cos(ang).astype(np.float32).T
    sin = np.sin(ang).astype(np.float32).T
    cos2 = np.concatenate([cos, cos], 0)
    sin2 = np.concatenate([sin, sin], 0)
    return np.ascontiguousarray(np.stack([cos2, sin2], 0))


def make_in_maps(inp):
    f = lambda a: np.ascontiguousarray(np.asarray(a, dtype=np.float32))
    shared = {n: f(inp[n]) for n in _W_NAMES}
    shared["cmat"] = _const_mats()
    shared["rope"] = _rope_tables()
    maps = []
    for c in range(NCORES):
        m = dict(shared)
        m["x_in"] = np.ascontiguousarray(np.concatenate(
            [f(inp["x_sample"][c]), f(inp["x_prompt"][2 * c]), f(inp["x_prompt"][2 * c + 1])], axis=0))
        m["cvec"] = np.ascontiguousarray(np.stack([f(inp["c"][c]), f(inp["c_ctx"])], axis=0))
        m["ck"] = f(inp["cache_diff_k"][c])
        m["cv"] = f(inp["cache_diff_v"][c])
        m["sg"] = f(inp["state_gla"][c])
        maps.append(m)
    return maps


def kernel(**inp):
    nc = build_nc()
    maps = make_in_maps(inp)
    res = run_bass_kernel_spmd(nc, maps, core_ids=list(range(NCORES)))
    R = res.results
    y_s = np.stack([R[c]["y"][:T_S] for c in range(NCORES)], axis=0)
    y_p = np.concatenate([R[c]["y"][T_S:].reshape(2, T_P, D) for c in range(NCORES)], axis=0)
    nk = np.concatenate([R[c]["nk"] for c in range(NCORES)], axis=0)
    nv = np.concatenate([R[c]["nv"] for c in range(NCORES)], axis=0)
    ng = np.concatenate([R[c]["ng"] for c in range(NCORES)], axis=0)
    return (y_p.astype(np.float32), y_s.astype(np.float32), nk.astype(np.float32), nv.astype(np.float32),
            ng.astype(np.float32))


SEQS = [(0, T_S, True, 0), (T_S, T_P, False, 1), (T_S + T_P, T_P, False, 1)]
MTILES = [(i * 512, 512, 0, True) for i in range(8)] + [(T_S, 512, 1, False)]
C_CONV, C_GQ, C_GK, C_GV, C_GG, C_LR, C_DQ, C_DK, C_DV = 0, 512, 768, 1024, 1280, 1536, 1568, 2080, 2592


def phase_mix_a(k, l):
    P, ps, I, S, Sb = k.P, k.ps, k.I, k.S, k.Sb
    ar = k.ar
    ar.reset()
    i = 1
    W = ar.bf16(KC, INC)
    WP = [0, 512, 1568, 2080, 2592, INC]
    BWp = [Buf("Wmix%d" % j) for j in range(5)]
    for pc in range(5):
        cols = slice(WP[pc], WP[pc + 1])
        P.dma("pool", lambda e, cols=cols: e.dma_start(
            out=W[:, :, cols], in_=I["w_in"][l][:, cols].rearrange("(kc p) c -> p kc c", p=128)), writes=[BWp[pc]], sb=BWp[pc])

    def wbufs(c0, c1):
        return [BWp[j] for j in range(5) if WP[j] < c1 and WP[j + 1] > c0]
    xb = [ar.f32(KC, 512)]
    Bx = [Buf("mx0")]
    hb = [ar.bf16(KC, 512) for _ in range(2)]
    Bhb = [Buf("mh0"), Buf("mh1")]
    hcur = [hb[0], Bhb[0]]
    tf = [ar.f32(512) for _ in range(2)]
    Btf = [Buf("tf0"), Buf("tf1")]
    YCt, SGt = ar.f32(2, 512), ar.f32(2, 512)
    BYC, BSG = Buf("YCt"), Buf("SGt")
    qf, kf = ar.f32(2, 512), ar.f32(2, 512)
    Bqf, Bkf = Buf("qf"), Buf("kf")
    lra = ar.f32(2, 512)
    Blra = Buf("lra")
    QDt, KDt = ar.bf16(4, 512), ar.bf16(4, 512)
    BQD, BKD = Buf("QDt"), Buf("KDt")
    qb = [ar.bf16(512) for _ in range(2)]
    Bqb = [Buf("qb0"), Buf("qb1")]
    rt = [ar.f32(512) for _ in range(2)]
    Brt = [Buf("rt0"), Buf("rt1")]
    cst = ar.f32(2, 512)
    Bcst = Buf("cossin")
    VGt = ar.bf16(4, 256)
    BVG = Buf("VGt")
    VDt = ar.bf16(4, 512)
    BVD = Buf("VDt")
    nkt, nvt = [ar.f32(512) for _ in range(2)], [ar.f32(512) for _ in range(2)]
    Bnk, Bnv = [Buf("nkt0"), Buf("nkt1")], [Buf("nvt0"), Buf("nvt1")]
    Bnko = Buf("nkv_out")
    k.outbufs.append(Bnko)
    ef2 = ar.f32(2, 512)
    Bef = Buf("ef")
    spf4 = ar.f32(2, 4, 256)
    Bsp = Buf("spf")
    ekh2 = ar.f32(4, 256)
    Bekh = Buf("ekh")
    ktm4 = ar.f32(4, 256)
    Bktm = Buf("ktm")
    eG4 = [ar.f32(512) for _ in range(2)]
    emG4 = [ar.f32(512) for _ in range(2)]
    BeG = [Buf("eG0"), Buf("eG1")]
    QTt = [ar.bf16(2, 512) for _ in range(2)]
    KTt = [ar.bf16(2, 512) for _ in range(2)]
    KHt = [ar.bf16(4, 256) for _ in range(2)]
    DECt = [ar.f32(2, 8) for _ in range(2)]
    BQT = [Buf("QTt0"), Buf("QTt1")]
    BKT = [Buf("KTt0"), Buf("KTt1")]
    BKH = [Buf("KHt0"), Buf("KHt1")]
    BDEC = [Buf("DECt0"), Buf("DECt1")]
    P.op("dve", lambda e: e.memset(lra[0:32], 1.0), writes=[Blra])
    bank_ctr = [0]

    def nb():
        b = bank_ctr[0] % 8
        bank_ctr[0] += 1
        return b

    tfc = [0]

    def fm(col0, bank, M=128):
        for kc in range(KC):
            P.op("pe", lambda e, kc=kc, h=hcur[0]: e.matmul(ps[0:M, bank, :], W[:, kc, col0:col0 + M], h[:, kc, :],
                                                               start=(kc == 0), stop=(kc == KC - 1)),
                 reads=wbufs(col0, col0 + M) + [hcur[1]], writes=[k.Bps[bank]])

    def gated(col_a, col_g, dst, Bdst):
        ba = nb()
        fm(col_a, ba)
        if col_g != col_a:
            bg = nb()
            fm(col_g, bg)
        else:
            bg = ba
        t = tf[tfc[0] % 2]
        Bt = Btf[tfc[0] % 2]
        tfc[0] += 1
        P.op("act", lambda e: e.activation(out=t, in_=ps[:, bg, :], func=AF.Exp, scale=-1.0), reads=[k.Bps[bg]], writes=[Bt])
        P.op("dve", lambda e: e.tensor_scalar(out=t, in0=t, scalar1=1.0, scalar2=None, op0=ALU.add), reads=[Bt], writes=[Bt])
        P.op("dve", lambda e: e.reciprocal(out=t, in_=t), reads=[Bt], writes=[Bt])
        P.op("dve", lambda e: e.tensor_tensor(out=dst, in0=ps[:, ba, :], in1=t, op=ALU.mult),
             reads=[k.Bps[ba], Bt], writes=[Bdst])

    def load_x(ti):
        tok0 = MTILES[ti][0]
        tsl = slice(tok0, tok0 + 512)
        P.dma("sp", lambda e, tsl=tsl: e.dma_start(out=xb[0], in_=k.XS[:, :, tsl]),
              reads=[xs_buf(k, (tok0 // 1024) * 1024 if tok0 < T_S else T_S)], writes=[Bx[0]], sb=Bx[0])

    def comp_h(ti):
        g = MTILES[ti][2]
        h_, Bh_ = hb[ti % 2], Bhb[ti % 2]
        for kc in range(KC):
            P.op("dve", lambda e, kc=kc, g=g, h_=h_: e.tensor_scalar(
                out=h_[:, kc, :], in0=xb[0][:, kc, :], scalar1=k.s1p[:, l, g, i, kc:kc + 1],
                scalar2=k.modT[:, l, (3 * i) * 8 + kc, g:g + 1], op0=ALU.mult, op1=ALU.add),
                reads=[Bx[0], k.Bcons], writes=[Bh_])

    load_x(0)
    comp_h(0)
    for ti, (tok0, nt, g, rope) in enumerate(MTILES):
        hcur[0], hcur[1] = hb[ti % 2], Bhb[ti % 2]
        h, Bh = hcur[0], hcur[1]
        tsl = slice(tok0, tok0 + 512)
        if ti + 1 < len(MTILES):
            load_x(ti + 1)
        if rope:
            P.dma("sp", lambda e, tsl=tsl: e.dma_start(out=cst, in_=I["rope"][:, :, tsl].rearrange("a p t -> p a t")),
                  writes=[Bcst], sb=Bcst)
        for cc in range(2):
            gated(C_CONV + cc * 128, C_CONV + 256 + cc * 128, YCt[:, cc, :], BYC)
        P.dma("sp", lambda e, tsl=tsl: e.dma_start(out=S["YC"][:, :, tsl], in_=YCt), reads=[BYC], writes=[Sb["YC"]], sb=BYC)
        for cc in range(2):
            gated(C_GG + cc * 128, C_GG + cc * 128, SGt[:, cc, :], BSG)
        P.dma("sp", lambda e, tsl=tsl: e.dma_start(out=S["SG"][:, :, tsl], in_=SGt), reads=[BSG], writes=[Sb["SG"]], sb=BSG)
        for cc in range(2):
            for col0, dst, Bd in ((C_GQ, qf, Bqf), (C_GK, kf, Bkf)):
                b = nb()
                fm(col0 + cc * 128, b)
                P.op("act", lambda e, b=b, dst=dst, cc=cc: e.activation(out=dst[:, cc, :], in_=ps[:, b, :], func=AF.Copy),
                     reads=[k.Bps[b]], writes=[Bd])
        for d_ in range(2):
            b = nb()
            fm(C_LR + d_ * 16, b, M=16)
            P.op("act", lambda e, b=b, d_=d_: e.activation(out=lra[0:16, d_, :], in_=ps[0:16, b, :], func=AF.Copy),
                 reads=[k.Bps[b]], writes=[Blra])
        if ti + 1 < len(MTILES):
            comp_h(ti + 1)
        pend_r = [None]
        for col0, dstt, Bd in ((C_DQ, QDt, BQD), (C_DK, KDt, BKD)):
            for hh in range(4):
                b = nb()
                fm(col0 + hh * 128, b)
                if not rope:
                    P.op("act", lambda e, b=b, dstt=dstt, hh=hh: e.activation(out=dstt[:, hh, :], in_=ps[:, b, :], func=AF.Copy),
                         reads=[k.Bps[b]], writes=[Bd])
                    continue
                q_ = qb[tfc[0] % 2]
                Bq = Bqb[tfc[0] % 2]
                r_ = rt[tfc[0] % 2]
                Br = Brt[tfc[0] % 2]
                t2 = tf[tfc[0] % 2]
                Bt2 = Btf[tfc[0] % 2]
                tfc[0] += 1
                P.op("act", lambda e, b=b, q_=q_: e.activation(out=q_, in_=ps[:, b, :], func=AF.Copy),
                     reads=[k.Bps[b]], writes=[Bq])
                P.op("dve", lambda e, b=b, r_=r_: e.tensor_tensor(out=r_, in0=ps[:, b, :], in1=cst[:, 0, :], op=ALU.mult),
                     reads=[k.Bps[b], Bcst], writes=[Br])

                def fin(q_=q_, Bq=Bq, r_=r_, Br=Br, t2=t2, Bt2=Bt2, dstt=dstt, hh=hh, Bd=Bd):
                    b2 = nb()
                    P.op("pe", lambda e, b2=b2: e.matmul(ps[:, b2, :], k.cmb[:, 5, :], q_, start=True, stop=True),
                         reads=[Bq, k.Bcons], writes=[k.Bps[b2]])
                    P.op("dve", lambda e, b2=b2: e.tensor_tensor(out=t2, in0=ps[:, b2, :], in1=cst[:, 1, :], op=ALU.mult),
                         reads=[k.Bps[b2], Bcst], writes=[Bt2])
                    P.op("dve", lambda e: e.tensor_tensor(out=dstt[:, hh, :], in0=r_, in1=t2, op=ALU.add),
                         reads=[Br, Bt2], writes=[Bd])

                if pend_r[0] is not None:
                    pend_r[0]()
                pend_r[0] = fin
        if pend_r[0] is not None:
            pend_r[0]()
        P.dma("sp", lambda e, tsl=tsl: e.dma_start(out=S["QD"][:, :, tsl], in_=QDt), reads=[BQD], writes=[Sb["QD"]], sb=BQD)
        P.dma("sp", lambda e, tsl=tsl: e.dma_start(out=S["KD"][:, :, tsl], in_=KDt), reads=[BKD], writes=[Sb["KD"]], sb=BKD)
        for bi in range(4):
            bsl = slice(bi * 128, (bi + 1) * 128)

            def tm(col0, bank, bsl=bsl):
                for kc in range(KC):
                    P.op("pe", lambda e, kc=kc, bsl=bsl, h=h: e.matmul(ps[:, bank, :], h[:, kc, bsl], W[:, kc, col0:col0 + 512],
                                                                        start=(kc == 0), stop=(kc == KC - 1)),
                         reads=wbufs(col0, col0 + 512) + [Bh], writes=[k.Bps[bank]])

            bkv = nb()
            tm(C_GK, bkv)
            P.op("act", lambda e, bkv=bkv, bi=bi: e.activation(out=VGt[:, bi, :], in_=ps[:, bkv, 256:512], func=AF.Copy),
                 reads=[k.Bps[bkv]], writes=[BVG])
            P.op("dve", lambda e, bkv=bkv, bi=bi: e.tensor_copy(out=ktm4[:, bi, :], in_=ps[:, bkv, 0:256]), reads=[k.Bps[bkv]], writes=[Bktm])
            bdv = nb()
            tm(C_DV, bdv)
            P.op("act", lambda e, bdv=bdv, bi=bi: e.activation(out=VDt[:, bi, :], in_=ps[:, bdv, :], func=AF.Copy),
                 reads=[k.Bps[bdv]], writes=[BVD])
            if not rope:
                sq_, b2 = bi // 2, bi % 2
                nv_, Bnv_ = nvt[bi % 2], Bnv[bi % 2]
                P.op("dve", lambda e, bdv=bdv, nv_=nv_: e.tensor_copy(out=nv_, in_=ps[:, bdv, :]),
                     reads=[k.Bps[bdv]], writes=[Bnv_])
                dst = k.O["nv"][sq_, l][:, b2 * 128:(b2 + 1) * 128, :].rearrange("h p c -> p h c")
                P.dma("sp", lambda e, dst=dst, nv_=nv_: e.dma_start(out=dst, in_=nv_.rearrange("p (h c) -> p h c", c=128)),
                      reads=[Bnv_], writes=[Bnko], sb=Bnv_)
                bdk = nb()
                tm(C_DK, bdk)
                nk_, Bnk_ = nkt[bi % 2], Bnk[bi % 2]
                P.op("dve", lambda e, bdk=bdk, nk_=nk_: e.tensor_copy(out=nk_, in_=ps[:, bdk, :]),
                     reads=[k.Bps[bdk]], writes=[Bnk_])
                dst = k.O["nk"][sq_, l][:, b2 * 128:(b2 + 1) * 128, :].rearrange("h p c -> p h c")
                P.dma("sp", lambda e, dst=dst, nk_=nk_: e.dma_start(out=dst, in_=nk_.rearrange("p (h c) -> p h c", c=128)),
                      reads=[Bnk_], writes=[Bnko], sb=Bnk_)
        for d_ in range(2):
            for bi in range(4):
                bsl = slice(bi * 128, (bi + 1) * 128)
                bank = 2 * d_ + bi // 2
                P.op("pe", lambda e, bank=bank, d_=d_, bsl=bsl, bi=bi: e.matmul(
                    ps[:, bank, (bi % 2) * 256:(bi % 2) * 256 + 256], lra[0:17, d_, bsl], k.wa2[0:17, l, d_, :], start=True, stop=True),
                    reads=[Blra, k.Bcons], writes=[k.Bps[bank]])
            P.op("act", lambda e, d_=d_: e.activation(out=ef2, in_=ps[:, 2 * d_:2 * d_ + 2, :], func=AF.Exp, scale=-1.0),
                 reads=[k.Bps[2 * d_], k.Bps[2 * d_ + 1]], writes=[Bef])
            P.op("act", lambda e, d_=d_: e.activation(out=spf4[:, d_, :, :], in_=ef2.rearrange("p a (b c) -> p (a b) c", c=256), func=AF.Ln, bias=1.0),
                 reads=[Bef], writes=[Bsp])
        for d_ in range(2):
            for bi in range(4):
                bank = 4 + 2 * d_ + bi // 2
                P.op("pe", lambda e, bank=bank, d_=d_, bi=bi: e.matmul(
                    ps[:, bank, (bi % 2) * 256:(bi % 2) * 256 + 256], k.cm[:, 3 + d_, :], spf4[:, d_, bi, :], start=True, stop=True),
                    reads=[Bsp, k.Bcons], writes=[k.Bps[bank]])
            P.op("act", lambda e, d_=d_: e.activation(out=ekh2.rearrange("p (a b) c -> p a (b c)", a=2), in_=ps[:, 4 + 2 * d_:4 + 2 * d_ + 2, :], func=AF.Exp, scale=-1.0 / 16.0),
                 reads=[k.Bps[4 + 2 * d_], k.Bps[4 + 2 * d_ + 1]], writes=[Bekh])
            P.op("dve", lambda e, d_=d_: e.tensor_tensor(out=KHt[d_], in0=ktm4, in1=ekh2, op=ALU.mult),
                 reads=[Bktm, Bekh], writes=[BKH[d_]])
        for d_ in range(2):
            for pr in range(2):
                bank = 2 * d_ + pr
                for bi in range(4):
                    P.op("pe", lambda e, bank=bank, d_=d_, pr=pr, bi=bi: e.matmul(
                        ps[:, bank, bi * 128:(bi + 1) * 128], spf4[:, d_, bi, pr * 128:(pr + 1) * 128], k.cm[:, 1 + d_, :],
                        start=True, stop=True), reads=[Bsp, k.Bcons], writes=[k.Bps[bank]])
                eg, emg, Beg = eG4[pr], emG4[pr], BeG[pr]
                P.op("act", lambda e, bank=bank, eg=eg: e.activation(out=eg, in_=ps[:, bank, :], func=AF.Exp, scale=-1.0 / 16.0),
                     reads=[k.Bps[bank]], writes=[Beg])
                P.op("act", lambda e, bank=bank, emg=emg: e.activation(out=emg, in_=ps[:, bank, :], func=AF.Exp, scale=1.0 / 16.0),
                     reads=[k.Bps[bank]], writes=[Beg])
                P.op("dve", lambda e, d_=d_, pr=pr, eg=eg: e.scalar_tensor_tensor(
                    out=QTt[d_][:, pr, :], in0=qf[:, pr, :], scalar=0.125, in1=eg, op0=ALU.mult, op1=ALU.mult),
                    reads=[Bqf, Beg], writes=[BQT[d_]])
                P.op("dve", lambda e, d_=d_, pr=pr, emg=emg: e.tensor_tensor(
                    out=KTt[d_][:, pr, :], in0=kf[:, pr, :], in1=emg, op=ALU.mult),
                    reads=[Bkf, Beg], writes=[BKT[d_]])
                c0 = 63 if d_ == 0 else 0
                P.op("dve", lambda e, d_=d_, pr=pr, eg=eg, c0=c0: e.tensor_copy(
                    out=DECt[d_][:, pr, :], in_=eg[:, c0:c0 + 449:64]),
                    reads=[Beg], writes=[BDEC[d_]])
        tb = slice(tok0, tok0 + 512)
        P.dma("sp", lambda e, tb=tb: e.dma_start(out=S["VG"][tb, :].rearrange("(b p) c -> p b c", p=128), in_=VGt),
              reads=[BVG], writes=[Sb["VG"]], sb=BVG)
        P.dma("sp", lambda e, tb=tb: e.dma_start(out=S["VD"][tb, :].rearrange("(b p) c -> p b c", p=128), in_=VDt),
              reads=[BVD], writes=[Sb["VD"]], sb=BVD)
        for d_ in range(2):
            P.dma("sp", lambda e, tb=tb, d_=d_: e.dma_start(out=S["KH%d" % d_][tb, :].rearrange("(b p) c -> p b c", p=128), in_=KHt[d_]),
                  reads=[BKH[d_]], writes=[Sb["KH%d" % d_]], sb=BKH[d_])
            P.dma("sp", lambda e, tsl=tsl, d_=d_: e.dma_start(out=S["QT%d" % d_][:, :, tsl], in_=QTt[d_]),
                  reads=[BQT[d_]], writes=[Sb["QT%d" % d_]], sb=BQT[d_])
            P.dma("sp", lambda e, tsl=tsl, d_=d_: e.dma_start(out=S["KT%d" % d_][:, :, tsl], in_=KTt[d_]),
                  reads=[BKT[d_]], writes=[Sb["KT%d" % d_]], sb=BKT[d_])
            P.dma("sp", lambda e, d_=d_, tok0=tok0: e.dma_start(out=S["DEC%d" % d_][:, :, tok0 // 64: tok0 // 64 + 8], in_=DECt[d_]),
                  reads=[BDEC[d_]], writes=[Sb["DEC%d" % d_]], sb=BDEC[d_])
    P.barrier()


def conv_alloc(k):
    ar = k.ar
    c = K()
    c.ypad = [ar.f32(2, T_S + 30), ar.f32(2, T_P + 30), ar.f32(2, T_P + 30)]
    c.acc = [ar.f32(2, T_S), ar.f32(2, T_P), ar.f32(2, T_P)]
    c.Byp = [[Buf("ypad%d_%d" % (s_, cc)) for cc in range(2)] for s_ in range(3)]
    c.Bacc = [[Buf("acc%d_%d" % (s_, cc)) for cc in range(2)] for s_ in range(3)]
    c.rbf, c.rsq = ar.bf16(2, 512), ar.bf16(2, 512)
    c.Bt = Buf("cln_t")
    c.mean, c.var = ar.f32(512), ar.f32(512)
    c.Bs = Buf("cln_s")
    c.u = ar.f32(2, 512)
    c.Bu = Buf("cln_u")
    c.ee = ar.f32(2, 512)
    c.Be = Buf("cln_e")
    c.yo = [ar.bf16(2, 512) for _ in range(2)]
    c.Byo = [Buf("cyo0"), Buf("cyo1")]
    return c


def conv_taps_gen(k, l, c):
    P, S, Sb = k.P, k.S, k.Sb
    for si, (tok0, T, ctx, g) in enumerate(SEQS):
        ypad, acc, Byp, Bacc = c.ypad[si], c.acc[si], c.Byp[si], c.Bacc[si]
        for cc in range(2):
            P.op("dve", lambda e, cc=cc, ypad=ypad: e.memset(ypad[:, cc, 0:15], 0.0), writes=[Byp[cc]])
            P.op("dve", lambda e, cc=cc, T=T, ypad=ypad: e.memset(ypad[:, cc, 15 + T:30 + T], 0.0), writes=[Byp[cc]])
            P.dma("sp", lambda e, cc=cc, T=T, tok0=tok0, ypad=ypad: e.dma_start(out=ypad[:, cc, 15:15 + T], in_=S["YC"][:, cc, tok0:tok0 + T]),
                  reads=[Sb["YC"]], writes=[Byp[cc]], sb=Byp[cc])
    for si, (tok0, T, ctx, g) in enumerate(SEQS):
        ypad, acc, Byp, Bacc = c.ypad[si], c.acc[si], c.Byp[si], c.Bacc[si]
        for cc in range(2):
            P.op("dve", lambda e, cc=cc, T=T, ypad=ypad, acc=acc: e.tensor_scalar(
                out=acc[:, cc, 0:T], in0=ypad[:, cc, 0:T], scalar1=k.convw[:, l, cc, 0:1],
                scalar2=k.convp[:, 0, l * 2 + cc:l * 2 + cc + 1], op0=ALU.mult, op1=ALU.add),
                reads=[Byp[cc], k.Bcons], writes=[Bacc[cc]])
            yield
            for j in range(1, 31):
                P.op("dve", lambda e, cc=cc, T=T, j=j, ypad=ypad, acc=acc: e.scalar_tensor_tensor(
                    out=acc[:, cc, 0:T], in0=ypad[:, cc, j:j + T], scalar=k.convw[:, l, cc, j:j + 1], in1=acc[:, cc, 0:T],
                    op0=ALU.mult, op1=ALU.add), reads=[Byp[cc], Bacc[cc], k.Bcons], writes=[Bacc[cc]])
                for _ in range(8 if T > 1024 else 1):
                    yield


def conv_ln_gen(k, l, c):
    P, ps, S, Sb = k.P, k.ps, k.S, k.Sb
    rbf, rsq, Bt, mean, var, Bs, u, Bu, ee, Be = c.rbf, c.rsq, c.Bt, c.mean, c.var, c.Bs, c.u, c.Bu, c.ee, c.Be
    nyo = 0
    for si, (tok0, T, ctx, g) in enumerate(SEQS):
        acc, Bacc = c.acc[si], c.Bacc[si]
        for t0 in range(0, T, 512):
            n = min(512, T - t0)
            sl = slice(t0, t0 + n)
            for cc in range(2):
                P.op("act", lambda e, cc=cc, sl=sl, n=n, acc=acc: e.activation(out=rbf[:, cc, 0:n], in_=acc[:, cc, sl], func=AF.Copy),
                     reads=[Bacc[cc]], writes=[Bt])
                P.op("act", lambda e, cc=cc, sl=sl, n=n, acc=acc: e.activation(out=rsq[:, cc, 0:n], in_=acc[:, cc, sl], func=AF.Square),
                     reads=[Bacc[cc]], writes=[Bt])
            yield
            for cc in range(2):
                P.op("pe", lambda e, cc=cc, n=n: e.matmul(ps[:, 7, 0:n], k.ones_bf, rbf[:, cc, 0:n], start=(cc == 0), stop=(cc == 1)),
                     reads=[Bt, k.Bcons], writes=[k.Bps[7]])
            P.op("dve", lambda e, n=n: e.tensor_scalar(out=mean[:, 0:n], in0=ps[:, 7, 0:n], scalar1=1.0 / 256, scalar2=None, op0=ALU.mult),
                 reads=[k.Bps[7]], writes=[Bs])
            for cc in range(2):
                P.op("pe", lambda e, cc=cc, n=n: e.matmul(ps[:, 7, 0:n], k.ones_bf, rsq[:, cc, 0:n], start=(cc == 0), stop=(cc == 1)),
                     reads=[Bt, k.Bcons], writes=[k.Bps[7]])
            P.op("dve", lambda e, n=n: e.tensor_tensor(out=var[:, 0:n], in0=mean[:, 0:n], in1=mean[:, 0:n], op=ALU.mult), reads=[Bs], writes=[Bs])
            P.op("dve", lambda e, n=n: e.scalar_tensor_tensor(out=var[:, 0:n], in0=ps[:, 7, 0:n], scalar=1.0 / 256, in1=var[:, 0:n],
                                                             op0=ALU.mult, op1=ALU.subtract), reads=[k.Bps[7], Bs], writes=[Bs])
            yield
            P.op("dve", lambda e, n=n: e.tensor_scalar(out=var[:, 0:n], in0=var[:, 0:n], scalar1=LN_EPS, scalar2=None, op0=ALU.add),
                 reads=[Bs], writes=[Bs])
            yield
            P.op("act", lambda e, n=n: e.activation(out=var[:, 0:n], in_=var[:, 0:n], func=AF.Ln), reads=[Bs], writes=[Bs])
            P.op("act", lambda e, n=n: e.activation(out=var[:, 0:n], in_=var[:, 0:n], func=AF.Exp, scale=-0.5), reads=[Bs], writes=[Bs])
            P.op("dve", lambda e, n=n: e.scalar_tensor_tensor(out=mean[:, 0:n], in0=mean[:, 0:n], scalar=-1.0, in1=var[:, 0:n],
                                                             op0=ALU.mult, op1=ALU.mult), reads=[Bs], writes=[Bs])
            yield
            y_ = c.yo[nyo % 2]
            By = c.Byo[nyo % 2]
            nyo += 1
            for cc in range(2):
                P.op("dve", lambda e, cc=cc, sl=sl, n=n, acc=acc: e.tensor_tensor(out=u[:, cc, 0:n], in0=acc[:, cc, sl], in1=var[:, 0:n], op=ALU.mult),
                     reads=[Bacc[cc], Bs], writes=[Bu])
                P.op("dve", lambda e, cc=cc, n=n: e.tensor_tensor(out=u[:, cc, 0:n], in0=u[:, cc, 0:n], in1=mean[:, 0:n], op=ALU.add),
                     reads=[Bu, Bs], writes=[Bu])
                P.op("act", lambda e, cc=cc, n=n: e.activation(
                    out=u[:, cc, 0:n], in_=u[:, cc, 0:n], func=AF.Identity,
                    scale=k.convp[:, 1, l * 2 + cc:l * 2 + cc + 1], bias=k.convp[:, 2, l * 2 + cc:l * 2 + cc + 1]),
                    reads=[Bu, k.Bcons], writes=[Bu])
                yield
                P.op("act", lambda e, cc=cc, n=n: e.activation(out=ee[:, cc, 0:n], in_=u[:, cc, 0:n], func=AF.Exp, scale=-1.0),
                     reads=[Bu], writes=[Be])
                yield
                P.op("dve", lambda e, cc=cc, n=n: e.tensor_scalar(out=ee[:, cc, 0:n], in0=ee[:, cc, 0:n], scalar1=1.0, scalar2=None, op0=ALU.add),
                     reads=[Be], writes=[Be])
                P.op("dve", lambda e, cc=cc, n=n: e.reciprocal(out=ee[:, cc, 0:n], in_=ee[:, cc, 0:n]), reads=[Be], writes=[Be])
                P.op("dve", lambda e, cc=cc, n=n, y_=y_: e.tensor_tensor(out=y_[:, cc, 0:n], in0=u[:, cc, 0:n], in1=ee[:, cc, 0:n], op=ALU.mult),
                     reads=[Bu, Be], writes=[By])
            P.dma("sp", lambda e, y_=y_, n=n, t0=t0, tok0=tok0: e.dma_start(
                out=S["YM"][:, 0:2, tok0 + t0:tok0 + t0 + n], in_=y_[:, :, 0:n]), reads=[By], writes=[Sb["YM"]], sb=By)


def phase_mix_out(k, l):
    P, ps, I, S, Sb = k.P, k.ps, k.I, k.S, k.Sb
    ar = k.ar
    ar.reset()
    i = 1
    Wo = ar.bf16(KC, D)
    BWo = Buf("Wo")
    BWo2 = [Buf("Wo_a"), Buf("Wo_b")]
    for j in range(2):
        P.dma("pool", lambda e, j=j: e.dma_start(
            out=Wo[:, :, j * 512:(j + 1) * 512], in_=I["w_out"][l][:, j * 512:(j + 1) * 512].rearrange("(kc p) c -> p kc c", p=128)),
            writes=[BWo2[j]], sb=BWo2[j])
    NBUF = 3
    xb = [ar.f32(KC, 1024) for _ in range(NBUF)]
    Bx = [Buf("ox%d" % j) for j in range(NBUF)]
    Bx2 = [Buf("oxh%d" % j) for j in range(NBUF)]
    NYB = 2
    ym = [ar.bf16(KC, 1024) for _ in range(NYB)]
    Bym = [Buf("ym%d" % j) for j in range(NYB)]
    tm2 = (ar.bf16(KC, 512), ar.bf16(KC, 512), ar.f32(512), ar.f32(512), Buf("lntmp2"), Buf("lnstat2"))
    rbf = ar.bf16(KC, 512)
    rsq = ar.bf16(KC, 512)
    mean = ar.f32(512)
    var = ar.f32(512)
    Bt = Buf("lntmp")
    Bstat = Buf("lnstat")
    cacc = [0]

    def loads(ti):
        tok0, nt, g = TILES[ti]
        x, bx, y_, By = xb[ti % NBUF], Bx[ti % NBUF], ym[ti % NYB], Bym[ti % NYB]
        P.dma("sp", lambda e, x=x, tok0=tok0, nt=nt: e.dma_start(out=x[:, :, 0:nt], in_=k.XS[:, :, tok0:tok0 + nt]),
              reads=[xs_buf(k, tok0)], writes=[bx, Bx2[ti % NBUF]], sb=bx)
        P.dma("sp", lambda e, y_=y_, tok0=tok0, nt=nt: e.dma_start(out=y_[:, :, 0:nt], in_=S["YM"][:, :, tok0:tok0 + nt]),
              reads=[Sb["YM"]], writes=[By], sb=By)

    pend = [None]

    def hook():
        if pend[0] is not None:
            if next(pend[0], "done") == "done":
                pend[0] = None

    loads(0)
    loads(1)
    for ti, (tok0, nt, g) in enumerate(TILES):
        x, bx, y_, By = xb[ti % NBUF], Bx[ti % NBUF], ym[ti % NYB], Bym[ti % NYB]
        bxh = [bx, Bx2[ti % NBUF]]
        nh = nt // 512
        for m in range(KC):
            for hf in range(nh):
                hs = slice(hf * 512, (hf + 1) * 512)
                bank = cacc[0] % 4
                cacc[0] += 1
                for kc in range(KC):
                    P.op("pe", lambda e, m=m, kc=kc, hs=hs, bank=bank, y_=y_: e.matmul(
                        ps[:, bank, :], Wo[:, kc, m * 128:(m + 1) * 128], y_[:, kc, hs], start=(kc == 0), stop=(kc == KC - 1)),
                        reads=[BWo2[m // 4], By], writes=[k.Bps[bank]])
                P.op("dve", lambda e, m=m, hs=hs, bank=bank, x=x, g=g: e.scalar_tensor_tensor(
                    out=x[:, m, hs], in0=ps[:, bank, :], scalar=k.gt[:, l, g, i, m:m + 1], in1=x[:, m, hs],
                    op0=ALU.mult, op1=ALU.add), reads=[k.Bps[bank], bxh[hf], k.Bcons], writes=[bxh[hf]])
                hook()
                hook()
        _drain(pend[0])
        pend[0] = None
        if ti + 2 < len(TILES):
            loads(ti + 2)

        def _store(x=x, bx=bx, bx2=Bx2[ti % NBUF], tok0=tok0, nt=nt):
            P.dma("sp", lambda e: e.dma_start(out=k.XS[:, :, tok0:tok0 + nt], in_=x[:, :, 0:nt]),
                  reads=[bx, bx2], writes=[xs_buf(k, tok0)], sb=bx)

        pend[0] = ln_gen(k, l, i, x, bx, nh, (rbf, rsq, mean, var, Bt, Bstat), after=_store, tmps2=tm2, Bx2=Bx2[ti % NBUF])
    _drain(pend[0])
    P.barrier()


def phase_gla(k, l):
    P, ps, I, S, Sb = k.P, k.ps, k.I, k.S, k.Sb
    ar = k.ar
    ar.reset()
    NBmax, NCmax = NTOK // 128, T_S // 64
    QTa = [ar.bf16(2, NTOK) for _ in range(2)]
    KTa = [ar.bf16(2, NTOK) for _ in range(2)]
    KHa = [ar.bf16(NBmax, 256) for _ in range(2)]
    VGa = ar.bf16(NBmax, 256)
    DECa = [ar.f32(2, NTOK // 64) for _ in range(2)]
    SA = [ar.bf16(NCmax, 2, 64) for _ in range(2)]
    Sf = [[ar.f32(2, 64) for _ in range(2)] for _ in range(2)]
    BQT, BKT, BKH, BDEC, BSA = ([Buf("gq%d" % d) for d in range(2)], [Buf("gk%d" % d) for d in range(2)],
                                [Buf("gkh%d" % d) for d in range(2)], [Buf("gdec%d" % d) for d in range(2)],
                                [Buf("gsa%d" % d) for d in range(2)])
    BVG = Buf("gvg")
    BSf = [[Buf("sf%d%d" % (d, j)) for j in range(2)] for d in range(2)]
    mask4 = ar.f32(4, 128)
    Bm = Buf("mask4")
    for q_ in range(4):
        P.op("dve", lambda e, q_=q_: e.tensor_copy(out=mask4[:, q_, :], in_=k.cm[:, 1 + q_ % 2, :]), reads=[k.Bcons], writes=[Bm])
    Am = [ar.bf16(2, 4, 128) for _ in range(2)]
    BAm = [Buf("Am0"), Buf("Am1")]
    pend_epi = [None]
    sq = ar.bf16(512)
    Bsq = Buf("gsq")
    lnv = ar.f32(512)
    Bln = Buf("glnv")
    tt = ar.f32(512)
    Btt = Buf("gtt")
    sgt = [ar.f32(2, 512) for _ in range(2)]
    Bsgt = [Buf("sgt0"), Buf("sgt1")]
    yo = [ar.bf16(2, 512) for _ in range(2)]
    Byo = [Buf("gyo0"), Buf("gyo1")]
    Bout = Buf("ng_out")
    k.outbufs.append(Bout)
    nA = 0
    nG = 0
    for d in range(2):
        P.dma("sp", lambda e, d=d: e.dma_start(out=KHa[d], in_=S["KH%d" % d].rearrange("(b p) c -> p b c", p=128)),
              reads=[Sb["KH%d" % d]], writes=[BKH[d]], sb=BKH[d])
    P.dma("sp", lambda e: e.dma_start(out=VGa, in_=S["VG"].rearrange("(b p) c -> p b c", p=128)),
          reads=[Sb["VG"]], writes=[BVG], sb=BVG)
    for d in range(2):
        P.dma("sp", lambda e, d=d: e.dma_start(out=DECa[d], in_=S["DEC%d" % d]), reads=[Sb["DEC%d" % d]], writes=[BDEC[d]], sb=BDEC[d])
    for d in range(2):
        P.dma("sp", lambda e, d=d: e.dma_start(out=QTa[d], in_=S["QT%d" % d]), reads=[Sb["QT%d" % d]], writes=[BQT[d]], sb=BQT[d])
        P.dma("sp", lambda e, d=d: e.dma_start(out=KTa[d], in_=S["KT%d" % d]), reads=[Sb["KT%d" % d]], writes=[BKT[d]], sb=BKT[d])
    def do_seq(si, tok0, T, ctx, g):
        nonlocal nA, nG
        NB, NCH = T // 128, T // 64
        tsl = slice(tok0, tok0 + T)
        QT = [QTa[d][:, :, tsl] for d in range(2)]
        KT = [KTa[d][:, :, tsl] for d in range(2)]
        KH = [KHa[d][:, tok0 // 128: tok0 // 128 + NB, :] for d in range(2)]
        VG = VGa[:, tok0 // 128: tok0 // 128 + NB, :]
        DEC = [DECa[d][:, :, tok0 // 64: tok0 // 64 + NCH] for d in range(2)]
        for d in range(2):
            if ctx:
                fns = []
                for hh in range(2):
                    src = I["sg"][l, d].rearrange("(pr hh) dk dv -> hh dk pr dv", hh=2)[hh]
                    fns.append(lambda e, d=d, hh=hh, src=src: e.dma_start(out=Sf[d][0][hh * 64:(hh + 1) * 64, :, :], in_=src))
                P.dma("sp", fns, writes=[BSf[d][0]], sb=BSf[d][0])
            else:
                P.op("dve", lambda e, d=d: e.memset(Sf[d][0], 0.0), writes=[BSf[d][0]])
        for step in range(NCH):
            for d in range(2):
                n = step if d == 0 else NCH - 1 - step
                blk, half = n // 2, n % 2
                s_ = 2 * step + d
                bank, slot = s_ % 4, (s_ // 4) % 4
                for h in range(4):
                    hh, pr = h % 2, h // 2
                    P.op("pe", lambda e, d=d, blk=blk, half=half, h=h, hh=hh, pr=pr, bank=bank, slot=slot: e.matmul(
                        ps[hh * 64:(hh + 1) * 64, bank, slot * 128 + pr * 64: slot * 128 + pr * 64 + 64],
                        KH[d][half * 64:(half + 1) * 64, blk, h * 64:(h + 1) * 64],
                        VG[half * 64:(half + 1) * 64, blk, h * 64:(h + 1) * 64], start=True, stop=True),
                        reads=[BKH[d], BVG], writes=[k.Bps[bank]])
                cur, nxt = step % 2, (step + 1) % 2
                P.op("act", lambda e, d=d, n=n, cur=cur: e.activation(
                    out=SA[d][:, n, :, :], in_=Sf[d][cur], func=AF.Copy), reads=[BSf[d][cur]], writes=[BSA[d]])
                for pr in range(2):
                    P.op("dve", lambda e, d=d, n=n, pr=pr, cur=cur, nxt=nxt, bank=bank, slot=slot: e.scalar_tensor_tensor(
                        out=Sf[d][nxt][:, pr, :], in0=Sf[d][cur][:, pr, :], scalar=DEC[d][:, pr, n:n + 1],
                        in1=ps[:, bank, slot * 128 + pr * 64: slot * 128 + pr * 64 + 64], op0=ALU.mult, op1=ALU.add),
                        reads=[BSf[d][cur], BDEC[d], k.Bps[bank]], writes=[BSf[d][nxt]])
        fin = NCH % 2
        if not ctx:
            for d in range(2):
                fns = []
                for hh in range(2):
                    dst = k.O["ng"][si - 1, l, d].rearrange("(pr hh) dk dv -> hh dk pr dv", hh=2)[hh]
                    fns.append(lambda e, d=d, hh=hh, dst=dst, fin=fin: e.dma_start(out=dst, in_=Sf[d][fin][hh * 64:(hh + 1) * 64, :, :]))
                P.dma("sp", fns, reads=[BSf[d][fin]], writes=[Bout], sb=BSf[d][fin])
        GB = min(4, NB)
        for gi in range(NB // GB):
            gt0 = gi * GB * 128
            ncol = GB * 128
            sg_ = sgt[nG % 2]
            Bsg_ = Bsgt[nG % 2]
            y_ = yo[nG % 2]
            By = Byo[nG % 2]
            nG += 1
            P.dma("sp", lambda e, sg_=sg_, ncol=ncol, a=tok0 + gt0: e.dma_start(out=sg_[:, :, 0:ncol], in_=S["SG"][:, :, a:a + ncol]),
                  reads=[Sb["SG"]], writes=[Bsg_], sb=Bsg_)
            for pr in range(2):
                bO = [2 * pr, 2 * pr + 1]
                for rd in range((GB + 1) // 2):
                    nb2 = min(2, GB - 2 * rd)
                    am = Am[nA % 2]
                    Bam = BAm[nA % 2]
                    nA += 1
                    for b2 in range(nb2):
                        blk = gi * GB + rd * 2 + b2
                        bt = slice(blk * 128, (blk + 1) * 128)
                        for d in range(2):
                            for hh in range(2):
                                P.op("pe", lambda e, d=d, hh=hh, pr=pr, bt=bt, b2=b2: e.matmul(
                                    ps[:, 4 + hh, (b2 * 2 + d) * 128:(b2 * 2 + d + 1) * 128],
                                    KT[d][hh * 64:(hh + 1) * 64, pr, bt], QT[d][hh * 64:(hh + 1) * 64, pr, bt], start=True, stop=True),
                                    reads=[BKT[d], BQT[d]], writes=[k.Bps[4 + hh]])
                    for hh in range(2):
                        P.op("dve", lambda e, am=am, hh=hh, nb2=nb2: e.tensor_tensor(
                            out=am[:, hh, 0:2 * nb2, :], in0=ps[:, 4 + hh, 0:256 * nb2].rearrange("p (a b) -> p a b", b=128),
                            in1=mask4[:, 0:2 * nb2, :], op=ALU.mult),
                            reads=[k.Bps[4 + hh], Bm], writes=[Bam])
                    if rd == 0 and pend_epi[0] is not None:
                        pend_epi[0]()
                        pend_epi[0] = None
                    for b2 in range(nb2):
                        bl = rd * 2 + b2
                        blk = gi * GB + bl
                        cols = slice(bl * 128, (bl + 1) * 128)
                        for hh in range(2):
                            h = pr * 2 + hh
                            for d in range(2):
                                P.op("pe", lambda e, d=d, hh=hh, h=h, blk=blk, am=am, cols=cols, bO=bO, b2=b2: e.matmul(
                                    ps[hh * 64:(hh + 1) * 64, bO[hh], cols], VG[:, blk, h * 64:(h + 1) * 64], am[:, hh, b2 * 2 + d, :],
                                    start=(d == 0), stop=False), reads=[BVG, Bam], writes=[k.Bps[bO[hh]]])
                                for cc in range(2):
                                    n = 2 * blk + cc
                                    ct = slice(n * 64, (n + 1) * 64)
                                    oc = slice(bl * 128 + cc * 64, bl * 128 + cc * 64 + 64)
                                    P.op("pe", lambda e, d=d, hh=hh, n=n, pr=pr, ct=ct, oc=oc, cc=cc, bO=bO: e.matmul(
                                        ps[hh * 64:(hh + 1) * 64, bO[hh], oc], SA[d][hh * 64:(hh + 1) * 64, n, pr, :],
                                        QT[d][hh * 64:(hh + 1) * 64, pr, ct], start=False, stop=(d == 1 and cc == 1)),
                                        reads=[BSA[d], BQT[d]], writes=[k.Bps[bO[hh]]])

                def epi(pr=pr, bO=bO, ncol=ncol, y_=y_, sg_=sg_, By=By, Bsg_=Bsg_, last=(pr == 1), a=tok0 + gt0):
                    for hh in range(2):
                        P.op("act", lambda e, hh=hh: e.activation(
                            out=sq[hh * 64:(hh + 1) * 64, 0:ncol], in_=ps[hh * 64:(hh + 1) * 64, bO[hh], 0:ncol], func=AF.Square),
                            reads=[k.Bps[bO[hh]]], writes=[Bsq])
                    bankR = 6 + pr
                    P.op("pe", lambda e: e.matmul(ps[:, bankR, 0:ncol], k.cmb[:, 6, :], sq[:, 0:ncol], start=True, stop=True),
                         reads=[Bsq, k.Bcons], writes=[k.Bps[bankR]])
                    P.op("dve", lambda e: e.tensor_scalar(
                        out=lnv[:, 0:ncol], in0=ps[:, bankR, 0:ncol], scalar1=1.0 / 64, scalar2=LN_EPS, op0=ALU.mult, op1=ALU.add),
                        reads=[k.Bps[bankR]], writes=[Bln])
                    P.op("act", lambda e: e.activation(out=lnv[:, 0:ncol], in_=lnv[:, 0:ncol], func=AF.Ln), reads=[Bln], writes=[Bln])
                    P.op("act", lambda e: e.activation(out=lnv[:, 0:ncol], in_=lnv[:, 0:ncol], func=AF.Exp, scale=-0.5),
                         reads=[Bln], writes=[Bln])
                    for hh in range(2):
                        P.op("dve", lambda e, hh=hh: e.tensor_tensor(
                            out=tt[hh * 64:(hh + 1) * 64, 0:ncol], in0=ps[hh * 64:(hh + 1) * 64, bO[hh], 0:ncol],
                            in1=lnv[hh * 64:(hh + 1) * 64, 0:ncol], op=ALU.mult),
                            reads=[k.Bps[bO[hh]], Bln], writes=[Btt])
                    P.op("dve", lambda e: e.scalar_tensor_tensor(
                        out=y_[:, pr, 0:ncol], in0=tt[:, 0:ncol], scalar=k.gnorm[:, l:l + 1], in1=sg_[:, pr, 0:ncol],
                        op0=ALU.mult, op1=ALU.mult), reads=[Btt, Bsg_, k.Bcons], writes=[By])
                    if last:
                        P.dma("sp", lambda e: e.dma_start(out=S["YM"][:, 2:4, a:a + ncol], in_=y_[:, :, 0:ncol]),
                              reads=[By], writes=[Sb["YM"]], sb=By)

                if pend_epi[0] is not None:
                    pend_epi[0]()
                pend_epi[0] = epi
        if pend_epi[0] is not None:
            pend_epi[0]()
            pend_epi[0] = None

    for si, (tok0, T, ctx, g) in enumerate(SEQS):
        do_seq(si, tok0, T, ctx, g)
    if pend_epi[0] is not None:
        pend_epi[0]()
    P.barrier()


def phase_attn(k, l):
    P, ps, I, S, Sb = k.P, k.ps, k.I, k.S, k.Sb
    ar = k.ar
    ar.reset()
    NKmax = T_S + PAST
    KTb = [ar.bf16(NKmax) for _ in range(2)]
    Vb = [ar.bf16(NKmax // 128, 130) for _ in range(2)]
    Qb = [ar.bf16(T_S) for _ in range(2)]
    BKTb = [Buf("aK0"), Buf("aK1")]
    BVb = [Buf("aV0"), Buf("aV1")]
    BQb = [Buf("aQ0"), Buf("aQ1")]
    ckst = ar.f32(4, 128)
    Bck = Buf("ckst")
    E = [ar.bf16(2, 512) for _ in range(3)]
    BE = [Buf("E%d" % j) for j in range(3)]
    rz = ar.f32(8)
    Brz = Buf("rz")
    tO = ar.f32(8, 128)
    BtO = Buf("tO")
    od = ar.f32(4, 128)
    Bod = Buf("od")
    sq = ar.f32(4, 128)
    Bsq = Buf("asq")
    ss = ar.f32(4)
    Bss = Buf("ass")
    yb = ar.bf16(4, 128)
    Byb = Buf("ayb")
    yT = [ar.bf16(512) for _ in range(2)]
    ByT = [Buf("yT0"), Buf("yT1")]
    for j in range(2):
        P.op("dve", lambda e, j=j: e.memset(Vb[j][:, :, 128:130], 1.0), writes=[BVb[j]])
    cv_ = conv_alloc(k)
    def _chain():
        yield from conv_taps_gen(k, l, cv_)
        yield from conv_ln_gen(k, l, cv_)
    cgen = [_chain()]

    def chook():
        if cgen[0] is not None:
            if next(cgen[0], "done") == "done":
                cgen[0] = None
    nE = [0]
    nY = [0]
    heads = []
    for si, (tok0, T, ctx, g) in enumerate(SEQS):
        for hh in range(4):
            hc = K()
            hc.si, hc.hh, hc.tok0, hc.T, hc.ctx = si, hh, tok0, T, ctx
            hc.NKT = (T + (PAST if ctx else 0)) // 128
            hc.QB = min(512, T)
            hc.QS = hc.QB // 128
            j = len(heads) % 2
            hc.Kt, hc.Vt, hc.Qt, hc.BK, hc.BV, hc.BQ = KTb[j], Vb[j], Qb[j], BKTb[j], BVb[j], BQb[j]
            heads.append(hc)
    units = [(hi_, qb) for hi_, hc in enumerate(heads) for qb in range(hc.T // hc.QB)]

    def emit_loads(hc):
        Kt, Vt, Qt, BK, BV, BQ, hh, T = hc.Kt, hc.Vt, hc.Qt, hc.BK, hc.BV, hc.BQ, hc.hh, hc.T
        tsl = slice(hc.tok0, hc.tok0 + T)
        P.dma("sp", lambda e: e.dma_start(out=Qt[:, 0:T], in_=S["QD"][:, hh, tsl]), reads=[Sb["QD"]], writes=[BQ], sb=BQ)
        P.dma("sp", lambda e: e.dma_start(out=Kt[:, 0:T], in_=S["KD"][:, hh, tsl]), reads=[Sb["KD"]], writes=[BK], sb=BK)
        P.dma("sp", lambda e: e.dma_start(
            out=Vt[:, 0:T // 128, 0:128], in_=S["VD"][tsl, hh * 128:(hh + 1) * 128].rearrange("(b p) c -> p b c", p=128)),
            reads=[Sb["VD"]], writes=[BV], sb=BV)
        if hc.ctx:
            P.dma("pool", lambda e: e.dma_start(
                out=Vt[:, T // 128:T // 128 + 4, 0:128], in_=I["cv"][l, hh].rearrange("(b p) c -> p b c", p=128)),
                writes=[BV], sb=BV)
            P.dma("sp", lambda e: e.dma_start(out=ckst, in_=I["ck"][l, hh].rearrange("(b p) c -> p b c", p=128)),
                  writes=[Bck], sb=Bck)
            for b4 in range(4):
                P.op("pe", lambda e, b4=b4: e.transpose(out=ps[:, 7, b4 * 128:(b4 + 1) * 128], in_=ckst[:, b4, :], identity=k.ident),
                     reads=[Bck, k.Bcons], writes=[k.Bps[7]])
            P.op("dve", lambda e: e.tensor_copy(out=Kt[:, T:T + 512], in_=ps[:, 7, :]), reads=[k.Bps[7]], writes=[BK])

    def qk(ui, kt):
        hc = heads[units[ui][0]]
        qb = units[ui][1]
        Kt, Qt, QB = hc.Kt, hc.Qt, hc.QB
        qsl = slice(qb * QB, (qb + 1) * QB)
        sb_ = kt % 2
        for mp in range(2):
            bank = sb_ * 2 + mp
            P.op("pe", lambda e, mp=mp, bank=bank: e.matmul(
                ps[:, bank, 0:QB], Kt[mp * 64:(mp + 1) * 64, kt * 128:(kt + 1) * 128], Qt[mp * 64:(mp + 1) * 64, qsl],
                start=True, stop=True), reads=[hc.BK, hc.BQ], writes=[k.Bps[bank]])

    def epi_gen(hc, qb):
        QB, QS, hh, tok0 = hc.QB, hc.QS, hc.hh, hc.tok0
        ns = 2 * QS
        for b3 in range((ns + 2) // 3):
            nsb = min(3, ns - 3 * b3)
            pv = ps[:, 4 + b3, 0:480].rearrange("p (s c) -> p s c", c=160)
            P.op("dve", lambda e, b3=b3, nsb=nsb, pv=pv: e.reciprocal(out=rz[:, 3 * b3:3 * b3 + nsb], in_=pv[:, 0:nsb, 128]),
                 reads=[k.Bps[4 + b3]], writes=[Brz])
            P.op("dve", lambda e, b3=b3, nsb=nsb, pv=pv: e.tensor_tensor(
                out=tO[:, 3 * b3:3 * b3 + nsb, :], in0=pv[:, 0:nsb, 0:128],
                in1=rz[:, 3 * b3:3 * b3 + nsb].unsqueeze(2).broadcast_to([128, nsb, 128]), op=ALU.mult),
                reads=[k.Bps[4 + b3], Brz], writes=[BtO])
        tv = tO.rearrange("p (q m) c -> p q m c", m=2)
        P.op("dve", lambda e: e.scalar_tensor_tensor(
            out=od[:, 0:QS, :], in0=tv[:, 0:QS, 1, :], scalar=k.nlam[:, l:l + 1], in1=tv[:, 0:QS, 0, :],
            op0=ALU.mult, op1=ALU.add), reads=[BtO, k.Bcons], writes=[Bod])
        P.op("dve", lambda e: e.tensor_tensor(out=sq[:, 0:QS, :], in0=od[:, 0:QS, :], in1=od[:, 0:QS, :], op=ALU.mult),
             reads=[Bod], writes=[Bsq])
        P.op("dve", lambda e: e.reduce_sum(out=ss[:, 0:QS], in_=sq[:, 0:QS, :], axis=AX.X), reads=[Bsq], writes=[Bss])
        P.op("dve", lambda e: e.tensor_scalar(out=ss[:, 0:QS], in0=ss[:, 0:QS], scalar1=1.0 / 128, scalar2=LN_EPS,
                                              op0=ALU.mult, op1=ALU.add), reads=[Bss], writes=[Bss])
        yield
        P.op("act", lambda e: e.activation(out=ss[:, 0:QS], in_=ss[:, 0:QS], func=AF.Ln), reads=[Bss], writes=[Bss])
        P.op("act", lambda e: e.activation(out=ss[:, 0:QS], in_=ss[:, 0:QS], func=AF.Exp, scale=-0.5), reads=[Bss], writes=[Bss])
        yield
        P.op("dve", lambda e: e.tensor_tensor(
            out=yb[:, 0:QS, :], in0=od[:, 0:QS, :], in1=ss[:, 0:QS].unsqueeze(2).broadcast_to([128, QS, 128]), op=ALU.mult),
            reads=[Bod, Bss], writes=[Byb])
        for qs in range(QS):
            P.op("pe", lambda e, qs=qs: e.matmul(ps[:, 7, qs * 128:(qs + 1) * 128], yb[:, qs, :], k.ident_bf, start=True, stop=True),
                 reads=[Byb, k.Bcons], writes=[k.Bps[7]])
        y_ = yT[nY[0] % 2]
        By = ByT[nY[0] % 2]
        nY[0] += 1
        P.op("act", lambda e: e.activation(out=y_[:, 0:QB], in_=ps[:, 7, 0:QB], func=AF.Copy, scale=k.dnorm[:, l:l + 1]),
             reads=[k.Bps[7], k.Bcons], writes=[By])
        a_ = tok0 + qb * QB
        P.dma("sp", lambda e: e.dma_start(out=S["YM"][:, 4 + hh, a_:a_ + QB], in_=y_[:, 0:QB]),
              reads=[By], writes=[Sb["YM"]], sb=By)
        yield

    pend = [None]

    def step_epi():
        if pend[0] is not None:
            if next(pend[0], "done") == "done":
                pend[0] = None

    emit_loads(heads[0])
    qk(0, 0)
    for ui, (hi_, qb) in enumerate(units):
        hc = heads[hi_]
        NKT, QB, QS, Vt = hc.NKT, hc.QB, hc.QS, hc.Vt
        if qb == 0 and hi_ + 1 < len(heads):
            emit_loads(heads[hi_ + 1])
        for kt in range(NKT):
            sb_ = kt % 2
            e_ = E[nE[0] % 3]
            Be_ = BE[nE[0] % 3]
            nE[0] += 1
            P.op("act", lambda e, e_=e_, sb_=sb_, QB=QB: e.activation(
                out=e_[:, :, 0:QB], in_=ps[:, sb_ * 2:sb_ * 2 + 2, 0:QB], func=AF.Exp, scale=0.125),
                reads=[k.Bps[sb_ * 2], k.Bps[sb_ * 2 + 1]], writes=[Be_])
            if kt + 1 < NKT:
                qk(ui, kt + 1)
            elif ui + 1 < len(units):
                qk(ui + 1, 0)
            if kt == 0:
                step_epi()
            elif kt in (2, 3, 5):
                step_epi()
            for qs in range(QS):
                for mp in range(2):
                    slot = qs * 2 + mp
                    bank = 4 + slot // 3
                    off = (slot % 3) * 160
                    P.op("pe", lambda e, e_=e_, kt=kt, qs=qs, mp=mp, bank=bank, off=off, slot=slot, Vt=Vt, NKT=NKT: e.matmul(
                        ps[:, bank, off:off + 129], e_[:, mp, qs * 128:(qs + 1) * 128], Vt[:, kt, 0:129],
                        start=(kt == 0 and slot % 3 == 0), stop=(kt == NKT - 1), skip_group_check=True),
                        reads=[Be_, hc.BV], writes=[k.Bps[bank]])
            chook()
        _drain(pend[0])
        pend[0] = epi_gen(hc, qb)
    _drain(pend[0])
    _drain(cgen[0])
    P.barrier()
```
